# Optimizing a Trainium2 kernel written in Bass

```python
import math
import jax, jax.numpy as jnp
from jax import lax
import numpy as np

D_MODEL = 4096
BATCH = 1
SEQ = 16384
DEPTH = 4

GRID_W = 64
CTX_LEN = 256
EPS = 1e-6
ROPE_BASE = 10000.0
MOD_RANK = 256
CHUNK = 64
NA_HEAD_DIM = 128
NA_HEADS = D_MODEL // (2 * NA_HEAD_DIM)
NA_W = NA_HEADS * NA_HEAD_DIM
NA_WIN_H = 8
NA_WIN_W = 16
NA_Q_BLOCK = 128
HG_KEY_DIM = 128
HG_VAL_DIM = 128
HG_HEADS = D_MODEL // (2 * HG_KEY_DIM)
HG_K_W = HG_HEADS * HG_KEY_DIM
HG_V_W = HG_HEADS * HG_VAL_DIM
ML_V_DIM = 512
ML_HEADS = D_MODEL // ML_V_DIM
ML_QK_DIM = ML_V_DIM // 2
ML_QK_W = ML_HEADS * ML_QK_DIM
ML_V_W = ML_HEADS * ML_V_DIM
FFN_HIDDEN = -(-8 * D_MODEL // (3 * 256)) * 256

EVEN_SIZES = (NA_W, NA_W, NA_W, HG_K_W, HG_K_W, HG_K_W, HG_V_W, HG_V_W)
ODD_SIZES = (ML_QK_W, ML_QK_W, ML_V_W, ML_V_W, 4 * ML_HEADS)
EVEN_IN = sum(EVEN_SIZES)
ODD_IN = sum(ODD_SIZES)

kernel_name = "hybrid_natten_hgrn2_mlstm_dit_prefix"


def _split(a, sizes):
    return jnp.split(a, np.cumsum(sizes)[:-1].tolist(), axis=-1)


def rms_norm(x, g):
    xf = x.astype(jnp.float32)
    y = xf * lax.rsqrt(jnp.mean(xf * xf, axis=-1, keepdims=True) + EPS)
    return (y * g.astype(jnp.float32)).astype(x.dtype)


def head_rms_norm(o, g):
    b, l = o.shape[:2]
    of = o.astype(jnp.float32)
    y = of * lax.rsqrt(jnp.mean(of * of, axis=-1, keepdims=True) + EPS)
    return (y.reshape(b, l, -1) * g.astype(jnp.float32)).astype(o.dtype)


def adaln(cond, w_a, w_b, b_mod):
    m = (jax.nn.silu(cond) @ w_a) @ w_b + b_mod
    return jnp.split(m[..., None, :], 6, axis=-1)


def swiglu(h, w_up, w_down):
    a, g = jnp.split(h @ w_up, 2, axis=-1)
    return (jax.nn.silu(g) * a) @ w_down


def axial_rope_tables(n, dim):
    t = jnp.arange(n)
    row = (t // GRID_W).astype(jnp.float32)
    col = (t % GRID_W).astype(jnp.float32)
    n_freq = dim // 4
    inv = ROPE_BASE ** (-jnp.arange(n_freq, dtype=jnp.float32) / n_freq)
    ar = row[:, None] * inv
    ac = col[:, None] * inv
    ang = jnp.concatenate([ar, ar, ac, ac], axis=-1)
    return jnp.cos(ang), jnp.sin(ang)


def rope_2d(x, cos, sin):
    x1, x2, x3, x4 = jnp.split(x, 4, axis=-1)
    rot = jnp.concatenate([-x2, x1, -x4, x3], axis=-1)
    return (x * cos[:, None, :] + rot * sin[:, None, :]).astype(x.dtype)


def _chunks(a):
    b, n = a.shape[:2]
    a = a.reshape(b, n // CHUNK, CHUNK, *a.shape[2:])
    return jnp.moveaxis(jnp.moveaxis(a, 1, 0), 3, 2)


def _unchunk(a):
    a = jnp.moveaxis(jnp.moveaxis(a, 2, 3), 0, 1)
    return a.reshape(a.shape[0], -1, *a.shape[3:])


def neighbourhood_index(n):
    rows = n // GRID_W
    kh = min(NA_WIN_H, rows)
    kw = NA_WIN_W
    r = jnp.arange(rows)
    cc = jnp.arange(GRID_W)
    r0 = jnp.clip(r - kh // 2, 0, rows - kh)
    c0 = jnp.clip(cc - kw // 2, 0, GRID_W - kw)
    kr = r0[:, None] + jnp.arange(kh)
    kc = c0[:, None] + jnp.arange(kw)
    key_row = jnp.broadcast_to(kr[:, None, :, None], (rows, GRID_W, kh, kw))
    key_col = jnp.broadcast_to(kc[None, :, None, :], (rows, GRID_W, kh, kw))
    idx = (key_row * GRID_W + key_col).reshape(n, kh * kw)
    d_row = key_row - r[:, None, None, None]
    d_col = key_col - cc[None, :, None, None]
    bias_idx = ((d_row + NA_WIN_H - 1) * (2 * NA_WIN_W - 1) + (d_col + NA_WIN_W - 1)).reshape(n, kh * kw)
    return idx, bias_idx


def neighbourhood_attention(q, k, v, k_ctx, v_ctx, rpb):
    b, n, h, dh = q.shape
    idx, bias_idx = neighbourhood_index(n)
    nb = n // NA_Q_BLOCK
    rpb_flat = rpb.reshape(h, -1).astype(jnp.float32)
    qb = jnp.moveaxis((q * dh ** -0.5).reshape(b, nb, NA_Q_BLOCK, h, dh), 1, 0)
    idxb = idx.reshape(nb, NA_Q_BLOCK, -1)
    biasb = bias_idx.reshape(nb, NA_Q_BLOCK, -1)

    def block(args):
        qi, ii, bi = args
        kn = k[:, ii]
        vn = v[:, ii]
        s_nb = jnp.einsum('bqhd,bqkhd->bhqk', qi, kn).astype(jnp.float32) + rpb_flat[:, bi]
        s_cx = jnp.einsum('bqhd,blhd->bhql', qi, k_ctx).astype(jnp.float32)
        p = jax.nn.softmax(jnp.concatenate([s_nb, s_cx], axis=-1), axis=-1)
        kk = kn.shape[2]
        p_nb = p[..., :kk].astype(v.dtype)
        p_cx = p[..., kk:].astype(v.dtype)
        return jnp.einsum('bhqk,bqkhd->bqhd', p_nb, vn) + jnp.einsum('bhql,blhd->bqhd', p_cx, v_ctx)

    out = lax.map(block, (qb, idxb, biasb))
    return jnp.moveaxis(out, 0, 1).reshape(b, n, h, dh)


def context_attention(q, k, v):
    s = jnp.einsum('bqhd,bkhd->bhqk', q, k).astype(jnp.float32) * q.shape[-1] ** -0.5
    p = jax.nn.softmax(s, axis=-1).astype(v.dtype)
    return jnp.einsum('bhqk,bkhd->bqhd', p, v)


def hgrn_forget(f_pre, lb):
    fp = f_pre.astype(jnp.float32)
    log_f = jnp.logaddexp(jnp.log(lb), jnp.log1p(-lb) + jax.nn.log_sigmoid(fp))
    key = (1.0 - lb) * jax.nn.sigmoid(-fp)
    return key, log_f


def gla_chunk_scan(q, k, v, log_f, s0):
    causal = jnp.tril(jnp.ones((CHUNK, CHUNK), dtype=bool))

    def step(s, xs):
        qc, kc, vc, gc = (a.astype(jnp.float32) for a in xs)
        cum = jnp.cumsum(gc, axis=2)
        diff = jnp.where(causal[:, :, None], cum[:, :, :, None, :] - cum[:, :, None, :, :], -jnp.inf)
        att = jnp.einsum('bhtd,bhsd,bhtsd->bhts', qc, kc, jnp.exp(diff))
        o = jnp.einsum('bhts,bhsv->bhtv', att, vc) + jnp.einsum('bhtd,bhdv->bhtv', qc * jnp.exp(cum), s)
        last = cum[:, :, -1]
        s_new = jnp.exp(last)[..., None] * s + jnp.einsum('bhsd,bhsv->bhdv', kc * jnp.exp(last[:, :, None] - cum), vc)
        return s_new, o

    s_fin, o = lax.scan(step, s0, tuple(_chunks(a) for a in (q, k, v, log_f)))
    return _unchunk(o).astype(v.dtype), s_fin


def mlstm_chunk_scan(q, k, v, i_gate, log_f, state0):
    causal = jnp.tril(jnp.ones((CHUNK, CHUNK), dtype=bool))

    def step(state, xs):
        c_mat, n_vec, m = state
        qc, kc, vc, ic, fc = (a.astype(jnp.float32) for a in xs)
        bcum = jnp.cumsum(fc, axis=-1)
        d_intra = jnp.where(causal, bcum[..., :, None] - bcum[..., None, :] + ic[..., None, :], -jnp.inf)
        d_inter = bcum + m[..., None]
        m_t = jnp.maximum(d_inter, jnp.max(d_intra, axis=-1))
        a = jnp.einsum('bhtd,bhsd->bhts', qc, kc) * jnp.exp(d_intra - m_t[..., None])
        w_inter = jnp.exp(d_inter - m_t)
        num = jnp.einsum('bhts,bhsv->bhtv', a, vc) + w_inter[..., None] * jnp.einsum('bhtd,bhdv->bhtv', qc, c_mat)
        den = jnp.sum(a, axis=-1) + w_inter * jnp.einsum('bhtd,bhd->bht', qc, n_vec)
        h = num / jnp.maximum(jnp.abs(den), jnp.exp(-m_t))[..., None]
        d_last = bcum[..., -1:] - bcum + ic
        m_new = jnp.maximum(bcum[..., -1] + m, jnp.max(d_last, axis=-1))
        w_s = jnp.exp(d_last - m_new[..., None])
        w_prev = jnp.exp(bcum[..., -1] + m - m_new)
        c_new = w_prev[..., None, None] * c_mat + jnp.einsum('bhs,bhsd,bhsv->bhdv', w_s, kc, vc)
        n_new = w_prev[..., None] * n_vec + jnp.einsum('bhs,bhsd->bhd', w_s, kc)
        return (c_new, n_new, m_new), h

    state, h = lax.scan(step, state0, tuple(_chunks(a) for a in (q, k, v, i_gate, log_f)))
    return _unchunk(h).astype(v.dtype), state


def even_mixer(hc, hx, w_in, w_out, rpb, lb, norm_g, ctx_out):
    def project(h):
        b, l, _ = h.shape
        qa, ka, va, qh, ffw, fbw, ih, gh = _split(h @ w_in, EVEN_SIZES)
        na = lambda a: a.reshape(b, l, NA_HEADS, NA_HEAD_DIM)
        hg = lambda a: a.reshape(b, l, HG_HEADS, -1)
        return na(qa), na(ka), na(va), hg(qh), hg(ffw), hg(fbw), hg(ih), gh

    qa_c, ka_c, va_c, qh_c, ff_c, fb_c, ih_c, gh_c = project(hc)
    qa_x, ka_x, va_x, qh_x, ff_x, fb_x, ih_x, gh_x = project(hx)
    b, n = hx.shape[:2]

    att_x = neighbourhood_attention(qa_x, ka_x, va_x, ka_c, va_c, rpb)

    s0 = jnp.zeros((hc.shape[0], HG_HEADS, HG_KEY_DIM, HG_VAL_DIM), jnp.float32)
    outs_c, outs_x = [], []
    for d, (f_c, f_x) in enumerate(((ff_c, ff_x), (fb_c, fb_x))):
        lbd = lb[d].reshape(HG_HEADS, HG_KEY_DIM)
        k_c, lf_c = hgrn_forget(f_c, lbd)
        k_x, lf_x = hgrn_forget(f_x, lbd)
        seq_c = (qh_c, k_c, ih_c, lf_c)
        seq_x = (qh_x, k_x, ih_x, lf_x)
        if d == 1:
            seq_c = tuple(jnp.flip(a, axis=1) for a in seq_c)
            seq_x = tuple(jnp.flip(a, axis=1) for a in seq_x)
        o_c, s_c = gla_chunk_scan(*seq_c, s0)
        o_x, _ = gla_chunk_scan(*seq_x, s_c)
        if d == 1:
            o_c = jnp.flip(o_c, axis=1)
            o_x = jnp.flip(o_x, axis=1)
        outs_c.append(o_c)
        outs_x.append(o_x)

    rec_x = head_rms_norm(outs_x[0] + outs_x[1], norm_g) * jax.nn.silu(gh_x)
    y_x = jnp.concatenate([att_x.reshape(b, n, NA_W), rec_x], axis=-1) @ w_out
    y_c = None
    if ctx_out:
        l = hc.shape[1]
        att_c = context_attention(qa_c, ka_c, va_c)
        rec_c = head_rms_norm(outs_c[0] + outs_c[1], norm_g) * jax.nn.silu(gh_c)
        y_c = jnp.concatenate([att_c.reshape(b, l, NA_W), rec_c], axis=-1) @ w_out
    return y_c, y_x


def odd_mixer(hc, hx, w_in, b_gates, w_out, norm_g, cos, sin, ctx_out):
    def project(h, rotate):
        b, l, _ = h.shape
        q, k, v, o, g = _split(h @ w_in, ODD_SIZES)
        q = q.reshape(b, l, ML_HEADS, ML_QK_DIM)
        k = k.reshape(b, l, ML_HEADS, ML_QK_DIM)
        v = v.reshape(b, l, ML_HEADS, ML_V_DIM)
        if rotate:
            q = rope_2d(q, cos, sin)
            k = rope_2d(k, cos, sin)
        k = k * ML_QK_DIM ** -0.5
        g = (g + b_gates).astype(jnp.float32).reshape(b, l, 4, ML_HEADS)
        return q, k, v, o, g

    q_c, k_c, v_c, o_c, g_c = project(hc, False)
    q_x, k_x, v_x, o_x, g_x = project(hx, True)
    bc = hc.shape[0]
    state0 = (jnp.zeros((bc, ML_HEADS, ML_QK_DIM, ML_V_DIM), jnp.float32),
              jnp.zeros((bc, ML_HEADS, ML_QK_DIM), jnp.float32),
              jnp.zeros((bc, ML_HEADS), jnp.float32))
    outs_c, outs_x = [], []
    for d in range(2):
        seq_c = (q_c, k_c, v_c, g_c[:, :, 2 * d], jax.nn.log_sigmoid(g_c[:, :, 2 * d + 1]))
        seq_x = (q_x, k_x, v_x, g_x[:, :, 2 * d], jax.nn.log_sigmoid(g_x[:, :, 2 * d + 1]))
        if d == 1:
            seq_c = tuple(jnp.flip(a, axis=1) for a in seq_c)
            seq_x = tuple(jnp.flip(a, axis=1) for a in seq_x)
        h_c, st_c = mlstm_chunk_scan(*seq_c, state0)
        h_x, _ = mlstm_chunk_scan(*seq_x, st_c)
        if d == 1:
            h_c = jnp.flip(h_c, axis=1)
            h_x = jnp.flip(h_x, axis=1)
        outs_c.append(h_c)
        outs_x.append(h_x)

    y_x = (head_rms_norm(outs_x[0] + outs_x[1], norm_g) * jax.nn.sigmoid(o_x)) @ w_out
    y_c = None
    if ctx_out:
        y_c = (head_rms_norm(outs_c[0] + outs_c[1], norm_g) * jax.nn.sigmoid(o_c)) @ w_out
    return y_c, y_x


def setup_inputs(seed: int = 0) -> dict:
    key = jax.random.key(seed)
    ks = jax.random.split(key, 24)
    f32 = jnp.float32
    n_even = (DEPTH + 1) // 2
    n_odd = DEPTH // 2

    def w(k, shape, fan_in, gain=1.0):
        return jax.random.normal(k, shape, f32) * (gain * fan_in ** -0.5)

    def gain(k, shape):
        return 1.0 + 0.02 * jax.random.normal(k, shape, f32)

    gate_offset = jnp.repeat(jnp.array([0.0, 3.0, 0.0, 3.0], f32), ML_HEADS)
    return {
        "x": jax.random.normal(ks[0], (BATCH, SEQ, D_MODEL), f32),
        "c": jax.random.normal(ks[1], (BATCH, D_MODEL), f32),
        "ctx": jax.random.normal(ks[2], (BATCH, CTX_LEN, D_MODEL), f32),
        "c_ctx": jax.random.normal(ks[3], (D_MODEL,), f32),
        "norm_mix_g": gain(ks[4], (DEPTH, D_MODEL)),
        "norm_ffn_g": gain(ks[5], (DEPTH, D_MODEL)),
        "mod_w_a": w(ks[6], (DEPTH, D_MODEL, MOD_RANK), D_MODEL),
        "mod_w_b": w(ks[7], (DEPTH, MOD_RANK, 6 * D_MODEL), MOD_RANK, 0.5),
        "mod_b": 0.02 * jax.random.normal(ks[8], (DEPTH, 6 * D_MODEL), f32),
        "even_w_in": w(ks[9], (n_even, D_MODEL, EVEN_IN), D_MODEL),
        "even_w_out": w(ks[10], (n_even, NA_W + HG_V_W, D_MODEL), NA_W + HG_V_W),
        "na_rpb": 0.1 * jax.random.normal(ks[11], (n_even, NA_HEADS, 2 * NA_WIN_H - 1, 2 * NA_WIN_W - 1), f32),
        "hg_lb_logits": jax.random.normal(ks[12], (2, n_even, HG_K_W), f32),
        "hg_norm_g": gain(ks[13], (n_even, HG_V_W)),
        "odd_w_in": w(ks[14], (n_odd, D_MODEL, ODD_IN), D_MODEL),
        "odd_b_gates": gate_offset + 0.1 * jax.random.normal(ks[15], (n_odd, 4 * ML_HEADS), f32),
        "odd_w_out": w(ks[16], (n_odd, ML_V_W, D_MODEL), ML_V_W),
        "ml_norm_g": gain(ks[17], (n_odd, ML_V_W)),
        "ffn_w_up": w(ks[18], (DEPTH, D_MODEL, 2 * FFN_HIDDEN), D_MODEL),
        "ffn_w_down": w(ks[19], (DEPTH, FFN_HIDDEN, D_MODEL), FFN_HIDDEN),
        "final_norm_g": gain(ks[20], (D_MODEL,)),
    }


def reference(x, c, ctx, c_ctx, norm_mix_g, norm_ffn_g, mod_w_a, mod_w_b, mod_b,
              even_w_in, even_w_out, na_rpb, hg_lb_logits, hg_norm_g,
              odd_w_in, odd_b_gates, odd_w_out, ml_norm_g,
              ffn_w_up, ffn_w_down, final_norm_g):
    n = x.shape[1]
    cos, sin = axial_rope_tables(n, ML_QK_DIM)
    sm = jax.nn.softmax(hg_lb_logits.astype(jnp.float32), axis=1)
    lower_bound = jnp.cumsum(sm, axis=1) - sm[:, :1]
    xc = ctx
    for layer in range(DEPTH):
        ctx_out = layer < DEPTH - 1
        sh_x, sc_x, g_x, sh2_x, sc2_x, g2_x = adaln(c, mod_w_a[layer], mod_w_b[layer], mod_b[layer])
        sh_c, sc_c, g_c, sh2_c, sc2_c, g2_c = adaln(c_ctx, mod_w_a[layer], mod_w_b[layer], mod_b[layer])
        hx = rms_norm(x, norm_mix_g[layer]) * (1.0 + sc_x) + sh_x
        hc = rms_norm(xc, norm_mix_g[layer]) * (1.0 + sc_c) + sh_c
        j = layer // 2
        if layer % 2 == 0:
            y_c, y_x = even_mixer(hc, hx, even_w_in[j], even_w_out[j], na_rpb[j],
                                  lower_bound[:, j], hg_norm_g[j], ctx_out)
        else:
            y_c, y_x = odd_mixer(hc, hx, odd_w_in[j], odd_b_gates[j], odd_w_out[j],
                                 ml_norm_g[j], cos, sin, ctx_out)
        x = x + g_x * y_x
        hx2 = rms_norm(x, norm_ffn_g[layer]) * (1.0 + sc2_x) + sh2_x
        x = x + g2_x * swiglu(hx2, ffn_w_up[layer], ffn_w_down[layer])
        if ctx_out:
            xc = xc + g_c * y_c
            hc2 = rms_norm(xc, norm_ffn_g[layer]) * (1.0 + sc2_c) + sh2_c
            xc = xc + g2_c * swiglu(hc2, ffn_w_up[layer], ffn_w_down[layer])
    return rms_norm(x, final_norm_g)
```

```python
import numpy as np
import ml_dtypes
from contextlib import ExitStack
import concourse.bass as bass
import concourse.mybir as mybir
from concourse.bass_utils import run_bass_kernel_spmd

F32 = mybir.dt.float32
BF16 = mybir.dt.bfloat16
AF = mybir.ActivationFunctionType
ALU = mybir.AluOpType
AX = mybir.AxisListType
NPBF = ml_dtypes.bfloat16

D = 4096
KC = 32
EPS = 1e-6


class Buf:
    __slots__ = ("name", "w", "r", "dkey")

    def __init__(self, name):
        self.name = name
        self.w = {}
        self.r = {}
        self.dkey = None


class KB:
    def __init__(self, nc):
        self.nc = nc
        self.eng = {"pe": nc.tensor, "act": nc.scalar, "dve": nc.vector, "pool": nc.gpsimd, "sp": nc.sync}
        self.semh = {}
        self.total = {}
        for e in ("pe", "act", "dve", "pool"):
            self.semh[e] = nc.alloc_semaphore(name="sem_" + e)
            self.total[e] = 0
        self.waited = {e: {} for e in self.eng}
        self.pending = {e: ([], []) for e in self.eng}
        self.ninstr = 0

    def buf(self, name):
        return Buf(name)

    def _wait(self, e, key, val):
        if key == "pe" and e == "pe":
            return
        if self.waited[e].get(key, 0) >= val:
            return
        self.eng[e].wait_ge(self.semh[key], val)
        self.waited[e][key] = val
        self.ninstr += 1

    def _deps(self, e, reads, writes):
        for b in reads:
            for k, v in b.w.items():
                self._wait(e, k, v)
        for b in writes:
            for k, v in b.w.items():
                self._wait(e, k, v)
            for k, v in b.r.items():
                self._wait(e, k, v)

    def op(self, e, fn, reads=(), writes=(), inc=True):
        self._deps(e, reads, writes)
        ins = fn()
        self.ninstr += 1
        pr, pw = self.pending[e]
        if not inc:
            pr.extend(reads)
            pw.extend(writes)
            return ins
        self.total[e] += 1
        ins.then_inc(self.semh[e], 1)
        v = self.total[e]
        for b in list(reads) + pr:
            b.r[e] = v
        for b in list(writes) + pw:
            b.w[e] = v
            b.r = {}
        pr.clear()
        pw.clear()
        return ins

    def dma(self, q, out, in_, ob, ib, **kw):
        if ob.dkey is None:
            ob.dkey = "d_" + ob.name
            self.semh[ob.dkey] = self.nc.alloc_semaphore(name=ob.dkey)
            self.total[ob.dkey] = 0
        self._deps(q, [ib], [ob])
        self.eng[q].dma_start(out=out, in_=in_, **kw).then_inc(self.semh[ob.dkey], 16)
        self.ninstr += 1
        self.total[ob.dkey] += 16
        v = self.total[ob.dkey]
        ib.r[ob.dkey] = v
        ob.w[ob.dkey] = v
        ob.r = {}

    def barrier(self):
        for e in self.eng:
            for k, v in self.total.items():
                if v > 0:
                    self._wait(e, k, v)


def mkap(h, off, pat):
    return bass.AP(h, off, [list(p) for p in pat])


class Base:
    def __init__(self):
        nc = bass.Bass("TRN2", target_bir_lowering=False)
        self.nc = nc
        self.k = KB(nc)
        self.b_in = self.k.buf("ext_in")
        self.es = ExitStack()
        self.ps = []
        self.bps = []
        for i in range(8):
            t = self.es.enter_context(nc.psum_tensor(f"ps{i}", [128, 512], F32))
            self.ps.append(t)
            self.bps.append(self.k.buf(f"ps{i}"))
        self.nbuf = 0

    def din(self, name, shape, dt=F32):
        return self.nc.dram_tensor(name, list(shape), dt, kind="ExternalInput")

    def dout(self, name, shape, dt=F32):
        return self.nc.dram_tensor(name, list(shape), dt, kind="ExternalOutput")

    def dint(self, name, shape, dt):
        return self.nc.dram_tensor(name, list(shape), dt), self.k.buf(name)

    def sb(self, es, name, shape, dt):
        self.nbuf += 1
        name = f"s{self.nbuf}_{name}"
        t = es.enter_context(self.nc.sbuf_tensor(name, list(shape), dt))
        return t, self.k.buf(name)

    def consts(self):
        nc, k = self.nc, self.k
        self.ones, self.b_ones = self.sb(self.es, "ones", [128, 128], BF16)
        k.op("dve", lambda: nc.vector.memset(self.ones[:, :], 1.0), writes=[self.b_ones])

    def finish(self):
        self.k.barrier()
        self.es.close()
        return self.nc

    def cast_tiles(self, src_h, K, N, dst_h, b_dst, es):
        nc, k = self.nc, self.k
        nk = K // 128
        ncc = N // 128
        CW = 2048 if N % 2048 == 0 else (1024 if N % 1024 == 0 else (512 if N % 512 == 0 else 128))
        if not hasattr(self, "_cst"):
            self._cst = [self.sb(es, f"cst{i}", [128, 2048], F32) for i in range(3)]
            self._csb = [self.sb(es, f"csb{i}", [128, 2048], BF16) for i in range(3)]
            self._cit = 0
        for kc in range(nk):
            for c0 in range(0, N, CW):
                it = self._cit
                self._cit += 1
                a, ba = self._cst[it % 3]
                b, bb = self._csb[it % 3]
                k.dma("sp", a[:, 0:CW], mkap(src_h, kc * 128 * N + c0, [[N, 128], [1, CW]]), ba, self.b_in)
                e = ("act", "dve", "pool")[it % 3]
                if e == "act":
                    k.op("act", lambda a=a, b=b: nc.scalar.copy(out=b[:, 0:CW], in_=a[:, 0:CW]), reads=[ba], writes=[bb])
                elif e == "dve":
                    k.op("dve", lambda a=a, b=b: nc.vector.tensor_copy(out=b[:, 0:CW], in_=a[:, 0:CW]), reads=[ba], writes=[bb])
                else:
                    k.op("pool", lambda a=a, b=b: nc.gpsimd.tensor_copy(out=b[:, 0:CW], in_=a[:, 0:CW]), reads=[ba], writes=[bb])
                cc0 = c0 // 128
                dst = mkap(dst_h, cc0 * 128 * nk * 128 + kc * 128, [[nk * 128, 128], [128 * nk * 128, CW // 128], [1, 128]])
                k.dma("pool", dst, b[:, 0:CW].rearrange("p (a c) -> p a c", c=128), b_dst, bb)

    def wtile_view(self, dst_h, nk, cc, k0=0, kn=None):
        kn = nk if kn is None else kn
        return mkap(dst_h, cc * 128 * nk * 128 + k0 * 128, [[nk * 128, 128], [128, kn], [1, 128]])

    def normmod(self, X, bX, SQ, bSQ, OUT, bOUT, R, bR, tmpk, n, gs_ap, sh_ap, bmods):
        nc, k = self.nc, self.k
        P, bP = self.ps[7], self.bps[7]
        k.op("act", lambda: nc.scalar.activation(out=SQ[:, :, 0:n], in_=X[:, :, 0:n], func=AF.Square), reads=[bX], writes=[bSQ])
        for kc in range(KC):
            k.op("pe", lambda kc=kc: nc.tensor.matmul(P[:, 0:n], lhsT=self.ones[:, :], rhs=SQ[:, kc, 0:n], start=(kc == 0), stop=(kc == KC - 1)),
                 reads=[bSQ, self.b_ones], writes=[bP], inc=(kc == KC - 1))
        k.op("dve", lambda: nc.vector.tensor_scalar(out=R[:, 0:n], in0=P[:, 0:n], scalar1=1.0 / D, scalar2=EPS, op0=ALU.mult, op1=ALU.add),
             reads=[bP], writes=[bR])
        k.op("act", lambda: nc.scalar.activation(out=R[:, 0:n], in_=R[:, 0:n], func=AF.Sqrt), reads=[bR], writes=[bR])
        k.op("dve", lambda: nc.vector.reciprocal(out=R[:, 0:n], in_=R[:, 0:n]), reads=[bR], writes=[bR])
        for kc in range(KC):
            tk, btk = tmpk[kc % len(tmpk)]
            k.op("dve", lambda kc=kc, tk=tk: nc.vector.tensor_tensor(out=tk[:, 0:n], in0=X[:, kc, 0:n], in1=R[:, 0:n], op=ALU.mult),
                 reads=[bX, bR], writes=[btk])
            k.op("act", lambda kc=kc, tk=tk: nc.scalar.activation(out=OUT[:, kc, 0:n], in_=tk[:, 0:n], func=AF.Identity,
                                                                  scale=gs_ap(kc), bias=sh_ap(kc)),
                 reads=[btk] + bmods, writes=[bOUT])

    def make_gs(self, gs, bgs, modT, bmod, sc0, ng, bng, tmp, btmp):
        nc, k = self.nc, self.k
        k.op("dve", lambda: nc.vector.tensor_scalar(out=tmp[:, :, :], in0=modT[:, sc0:sc0 + 32, :], scalar1=1.0, scalar2=None, op0=ALU.add),
             reads=[bmod], writes=[btmp])
        k.op("dve", lambda: nc.vector.tensor_tensor(out=gs[:, :, :], in0=tmp[:, :, :], in1=ng[:, 0:32].unsqueeze(2).broadcast_to([128, 32, 2]), op=ALU.mult),
             reads=[btmp, bng], writes=[bgs])


def tblocks(TL, TC, NB=256):
    blocks = []
    for t0 in range(0, TL, NB):
        blocks.append((t0, min(NB, TL - t0), 0))
    blocks.append((TL, TC, 1))
    return blocks


def fm_view(h, ncols_total, t0, n):
    return mkap(h, t0, [[ncols_total, 128], [128 * ncols_total, KC], [1, n]])


def build_M():
    p = Base()
    nc, k = p.nc, p.k
    i_wa = p.din("wa", [D, 256])
    i_wb = p.din("wb", [256, 6 * D])
    i_modb = p.din("modb", [128, 192])
    i_cvec = p.din("cvec", [128, 64])
    o_mod = p.dout("modT", [128, 384])
    b_out = k.buf("o_mod")
    es = p.es
    wa, b_wa = p.sb(es, "wa", [128, 32, 256], F32)
    wb, b_wb = p.sb(es, "wb", [128, 2, 6144], F32)
    cv, b_cv = p.sb(es, "cv", [128, 64], F32)
    cT, b_cT = p.sb(es, "cT", [128, 32, 2], F32)
    hT, b_hT = p.sb(es, "hT", [128, 2, 2], F32)
    mb, b_mb = p.sb(es, "mb", [128, 192], F32)
    mo, b_mo = p.sb(es, "mo", [128, 192, 2], F32)
    k.dma("sp", cv[:, :], i_cvec[:, :], b_cv, p.b_in)
    k.dma("sp", mb[:, :], i_modb[:, :], b_mb, p.b_in)
    k.dma("sp", wa[:, :, :], mkap(i_wa, 0, [[256, 128], [128 * 256, 32], [1, 256]]), b_wa, p.b_in)
    for c in range(2):
        k.op("act", lambda c=c: nc.scalar.activation(out=cT[:, :, c], in_=cv[:, c * 32:(c + 1) * 32], func=AF.Silu), reads=[b_cv], writes=[b_cT])
    P0, bP0 = p.ps[0], p.bps[0]
    P1, bP1 = p.ps[1], p.bps[1]
    for r in range(2):
        for kc in range(32):
            k.op("pe", lambda r=r, kc=kc: nc.tensor.matmul(P0[:, r * 2:(r + 1) * 2], lhsT=wa[:, kc, r * 128:(r + 1) * 128], rhs=cT[:, kc, :],
                                                           start=(kc == 0), stop=(kc == 31)),
                 reads=[b_wa, b_cT], writes=[bP0], inc=(kc == 31))
    k.op("dve", lambda: nc.vector.tensor_copy(out=hT[:, :, :], in_=P0[:, 0:4].rearrange("p (r c) -> p r c", c=2)), reads=[bP0], writes=[b_hT])
    for piece in range(4):
        k.dma("sp", wb[:, :, :], mkap(i_wb, piece * 6144, [[6 * D, 128], [128 * 6 * D, 2], [1, 6144]]), b_wb, p.b_in)
        for jj in range(48):
            jg = piece * 48 + jj
            for r in range(2):
                k.op("pe", lambda r=r, jj=jj, jg=jg: nc.tensor.matmul(P1[:, jg * 2:(jg + 1) * 2], lhsT=wb[:, r, jj * 128:(jj + 1) * 128], rhs=hT[:, r, :],
                                                                      start=(r == 0), stop=(r == 1)),
                     reads=[b_wb, b_hT], writes=[bP1], inc=(r == 1 and jj == 47))
    k.op("dve", lambda: nc.vector.tensor_tensor(out=mo[:, :, :], in0=P1[:, 0:384].rearrange("p (j c) -> p j c", c=2),
                                                in1=mb[:, :].unsqueeze(2).broadcast_to([128, 192, 2]), op=ALU.add),
         reads=[bP1, b_mb], writes=[b_mo])
    k.dma("pool", o_mod[:, :], mo[:, :, :].rearrange("p j c -> p (j c)"), b_out, b_mo)
    return p.finish()


def build_A(TL, TC, out_f32):
    p = Base()
    nc, k = p.nc, p.k
    TLOC = TL + TC
    i_x = p.din("xT", [D, TLOC])
    i_mod = p.din("modT", [128, 384])
    i_ng = p.din("ng", [128, 32])
    odt = F32 if out_f32 else BF16
    o_hx = p.dout("hx", [D, TLOC], odt)
    b_out = k.buf("o_hx")
    es = p.es
    p.consts()
    modT, b_mod = p.sb(es, "modT", [128, 192, 2], F32)
    ng, b_ng = p.sb(es, "ng", [128, 32], F32)
    gs, b_gs = p.sb(es, "gs", [128, 32, 2], F32)
    tmp, b_tmp = p.sb(es, "tmp", [128, 32, 2], F32)
    k.dma("sp", modT[:, :, :].rearrange("p j c -> p (j c)"), i_mod[:, :], b_mod, p.b_in)
    k.dma("sp", ng[:, :], i_ng[:, :], b_ng, p.b_in)
    p.make_gs(gs, b_gs, modT, b_mod, 32, ng, b_ng, tmp, b_tmp)
    Xs = [p.sb(es, f"X{i}", [128, 32, 256], F32) for i in range(2)]
    SQ, bSQ = p.sb(es, "SQ", [128, 32, 256], BF16)
    OUTs = [p.sb(es, f"O{i}", [128, 32, 256], odt) for i in range(2)]
    R, bR = p.sb(es, "R", [128, 256], F32)
    tmpk = [p.sb(es, f"tk{i}", [128, 256], F32) for i in range(3)]
    for bi, (t0, n, cond) in enumerate(tblocks(TL, TC)):
        X, bX = Xs[bi % 2]
        O, bO = OUTs[bi % 2]
        k.dma("sp", X[:, :, 0:n], fm_view(i_x, TLOC, t0, n), bX, p.b_in)
        p.normmod(X, bX, SQ, bSQ, O, bO, R, bR, tmpk, n,
                  lambda kc, cond=cond: gs[:, kc, cond:cond + 1], lambda kc, cond=cond: modT[:, kc, cond:cond + 1], [b_gs, b_mod])
        k.dma("pool", fm_view(o_hx, TLOC, t0, n), O[:, :, 0:n], b_out, bO)
    return p.finish()


def build_T(TL, TC, FF):
    p = Base()
    nc, k = p.nc, p.k
    TLOC = TL + TC
    HC = FF // 128
    i_x = p.din("xT", [D, TLOC])
    i_m = p.din("mT", [D, TLOC], BF16)
    i_wout = p.din("wout", [D, D])
    i_wup = p.din("wup", [D, 2 * FF])
    i_wdn = p.din("wdn", [FF, D])
    i_mod = p.din("modT", [128, 384])
    i_nfg = p.din("nfg", [128, 32])
    i_modn = p.din("modTn", [128, 384])
    i_ngn = p.din("ngn", [128, 32])
    o_x = p.dout("xo", [D, TLOC], F32)
    o_hx = p.dout("hxo", [D, TLOC], BF16)
    b_ox = k.buf("o_x")
    b_ohx = k.buf("o_hx")
    es = p.es
    p.consts()
    woutb, b_woutb = p.dint("woutb", [32 * 128 * 32 * 128], BF16)
    wupb, b_wupb = p.dint("wupb", [(2 * FF // 128) * 128 * 32 * 128], BF16)
    wdnb, b_wdnb = p.dint("wdnb", [32 * 128 * HC * 128], BF16)
    with ExitStack() as es1:
        p.cast_tiles(i_wout, D, D, woutb, b_woutb, es1)
        p.cast_tiles(i_wup, D, 2 * FF, wupb, b_wupb, es1)
        p.cast_tiles(i_wdn, FF, D, wdnb, b_wdnb, es1)
        k.barrier()
    del p._cst
    modT, b_mod = p.sb(es, "modT", [128, 192, 2], F32)
    modn, b_modn = p.sb(es, "modn", [128, 192, 2], F32)
    nfg, b_nfg = p.sb(es, "nfg", [128, 32], F32)
    ngn, b_ngn = p.sb(es, "ngn", [128, 32], F32)
    gs2, b_gs2 = p.sb(es, "gs2", [128, 32, 2], F32)
    gsn, b_gsn = p.sb(es, "gsn", [128, 32, 2], F32)
    tmp, b_tmp = p.sb(es, "tmp", [128, 32, 2], F32)
    k.dma("sp", modT[:, :, :].rearrange("p j c -> p (j c)"), i_mod[:, :], b_mod, p.b_in)
    k.dma("sp", modn[:, :, :].rearrange("p j c -> p (j c)"), i_modn[:, :], b_modn, p.b_in)
    k.dma("sp", nfg[:, :], i_nfg[:, :], b_nfg, p.b_in)
    k.dma("sp", ngn[:, :], i_ngn[:, :], b_ngn, p.b_in)
    p.make_gs(gs2, b_gs2, modT, b_mod, 128, nfg, b_nfg, tmp, b_tmp)
    p.make_gs(gsn, b_gsn, modn, b_modn, 32, ngn, b_ngn, tmp, b_tmp)
    X, bX = p.sb(es, "X", [128, 32, 256], F32)
    MT, bMT = p.sb(es, "MT", [128, 32, 256], BF16)
    HX, bHX = p.sb(es, "HX", [128, 32, 256], BF16)
    Fh, bF = p.sb(es, "F", [128, HC, 256], BF16)
    R, bR = p.sb(es, "R", [128, 256], F32)
    tmpk = [p.sb(es, f"tk{i}", [128, 256], F32) for i in range(3)]
    W8 = [p.sb(es, f"W8_{i}", [128, 32, 128], BF16) for i in range(5)]
    hh = (HC + 1) // 2
    WD = [p.sb(es, f"WD_{i}", [128, hh, 128], BF16) for i in range(3)]
    SG, bSG = p.sb(es, "SG", [128, 256], F32)
    cnt = {"w8": 0, "wd": 0}

    def load_w8(h, bh, cc):
        W, bW = W8[cnt["w8"] % len(W8)]
        cnt["w8"] += 1
        k.dma("sp", W[:, :, :], p.wtile_view(h, 32, cc), bW, bh)
        return W, bW

    for bi, (t0, n, cond) in enumerate(tblocks(TL, TC)):
        k.dma("sp", X[:, :, 0:n], fm_view(i_x, TLOC, t0, n), bX, p.b_in)
        k.dma("sp", MT[:, :, 0:n], fm_view(i_m, TLOC, t0, n), bMT, p.b_in)
        for fo in range(32):
            W, bW = load_w8(woutb, b_woutb, fo)
            P, bP = p.ps[fo % 2], p.bps[fo % 2]
            for kc in range(32):
                k.op("pe", lambda kc=kc, W=W, P=P: nc.tensor.matmul(P[:, 0:n], lhsT=W[:, kc, :], rhs=MT[:, kc, 0:n], start=(kc == 0), stop=(kc == 31)),
                     reads=[bW, bMT], writes=[bP], inc=(kc == 31))
            k.op("dve", lambda fo=fo, P=P: nc.vector.scalar_tensor_tensor(out=X[:, fo, 0:n], in0=P[:, 0:n], scalar=modT[:, 64 + fo, cond:cond + 1],
                                                                         in1=X[:, fo, 0:n], op0=ALU.mult, op1=ALU.add),
                 reads=[bP, b_mod, bX], writes=[bX])
        p.normmod(X, bX, HX, bHX, HX, bHX, R, bR, tmpk, n,
                  lambda kc, cond=cond: gs2[:, kc, cond:cond + 1], lambda kc, cond=cond: modT[:, 96 + kc, cond:cond + 1], [b_gs2, b_mod])
        for hc in range(HC):
            Wa, bWa = load_w8(wupb, b_wupb, hc)
            Wg, bWg = load_w8(wupb, b_wupb, HC + hc)
            Pa, bPa = p.ps[2 + (hc % 2) * 2], p.bps[2 + (hc % 2) * 2]
            Pg, bPg = p.ps[3 + (hc % 2) * 2], p.bps[3 + (hc % 2) * 2]
            for kc in range(32):
                k.op("pe", lambda kc=kc, Wa=Wa, Pa=Pa: nc.tensor.matmul(Pa[:, 0:n], lhsT=Wa[:, kc, :], rhs=HX[:, kc, 0:n], start=(kc == 0), stop=(kc == 31)),
                     reads=[bWa, bHX], writes=[bPa], inc=(kc == 31))
            for kc in range(32):
                k.op("pe", lambda kc=kc, Wg=Wg, Pg=Pg: nc.tensor.matmul(Pg[:, 0:n], lhsT=Wg[:, kc, :], rhs=HX[:, kc, 0:n], start=(kc == 0), stop=(kc == 31)),
                     reads=[bWg, bHX], writes=[bPg], inc=(kc == 31))
            k.op("act", lambda Pg=Pg: nc.scalar.activation(out=SG[:, 0:n], in_=Pg[:, 0:n], func=AF.Silu), reads=[bPg], writes=[bSG])
            k.op("dve", lambda hc=hc, Pa=Pa: nc.vector.tensor_tensor(out=Fh[:, hc, 0:n], in0=Pa[:, 0:n], in1=SG[:, 0:n], op=ALU.mult),
                 reads=[bPa, bSG], writes=[bF])
        for fo in range(32):
            P, bP = p.ps[fo % 2], p.bps[fo % 2]
            for half in range(2):
                h0 = half * hh
                hn = min(hh, HC - h0)
                if hn <= 0:
                    continue
                W, bW = WD[cnt["wd"] % len(WD)]
                cnt["wd"] += 1
                k.dma("sp", W[:, 0:hn, :], p.wtile_view(wdnb, HC, fo, h0, hn), bW, b_wdnb)
                for i in range(hn):
                    hcx = h0 + i
                    k.op("pe", lambda i=i, hcx=hcx, W=W, P=P: nc.tensor.matmul(P[:, 0:n], lhsT=W[:, i, :], rhs=Fh[:, hcx, 0:n],
                                                                               start=(hcx == 0), stop=(hcx == HC - 1)),
                         reads=[bW, bF], writes=[bP], inc=(i == hn - 1))
            k.op("dve", lambda fo=fo, P=P: nc.vector.scalar_tensor_tensor(out=X[:, fo, 0:n], in0=P[:, 0:n], scalar=modT[:, 160 + fo, cond:cond + 1],
                                                                         in1=X[:, fo, 0:n], op0=ALU.mult, op1=ALU.add),
                 reads=[bP, b_mod, bX], writes=[bX])
        k.dma("pool", fm_view(o_x, TLOC, t0, n), X[:, :, 0:n], b_ox, bX)
        p.normmod(X, bX, HX, bHX, HX, bHX, R, bR, tmpk, n,
                  lambda kc, cond=cond: gsn[:, kc, cond:cond + 1], lambda kc, cond=cond: modn[:, kc, cond:cond + 1], [b_gsn, b_modn])
        k.dma("pool", fm_view(o_hx, TLOC, t0, n), HX[:, :, 0:n], b_ohx, bHX)
    return p.finish()


def pblocks(S):
    blocks = [(0, 256)]
    s = 256
    while s < S:
        n = min(512, S - s)
        blocks.append((s, n))
        s += n
    return blocks


class PBase(Base):
    def setup(self, S):
        nc, k = self.nc, self.k
        self.S = S
        self.consts()
        self.i_ident = self.din("ident", [128, 128], BF16)
        self.i_tri = self.din("tri", [64, 256])
        self.ident, self.b_ident = self.sb(self.es, "ident", [128, 128], BF16)
        self.tri, self.b_tri = self.sb(self.es, "tri", [64, 256], F32)
        k.dma("sp", self.ident[:, :], self.i_ident[:, :], self.b_ident, self.b_in)
        k.dma("sp", self.tri[:, :], self.i_tri[:, :], self.b_tri, self.b_in)
        self.rr = 0

    def proj_fm(self, W, bW, ci, HXB, bHXB, n, P, bP, M=128):
        nc, k = self.nc, self.k
        for kc in range(KC):
            k.op("pe", lambda kc=kc: nc.tensor.matmul(P[0:M, 0:n], lhsT=W[:, kc, ci * 128:ci * 128 + M], rhs=HXB[:, kc, 0:n], start=(kc == 0), stop=(kc == KC - 1)),
                 reads=[bW, bHXB], writes=[bP], inc=(kc == KC - 1))

    def proj_tm(self, W, bW, c0, ncols, HXB, bHXB, ts, P, bP):
        nc, k = self.nc, self.k
        for kc in range(KC):
            k.op("pe", lambda kc=kc: nc.tensor.matmul(P[:, 0:ncols], lhsT=HXB[:, kc, ts * 128:(ts + 1) * 128], rhs=W[:, kc, c0:c0 + ncols], start=(kc == 0), stop=(kc == KC - 1)),
                 reads=[bW, bHXB], writes=[bP], inc=(kc == KC - 1))

    def evac_copy(self, out, in_, reads, writes, scale=None, func=None):
        nc, k = self.nc, self.k
        self.rr += 1
        if func is not None or scale is not None or self.rr % 2 == 0:
            f = func if func is not None else AF.Copy
            if scale is None:
                k.op("act", lambda: nc.scalar.activation(out=out, in_=in_, func=f), reads=reads, writes=writes)
            else:
                k.op("act", lambda: nc.scalar.activation(out=out, in_=in_, func=f, scale=scale), reads=reads, writes=writes)
        else:
            k.op("dve", lambda: nc.vector.tensor_copy(out=out, in_=in_), reads=reads, writes=writes)

    def head_post(self, H, bH, nv, T, SQ, bSQ, R, bR, G, bG, gain_ap, b_gain, OUTb, bOUTb, o_m, b_om, row0, s0):
        nc, k = self.nc, self.k
        S = self.S
        P, bP = self.ps[7], self.bps[7]
        k.op("act", lambda: nc.scalar.activation(out=SQ[:, :, 0:T], in_=H[:, :, 0:T], func=AF.Square), reads=[bH], writes=[bSQ])
        for c0 in range(0, T, 512):
            cn = min(512, T - c0)
            for vj in range(nv):
                k.op("pe", lambda vj=vj: nc.tensor.matmul(P[:, 0:cn], lhsT=self.ones[:, :], rhs=SQ[:, vj, c0:c0 + cn], start=(vj == 0), stop=(vj == nv - 1)),
                     reads=[bSQ, self.b_ones], writes=[bP], inc=(vj == nv - 1))
            k.op("dve", lambda: nc.vector.tensor_scalar(out=R[:, 0:cn], in0=P[:, 0:cn], scalar1=1.0 / (128 * nv), scalar2=EPS, op0=ALU.mult, op1=ALU.add),
                 reads=[bP], writes=[bR])
            k.op("act", lambda: nc.scalar.activation(out=R[:, 0:cn], in_=R[:, 0:cn], func=AF.Sqrt), reads=[bR], writes=[bR])
            k.op("dve", lambda: nc.vector.reciprocal(out=R[:, 0:cn], in_=R[:, 0:cn]), reads=[bR], writes=[bR])
            for vj in range(nv):
                k.op("dve", lambda vj=vj: nc.vector.tensor_tensor(out=H[:, vj, c0:c0 + cn], in0=H[:, vj, c0:c0 + cn], in1=R[:, 0:cn], op=ALU.mult),
                     reads=[bH, bR], writes=[bH])
                k.op("dve", lambda vj=vj: nc.vector.scalar_tensor_tensor(out=OUTb[:, vj, c0:c0 + cn], in0=H[:, vj, c0:c0 + cn], scalar=gain_ap(vj),
                                                                        in1=G[:, vj, c0:c0 + cn], op0=ALU.mult, op1=ALU.mult),
                     reads=[bH, bG, b_gain], writes=[bOUTb])
        for vj in range(nv):
            k.dma("pool", mkap(o_m, (row0 + vj * 128) * S + s0, [[S, 128], [1, T]]), OUTb[:, vj, 0:T], b_om, bOUTb)


def build_PE(ROWS):
    p = PBase()
    nc, k = p.nc, p.k
    SEQ = ROWS * 64
    S = SEQ + 256
    NT = ROWS // 2
    i_hx = p.din("hx", [D, S], BF16)
    i_win = p.din("win", [D, 2048])
    i_lbl = p.din("lbl", [128, 8])
    i_jsel = p.din("jsel", [128, 1])
    i_hgg = p.din("hgg", [128, 2])
    i_nab = p.din("nab", [128, 2 * 5 * 640])
    i_rst = p.din("rst", [128, 2048])
    o_m = p.dout("mloc", [512, S], BF16)
    b_om = k.buf("o_m")
    p.setup(S)
    es = p.es
    wbf, b_wbf = p.dint("wbf", [16 * 128 * 32 * 128], BF16)
    na_q, b_naq = p.dint("na_q", [2 * 128 * S], BF16)
    na_k, b_nak = p.dint("na_k", [2 * 128 * S], BF16)
    na_v, b_nav = p.dint("na_v", [S * 256], BF16)
    hg_q, b_hgq = p.dint("hg_q", [2 * 128 * S], F32)
    hg_ff, b_hgff = p.dint("hg_ff", [2 * 128 * S], F32)
    hg_fb, b_hgfb = p.dint("hg_fb", [2 * 128 * S], F32)
    hg_g, b_hgg_ = p.dint("hg_g", [2 * 128 * S], BF16)
    hg_i, b_hgi = p.dint("hg_i", [S * 256], BF16)
    hg_o, b_hgo = p.dint("hg_o", [2 * 128 * S], F32)
    with ExitStack() as e0:
        p.cast_tiles(i_win, D, 2048, wbf, b_wbf, e0)
        k.barrier()
    del p._cst

    def rows_view(h, hidx, s0, n):
        return mkap(h, hidx * 128 * S + s0, [[S, 128], [1, n]])

    with ExitStack() as e1:
        Wfm, bWfm = p.sb(e1, "Wfm", [128, 32, 512], BF16)
        Wtm, bWtm = p.sb(e1, "Wtm", [128, 32, 256], BF16)
        for i in range(4):
            k.dma("sp", Wfm[:, :, i * 128:(i + 1) * 128], p.wtile_view(wbf, 32, i), bWfm, b_wbf)
        for i in range(2):
            k.dma("sp", Wtm[:, :, i * 128:(i + 1) * 128], p.wtile_view(wbf, 32, 4 + i), bWtm, b_wbf)
        HXBs = [p.sb(e1, f"HXB{i}", [128, 32, 512], BF16) for i in range(2)]
        stg = [p.sb(e1, f"stg{i}", [128, 512], BF16) for i in range(4)]
        si = 0
        for bi, (s0, n) in enumerate(pblocks(S)):
            HXB, bHXB = HXBs[bi % 2]
            k.dma("sp", HXB[:, :, 0:n], fm_view(i_hx, S, s0, n), bHXB, p.b_in)
            for ci in range(4):
                P, bP = p.ps[ci % 4], p.bps[ci % 4]
                p.proj_fm(Wfm, bWfm, ci, HXB, bHXB, n, P, bP)
                st, bst = stg[si % 4]
                si += 1
                p.evac_copy(st[:, 0:n], P[:, 0:n], [bP], [bst], scale=(128 ** -0.5 if ci < 2 else None))
                dh, bd = (na_q, b_naq) if ci < 2 else (na_k, b_nak)
                k.dma("pool", rows_view(dh, ci % 2, s0, n), st[:, 0:n], bd, bst)
            for ts in range(n // 128):
                P, bP = p.ps[4 + ts % 2], p.bps[4 + ts % 2]
                p.proj_tm(Wtm, bWtm, 0, 256, HXB, bHXB, ts, P, bP)
                st, bst = stg[si % 4]
                si += 1
                p.evac_copy(st[:, 0:256], P[:, 0:256], [bP], [bst])
                k.dma("pool", mkap(na_v, (s0 + ts * 128) * 256, [[256, 128], [1, 256]]), st[:, 0:256], b_nav, bst)
        k.barrier()

    with ExitStack() as e2:
        kT, bkT = p.sb(e2, "kT", [128, S], BF16)
        qT, bqT = p.sb(e2, "qT", [128, S], BF16)
        V, bV = p.sb(e2, "V", [128, S // 128, 128], BF16)
        bias, bbias = p.sb(e2, "bias", [128, 5, 640], F32)
        Sc, bSc = p.sb(e2, "Sc", [128, 896], F32)
        Pn, bPn = p.sb(e2, "Pn", [128, 896], BF16)
        PT, bPT = p.sb(e2, "PT", [128, 7, 128], BF16)
        mx, bmx = p.sb(e2, "mx", [128, 1], F32)
        sm, bsm = p.sb(e2, "sm", [128, 1], F32)
        ostg = [p.sb(e2, f"ostg{i}", [128, 128], BF16) for i in range(3)]
        oi = 0
        PTp = p.ps[2][:, :].bitcast(BF16)
        bPTp = p.bps[2]

        def softmax_pv(nk, vtiles, h, q0):
            nonlocal oi
            k.op("dve", lambda: nc.vector.tensor_reduce(out=mx[:, :], in_=Sc[:, 0:nk], axis=AX.X, op=ALU.max, negate=True), reads=[bSc], writes=[bmx])
            k.op("act", lambda: nc.scalar.activation(out=Sc[:, 0:nk], in_=Sc[:, 0:nk], func=AF.Exp, bias=mx[:, 0:1], scale=1.0), reads=[bSc, bmx], writes=[bSc])
            k.op("dve", lambda: nc.vector.tensor_reduce(out=sm[:, :], in_=Sc[:, 0:nk], axis=AX.X, op=ALU.add), reads=[bSc], writes=[bsm])
            k.op("dve", lambda: nc.vector.reciprocal(out=sm[:, :], in_=sm[:, :]), reads=[bsm], writes=[bsm])
            k.op("dve", lambda: nc.vector.tensor_scalar(out=Pn[:, 0:nk], in0=Sc[:, 0:nk], scalar1=sm[:, 0:1], scalar2=None, op0=ALU.mult), reads=[bSc, bsm], writes=[bPn])
            nt = nk // 128
            for i in range(nt):
                k.op("pe", lambda i=i: nc.tensor.transpose(out=PTp[:, i * 128:(i + 1) * 128], in_=Pn[:, i * 128:(i + 1) * 128], identity=p.ident[:, :]),
                     reads=[bPn, p.b_ident], writes=[bPTp], inc=(i == nt - 1))
            p.evac_copy(PT[:, 0:nt, :], PTp[:, 0:nt * 128].rearrange("p (a c) -> p a c", c=128), [bPTp], [bPT])
            OA, bOA = p.ps[3], p.bps[3]
            for i, vt in enumerate(vtiles):
                k.op("pe", lambda i=i, vt=vt: nc.tensor.matmul(OA[:, 0:128], lhsT=V[:, vt, :], rhs=PT[:, i, :], start=(i == 0), stop=(i == nt - 1)),
                     reads=[bV, bPT], writes=[bOA], inc=(i == nt - 1))
            st, bst = ostg[oi % 3]
            oi += 1
            p.evac_copy(st[:, :], OA[:, 0:128], [bOA], [bst])
            k.dma("pool", mkap(o_m, h * 128 * S + q0, [[S, 128], [1, 128]]), st[:, :], b_om, bst)

        for h in range(2):
            k.dma("sp", kT[:, :], rows_view(na_k, h, 0, S), bkT, b_nak)
            k.dma("sp", qT[:, :], rows_view(na_q, h, 0, S), bqT, b_naq)
            k.dma("sp", V[:, :, :], mkap(na_v, h * 128, [[256, 128], [128 * 256, S // 128], [1, 128]]), bV, b_nav)
            k.dma("sp", bias[:, :, :].rearrange("p a b -> p (a b)"), i_nab[:, h * 3200:(h + 1) * 3200], bbias, p.b_in)
            SA, bSA = p.ps[0], p.bps[0]
            SB, bSB = p.ps[1], p.bps[1]
            for qb in range(2):
                q0 = qb * 128
                k.op("pe", lambda: nc.tensor.matmul(SB[:, 0:256], lhsT=qT[:, q0:q0 + 128], rhs=kT[:, 0:256], start=True, stop=True),
                     reads=[bqT, bkT], writes=[bSB])
                p.evac_copy(Sc[:, 0:256], SB[:, 0:256], [bSB], [bSc])
                softmax_pv(256, [0, 1], h, q0)
            for j in range(NT):
                ts = min(max(j - 2, 0), NT - 5)
                typ = 0 if j == 0 else (1 if j == 1 else (3 if j == NT - 2 else (4 if j == NT - 1 else 2)))
                q0 = 256 + j * 128
                kl0 = 256 + ts * 128
                k.op("pe", lambda: nc.tensor.matmul(SA[:, 0:512], lhsT=qT[:, q0:q0 + 128], rhs=kT[:, kl0:kl0 + 512], start=True, stop=True),
                     reads=[bqT, bkT], writes=[bSA])
                k.op("pe", lambda: nc.tensor.matmul(SB[:, 0:128], lhsT=qT[:, q0:q0 + 128], rhs=kT[:, kl0 + 512:kl0 + 640], start=True, stop=True),
                     reads=[bqT, bkT], writes=[bSB], inc=False)
                k.op("pe", lambda: nc.tensor.matmul(SB[:, 128:384], lhsT=qT[:, q0:q0 + 128], rhs=kT[:, 0:256], start=True, stop=True),
                     reads=[bqT, bkT], writes=[bSB])
                k.op("dve", lambda: nc.vector.tensor_tensor(out=Sc[:, 0:512], in0=SA[:, 0:512], in1=bias[:, typ, 0:512], op=ALU.add), reads=[bSA, bbias], writes=[bSc])
                k.op("dve", lambda: nc.vector.tensor_tensor(out=Sc[:, 512:640], in0=SB[:, 0:128], in1=bias[:, typ, 512:640], op=ALU.add), reads=[bSB, bbias], writes=[bSc])
                k.op("act", lambda: nc.scalar.copy(out=Sc[:, 640:896], in_=SB[:, 128:384]), reads=[bSB], writes=[bSc])
                softmax_pv(896, [2 + ts + i for i in range(5)] + [0, 1], h, q0)
        k.barrier()

    with ExitStack() as e3:
        Wfm, bWfm = p.sb(e3, "Wfm3", [128, 32, 1024], BF16)
        Wtm, bWtm = p.sb(e3, "Wtm3", [128, 32, 256], BF16)
        for i, cc in enumerate([6, 7, 8, 9, 10, 11, 14, 15]):
            k.dma("sp", Wfm[:, :, i * 128:(i + 1) * 128], p.wtile_view(wbf, 32, cc), bWfm, b_wbf)
        for i in range(2):
            k.dma("sp", Wtm[:, :, i * 128:(i + 1) * 128], p.wtile_view(wbf, 32, 12 + i), bWtm, b_wbf)
        HXBs = [p.sb(e3, f"HXB3{i}", [128, 32, 512], BF16) for i in range(2)]
        stgf = [p.sb(e3, f"stgf{i}", [128, 512], F32) for i in range(4)]
        stgb = [p.sb(e3, f"stgb{i}", [128, 512], BF16) for i in range(3)]
        si = 0
        sj = 0
        dests = [(hg_q, b_hgq), (hg_ff, b_hgff), (hg_fb, b_hgfb), (hg_g, b_hgg_)]
        for bi, (s0, n) in enumerate(pblocks(S)):
            HXB, bHXB = HXBs[bi % 2]
            k.dma("sp", HXB[:, :, 0:n], fm_view(i_hx, S, s0, n), bHXB, p.b_in)
            for ci in range(8):
                P, bP = p.ps[ci % 4], p.bps[ci % 4]
                p.proj_fm(Wfm, bWfm, ci, HXB, bHXB, n, P, bP)
                dh, bd = dests[ci // 2]
                if ci < 6:
                    st, bst = stgf[si % 4]
                    si += 1
                    p.evac_copy(st[:, 0:n], P[:, 0:n], [bP], [bst])
                else:
                    st, bst = stgb[sj % 3]
                    sj += 1
                    p.evac_copy(st[:, 0:n], P[:, 0:n], [bP], [bst], func=AF.Silu)
                k.dma("pool", rows_view(dh, ci % 2, s0, n), st[:, 0:n], bd, bst)
            for ts in range(n // 128):
                P, bP = p.ps[4 + ts % 2], p.bps[4 + ts % 2]
                p.proj_tm(Wtm, bWtm, 0, 256, HXB, bHXB, ts, P, bP)
                st, bst = stgb[sj % 3]
                sj += 1
                p.evac_copy(st[:, 0:256], P[:, 0:256], [bP], [bst])
                k.dma("pool", mkap(hg_i, (s0 + ts * 128) * 256, [[256, 128], [1, 256]]), st[:, 0:256], b_hgi, bst)
        k.barrier()

    with ExitStack() as e4:
        TM = 2048
        lbl, blbl = p.sb(e4, "lbl", [128, 8], F32)
        jsel, bjsel = p.sb(e4, "jsel", [128, 1], F32)
        hgg, bhgg = p.sb(e4, "hgg", [128, 2], F32)
        rst, brst = p.sb(e4, "rst", [128, TM], F32)
        lb, blb = p.sb(e4, "lb", [128, 4], F32)
        oml, boml = p.sb(e4, "oml", [128, 4], F32)
        k.dma("sp", lbl[:, :], i_lbl[:, :], blbl, p.b_in)
        k.dma("sp", jsel[:, :], i_jsel[:, :], bjsel, p.b_in)
        k.dma("sp", hgg[:, :], i_hgg[:, :], bhgg, p.b_in)
        k.dma("sp", rst[:, :], i_rst[:, :], brst, p.b_in)
        k.op("dve", lambda: nc.vector.tensor_tensor(out=lb[:, :], in0=lbl[:, 4:8], in1=lbl[:, 0:4], op=ALU.subtract), reads=[blbl], writes=[blb])
        k.op("act", lambda: nc.scalar.activation(out=lb[:, :], in_=lb[:, :], func=AF.Sigmoid), reads=[blb], writes=[blb])
        k.op("dve", lambda: nc.vector.tensor_scalar(out=lb[:, :], in0=lb[:, :], scalar1=jsel[:, 0:1], scalar2=None, op0=ALU.mult), reads=[blb, bjsel], writes=[blb])
        k.op("dve", lambda: nc.vector.tensor_scalar(out=oml[:, :], in0=lb[:, :], scalar1=-1.0, scalar2=1.0, op0=ALU.mult, op1=ALU.add), reads=[blb], writes=[boml])
        q, bq = p.sb(e4, "q", [128, TM], F32)
        fp, bfp = p.sb(e4, "fp", [128, TM], F32)
        F1, bF1 = p.sb(e4, "F1", [128, TM], F32)
        K1, bK1 = p.sb(e4, "K1", [128, TM], F32)
        Bc, bBc = p.sb(e4, "Bc", [128, TM], F32)
        CUM, bCUM = p.sb(e4, "CUM", [128, TM], F32)
        EX, bEX = p.sb(e4, "EX", [128, TM], F32)
        QT, bQT = p.sb(e4, "QT", [128, TM], BF16)
        KTb, bKTb = p.sb(e4, "KTb", [128, TM], BF16)
        KH, bKH = p.sb(e4, "KH", [128, TM], BF16)
        EL, bEL = p.sb(e4, "EL", [128, TM // 32], F32)
        Vt, bVt = p.sb(e4, "Vt", [32, TM // 32, 128], BF16)
        O, bO = p.sb(e4, "O", [128, 1, TM], F32)
        Ofw, bOfw = p.sb(e4, "Ofw", [128, TM], F32)
        G, bG = p.sb(e4, "G", [128, 1, TM], BF16)
        SQ, bSQ = p.sb(e4, "SQ4", [128, 1, TM], BF16)
        OUTb, bOUTb = p.sb(e4, "OUTb", [128, 1, TM], BF16)
        R, bR = p.sb(e4, "R4", [128, 512], F32)
        St, bSt = p.sb(e4, "St", [128, 128], F32)
        Sbf, bSbf = p.sb(e4, "Sbf", [128, 128], BF16)
        A, bA = p.sb(e4, "A", [32, 32], BF16)
        KHt, bKHt = p.sb(e4, "KHt", [32, 128], BF16)
        sbs = [(0, 256)] + [(256 + i * TM, min(TM, SEQ - i * TM)) for i in range((SEQ + TM - 1) // TM)]
        KHp = p.ps[1][:, :].bitcast(BF16)
        for h in range(2):
            for d in range(2):
                col = d * 2 + h
                k.op("dve", lambda: nc.vector.memset(St[:, :], 0.0), writes=[bSt])
                k.op("dve", lambda: nc.vector.memset(Sbf[:, :], 0.0), writes=[bSbf])
                order = sbs if d == 0 else [sbs[0]] + sbs[:0:-1]
                for (s0, T) in order:
                    nch = T // 32
                    fph, bfph = (hg_ff, b_hgff) if d == 0 else (hg_fb, b_hgfb)
                    k.dma("sp", q[:, 0:T], rows_view(hg_q, h, s0, T), bq, b_hgq)
                    k.dma("sp", fp[:, 0:T], rows_view(fph, h, s0, T), bfp, bfph)
                    k.dma("sp", Vt[:, 0:nch, :], mkap(hg_i, s0 * 256 + h * 128, [[256, 32], [32 * 256, nch], [1, 128]]), bVt, b_hgi)
                    if d == 1:
                        k.dma("sp", Ofw[:, 0:T], rows_view(hg_o, h, s0, T), bOfw, b_hgo)
                        k.dma("sp", G[:, 0, 0:T], rows_view(hg_g, h, s0, T), bG, b_hgg_)
                    k.op("act", lambda: nc.scalar.activation(out=F1[:, 0:T], in_=fp[:, 0:T], func=AF.Sigmoid), reads=[bfp], writes=[bF1])
                    k.op("dve", lambda: nc.vector.tensor_scalar(out=F1[:, 0:T], in0=F1[:, 0:T], scalar1=oml[:, col:col + 1], scalar2=lb[:, col:col + 1], op0=ALU.mult, op1=ALU.add),
                         reads=[bF1, boml, blb], writes=[bF1])
                    k.op("act", lambda: nc.scalar.activation(out=K1[:, 0:T], in_=fp[:, 0:T], func=AF.Sigmoid, scale=-1.0), reads=[bfp], writes=[bK1])
                    k.op("pool", lambda: nc.gpsimd.tensor_scalar(out=K1[:, 0:T], in0=K1[:, 0:T], scalar1=oml[:, col:col + 1], scalar2=None, op0=ALU.mult),
                         reads=[bK1, boml], writes=[bK1])
                    k.op("act", lambda: nc.scalar.activation(out=F1[:, 0:T], in_=F1[:, 0:T], func=AF.Ln), reads=[bF1], writes=[bF1])
                    k.op("dve", lambda: nc.vector.tensor_tensor_scan(out=Bc[:, 0:T], data0=rst[:, 0:T], data1=F1[:, 0:T], initial=0.0, op0=ALU.mult, op1=ALU.add),
                         reads=[brst, bF1], writes=[bBc])
                    B3 = Bc[:, 0:T].rearrange("p (c s) -> p c s", s=32)
                    if d == 0:
                        cum = Bc
                        bcum = bBc
                        C3 = B3
                        last_ap = B3[:, :, 31]
                    else:
                        k.op("dve", lambda: nc.vector.tensor_tensor(out=CUM[:, 0:T], in0=F1[:, 0:T], in1=Bc[:, 0:T], op=ALU.subtract), reads=[bF1, bBc], writes=[bCUM])
                        C3 = CUM[:, 0:T].rearrange("p (c s) -> p c s", s=32)
                        k.op("dve", lambda: nc.vector.tensor_tensor(out=C3, in0=C3, in1=B3[:, :, 31:32].broadcast_to([128, nch, 32]), op=ALU.add), reads=[bCUM, bBc], writes=[bCUM])
                        cum = CUM
                        bcum = bCUM
                        last_ap = C3[:, :, 0]
                    k.op("act", lambda: nc.scalar.activation(out=EX[:, 0:T], in_=cum[:, 0:T], func=AF.Exp), reads=[bcum], writes=[bEX])
                    k.op("dve", lambda: nc.vector.tensor_tensor(out=QT[:, 0:T], in0=q[:, 0:T], in1=EX[:, 0:T], op=ALU.mult), reads=[bq, bEX], writes=[bQT])
                    k.op("act", lambda: nc.scalar.activation(out=EX[:, 0:T], in_=cum[:, 0:T], func=AF.Exp, scale=-1.0), reads=[bcum], writes=[bEX])
                    k.op("dve", lambda: nc.vector.tensor_tensor(out=K1[:, 0:T], in0=K1[:, 0:T], in1=EX[:, 0:T], op=ALU.mult), reads=[bK1, bEX], writes=[bK1])
                    k.op("act", lambda: nc.scalar.copy(out=KTb[:, 0:T], in_=K1[:, 0:T]), reads=[bK1], writes=[bKTb])
                    k.op("act", lambda: nc.scalar.activation(out=EL[:, 0:nch], in_=last_ap, func=AF.Exp), reads=[bcum], writes=[bEL])
                    k.op("dve", lambda: nc.vector.tensor_tensor(out=KH[:, 0:T].rearrange("p (c s) -> p c s", s=32), in0=K1[:, 0:T].rearrange("p (c s) -> p c s", s=32),
                                                                in1=EL[:, 0:nch].unsqueeze(2).broadcast_to([128, nch, 32]), op=ALU.mult),
                         reads=[bK1, bEL], writes=[bKH])
                    crange = range(nch) if d == 0 else range(nch - 1, -1, -1)
                    moff = 0 if d == 0 else 32
                    for c in crange:
                        cs = slice(c * 32, (c + 1) * 32)
                        AT, bAT = p.ps[0], p.bps[0]
                        k.op("pe", lambda: nc.tensor.matmul(AT[0:32, 0:32], lhsT=KTb[:, cs], rhs=QT[:, cs], start=True, stop=True), reads=[bKTb, bQT], writes=[bAT])
                        k.op("dve", lambda: nc.vector.tensor_tensor(out=A[:, :], in0=AT[0:32, 0:32], in1=p.tri[0:32, moff:moff + 32], op=ALU.mult), reads=[bAT, p.b_tri], writes=[bA])
                        k.op("pe", lambda: nc.tensor.transpose(out=KHp[0:32, 0:128], in_=KH[:, cs], identity=p.ident[:, :]), reads=[bKH, p.b_ident], writes=[p.bps[1]])
                        k.op("act", lambda: nc.scalar.copy(out=KHt[:, :], in_=KHp[0:32, 0:128]), reads=[p.bps[1]], writes=[bKHt])
                        OP, bOP = p.ps[2], p.bps[2]
                        k.op("pe", lambda: nc.tensor.matmul(OP[:, 0:32], lhsT=Vt[:, c, :], rhs=A[:, :], start=True, stop=False), reads=[bVt, bA], writes=[bOP], inc=False)
                        k.op("pe", lambda: nc.tensor.matmul(OP[:, 0:32], lhsT=Sbf[:, :], rhs=QT[:, cs], start=False, stop=True), reads=[bSbf, bQT], writes=[bOP])
                        UP, bUP = p.ps[3], p.bps[3]
                        k.op("pe", lambda: nc.tensor.matmul(UP[:, 0:128], lhsT=KHt[:, :], rhs=Vt[:, c, :], start=True, stop=True), reads=[bKHt, bVt], writes=[bUP])
                        if d == 0:
                            k.op("act", lambda: nc.scalar.copy(out=O[:, 0, cs], in_=OP[:, 0:32]), reads=[bOP], writes=[bO])
                        else:
                            k.op("pool" if False else "dve", lambda: nc.vector.tensor_tensor(out=O[:, 0, cs], in0=OP[:, 0:32], in1=Ofw[:, cs], op=ALU.add), reads=[bOP, bOfw], writes=[bO])
                        k.op("dve", lambda: nc.vector.scalar_tensor_tensor(out=St[:, :], in0=St[:, :], scalar=EL[:, c:c + 1], in1=UP[:, 0:128], op0=ALU.mult, op1=ALU.add),
                             reads=[bSt, bEL, bUP], writes=[bSt])
                        k.op("act", lambda: nc.scalar.copy(out=Sbf[:, :], in_=St[:, :]), reads=[bSt], writes=[bSbf])
                    if d == 0:
                        k.dma("pool", rows_view(hg_o, h, s0, T), O[:, 0, 0:T], b_hgo, bO)
                    else:
                        p.head_post(O, bO, 1, T, SQ, bSQ, R, bR, G, bG, lambda vj: hgg[:, h:h + 1], bhgg, OUTb, bOUTb, o_m, b_om, 256 + h * 128, s0)
        k.barrier()
    return p.finish()


def col_layout(v):
    v = np.asarray(v)
    return np.ascontiguousarray(v.reshape(-1, 128).T)


def const_tables():
    s = np.arange(64)
    tri = np.zeros((64, 256), np.float32)
    f32_ = (s[:32, None] <= s[None, :32]).astype(np.float32)
    tri[:32, 0:32] = f32_
    tri[:32, 32:64] = f32_.T
    f64_ = (s[:, None] <= s[None, :]).astype(np.float32)
    tri[:, 64:128] = f64_
    tri[:, 128:192] = f64_.T
    rst = np.ones((128, 2048), np.float32)
    rst[:, ::32] = 0.0
    rst64 = np.ones((1, 2048), np.float32)
    rst64[:, ::64] = 0.0
    ident = np.eye(128, dtype=np.float32).astype(NPBF)
    return tri, rst, rst64, ident


def prep_even_win(w, c):
    return np.ascontiguousarray(np.concatenate([w[:, sec * 2048 + 256 * c: sec * 2048 + 256 * c + 256] for sec in range(8)], axis=1))


def na_bias_tables(rpb, c, ROWS):
    NT = ROWS // 2
    out = np.empty((2, 5, 128, 640), np.float32)
    qi = np.arange(128)
    ki = np.arange(640)
    for ti, j in enumerate([0, 1, 2, NT - 2, NT - 1]):
        ts = min(max(j - 2, 0), NT - 5)
        r = 2 * j + qi // 64
        cq = qi % 64
        kr = 2 * ts + ki // 64
        kc = ki % 64
        r0 = np.clip(r - 4, 0, ROWS - 8)
        c0 = np.clip(cq - 8, 0, 64 - 16)
        inwin = ((kr[None, :] >= r0[:, None]) & (kr[None, :] < r0[:, None] + 8) &
                 (kc[None, :] >= c0[:, None]) & (kc[None, :] < c0[:, None] + 16))
        di = np.clip(kr[None, :] - r[:, None] + 7, 0, 14)
        dj = np.clip(kc[None, :] - cq[:, None] + 15, 0, 30)
        for h in range(2):
            g = rpb[2 * c + h][di, dj]
            out[h, ti] = np.where(inwin, g, np.float32(-30000.0))
    return np.ascontiguousarray(out.transpose(2, 0, 1, 3).reshape(128, 2 * 5 * 640))


def even_small_params(lbl_all, j, hgg_all, c):
    lbl = np.empty((128, 8), np.float32)
    hg = np.empty((128, 2), np.float32)
    for d in range(2):
        for h in range(2):
            sl = slice((2 * c + h) * 128, (2 * c + h + 1) * 128)
            lbl[:, d * 2 + h] = lbl_all[d, 0, sl]
            lbl[:, 4 + d * 2 + h] = lbl_all[d, 1, sl]
    for h in range(2):
        hg[:, h] = hgg_all[j, (2 * c + h) * 128:(2 * c + h + 1) * 128]
    jsel = np.full((128, 1), float(j), np.float32)
    return lbl, jsel, hg


def build_PO(ROWS):
    p = PBase()
    nc, k = p.nc, p.k
    SEQ = ROWS * 64
    S = SEQ + 256
    i_hx = p.din("hx", [D, S], BF16)
    i_win = p.din("win", [D, 2048])
    i_wg = p.din("wg", [D, 4])
    i_bg = p.din("bg", [1, 4])
    i_mlg = p.din("mlg", [128, 4])
    i_ropeR = p.din("ropeR", [128, 2 * ROWS])
    i_ropeC = p.din("ropeC", [128, 128])
    i_rst64 = p.din("rst64", [1, 2048])
    o_m = p.dout("mloc", [512, S], BF16)
    b_om = k.buf("o_m")
    p.setup(S)
    es = p.es
    wbf, b_wbf = p.dint("wbf", [16 * 128 * 32 * 128], BF16)
    ml_q, b_mlq = p.dint("ml_q", [2 * 128 * S], BF16)
    ml_k, b_mlk = p.dint("ml_k", [2 * 128 * S], BF16)
    ml_o, b_mlo = p.dint("ml_o", [4 * 128 * S], BF16)
    ml_v, b_mlv = p.dint("ml_v", [S * 512], BF16)
    ml_g, b_mlg_ = p.dint("ml_g", [4 * S], F32)
    ml_h, b_mlh = p.dint("ml_h", [4 * 128 * S], F32)
    with ExitStack() as e0:
        p.cast_tiles(i_win, D, 2048, wbf, b_wbf, e0)
        k.barrier()
    del p._cst

    def rows_view(h, hidx, s0, n):
        return mkap(h, hidx * 128 * S + s0, [[S, 128], [1, n]])

    with ExitStack() as e1:
        Wfm, bWfm = p.sb(e1, "Wfm", [128, 32, 1024], BF16)
        for i in range(8):
            k.dma("sp", Wfm[:, :, i * 128:(i + 1) * 128], p.wtile_view(wbf, 32, i), bWfm, b_wbf)
        wg32, bwg32 = p.sb(e1, "wg32", [128, 32, 4], F32)
        Wg, bWg = p.sb(e1, "Wg", [128, 32, 4], BF16)
        k.dma("sp", wg32[:, :, :], mkap(i_wg, 0, [[4, 128], [128 * 4, 32], [1, 4]]), bwg32, p.b_in)
        k.op("dve", lambda: nc.vector.tensor_copy(out=Wg[:, :, :], in_=wg32[:, :, :]), reads=[bwg32], writes=[bWg])
        ropeR, bropeR = p.sb(e1, "ropeR", [128, 2 * ROWS], F32)
        ropeC, bropeC = p.sb(e1, "ropeC", [128, 128], F32)
        k.dma("sp", ropeR[:, :], i_ropeR[:, :], bropeR, p.b_in)
        k.dma("sp", ropeC[:, :], i_ropeC[:, :], bropeC, p.b_in)
        HXBs = [p.sb(e1, f"HXB{i}", [128, 32, 512], BF16) for i in range(2)]
        stg = [p.sb(e1, f"stg{i}", [128, 512], BF16) for i in range(4)]
        T1, bT1 = p.sb(e1, "T1", [128, 512], F32)
        T2, bT2 = p.sb(e1, "T2", [128, 512], F32)
        gst, bgst = p.sb(e1, "gst", [4, 512], F32)
        si = 0
        for bi, (s0, n) in enumerate(pblocks(S)):
            HXB, bHXB = HXBs[bi % 2]
            k.dma("sp", HXB[:, :, 0:n], fm_view(i_hx, S, s0, n), bHXB, p.b_in)
            for which in range(2):
                scale = 1.0 if which == 0 else 256 ** -0.5
                dh, bd = (ml_q, b_mlq) if which == 0 else (ml_k, b_mlk)
                for c in range(2):
                    cc = which * 4 + c
                    P1, bP1 = p.ps[(which * 2 + c) % 2 * 2], p.bps[(which * 2 + c) % 2 * 2]
                    p.proj_fm(Wfm, bWfm, cc, HXB, bHXB, n, P1, bP1)
                    st, bst = stg[si % 4]
                    si += 1
                    if s0 < 256:
                        p.evac_copy(st[:, 0:n], P1[:, 0:n], [bP1], [bst], scale=scale)
                    else:
                        P2, bP2 = p.ps[(which * 2 + c) % 2 * 2 + 1], p.bps[(which * 2 + c) % 2 * 2 + 1]
                        p.proj_fm(Wfm, bWfm, cc + 2, HXB, bHXB, n, P2, bP2)
                        t0 = s0 - 256
                        r0, nr = t0 // 64, n // 64
                        if c == 0:
                            cosb = ropeR[:, r0:r0 + nr].unsqueeze(2).broadcast_to([128, nr, 64])
                            sinb = ropeR[:, ROWS + r0:ROWS + r0 + nr].unsqueeze(2).broadcast_to([128, nr, 64])
                        else:
                            cosb = ropeC[:, 0:64].unsqueeze(1).broadcast_to([128, nr, 64])
                            sinb = ropeC[:, 64:128].unsqueeze(1).broadcast_to([128, nr, 64])
                        v3 = lambda t: t[:, 0:n].rearrange("p (r c) -> p r c", c=64)
                        k.op("dve", lambda: nc.vector.tensor_tensor(out=v3(T1), in0=v3(P1), in1=cosb, op=ALU.mult), reads=[bP1, bropeR, bropeC], writes=[bT1])
                        k.op("dve", lambda: nc.vector.tensor_tensor(out=v3(T2), in0=v3(P2), in1=sinb, op=ALU.mult), reads=[bP2, bropeR, bropeC], writes=[bT2])
                        k.op("pool", lambda: nc.gpsimd.tensor_tensor(out=T1[:, 0:n], in0=T1[:, 0:n], in1=T2[:, 0:n], op=ALU.add), reads=[bT1, bT2], writes=[bT1])
                        k.op("act", lambda: nc.scalar.activation(out=st[:, 0:n], in_=T1[:, 0:n], func=AF.Copy, scale=scale), reads=[bT1], writes=[bst])
                    k.dma("pool", rows_view(dh, c, s0, n), st[:, 0:n], bd, bst)
            PG, bPG = p.ps[6], p.bps[6]
            p.proj_fm(Wg, bWg, 0, HXB, bHXB, n, PG, bPG, M=4)
            k.op("dve", lambda: nc.vector.tensor_copy(out=gst[:, 0:n], in_=PG[0:4, 0:n]), reads=[bPG], writes=[bgst])
            k.dma("pool", mkap(ml_g, s0, [[S, 4], [1, n]]), gst[:, 0:n], b_mlg_, bgst)
        k.barrier()

    with ExitStack() as e1:
        Wfm, bWfm = p.sb(e1, "Wfmb", [128, 32, 512], BF16)
        Wtm, bWtm = p.sb(e1, "Wtmb", [128, 32, 512], BF16)
        for i in range(4):
            k.dma("sp", Wfm[:, :, i * 128:(i + 1) * 128], p.wtile_view(wbf, 32, 8 + i), bWfm, b_wbf)
            k.dma("sp", Wtm[:, :, i * 128:(i + 1) * 128], p.wtile_view(wbf, 32, 12 + i), bWtm, b_wbf)
        HXBs = [p.sb(e1, f"HXBb{i}", [128, 32, 512], BF16) for i in range(2)]
        stg = [p.sb(e1, f"stgb{i}", [128, 512], BF16) for i in range(4)]
        si = 0
        for bi, (s0, n) in enumerate(pblocks(S)):
            HXB, bHXB = HXBs[bi % 2]
            k.dma("sp", HXB[:, :, 0:n], fm_view(i_hx, S, s0, n), bHXB, p.b_in)
            for ci in range(4):
                P, bP = p.ps[ci % 4], p.bps[ci % 4]
                p.proj_fm(Wfm, bWfm, ci, HXB, bHXB, n, P, bP)
                st, bst = stg[si % 4]
                si += 1
                p.evac_copy(st[:, 0:n], P[:, 0:n], [bP], [bst], func=AF.Sigmoid)
                k.dma("pool", rows_view(ml_o, ci, s0, n), st[:, 0:n], b_mlo, bst)
            for ts in range(n // 128):
                P, bP = p.ps[4 + ts % 2], p.bps[4 + ts % 2]
                p.proj_tm(Wtm, bWtm, 0, 512, HXB, bHXB, ts, P, bP)
                st, bst = stg[si % 4]
                si += 1
                p.evac_copy(st[:, 0:512], P[:, 0:512], [bP], [bst])
                k.dma("pool", mkap(ml_v, (s0 + ts * 128) * 512, [[512, 128], [1, 512]]), st[:, 0:512], b_mlv, bst)
        k.barrier()

    with ExitStack() as e2:
        TM = 1024
        NCH = TM // 64
        bg, bbg = p.sb(e2, "bg", [1, 4], F32)
        mlg, bmlg = p.sb(e2, "mlg", [128, 4], F32)
        rst64, brst = p.sb(e2, "rst64", [1, TM], F32)
        onesf, bonesf = p.sb(e2, "onesf", [1, 128], F32)
        k.dma("sp", bg[:, :], i_bg[:, :], bbg, p.b_in)
        k.dma("sp", mlg[:, :], i_mlg[:, :], bmlg, p.b_in)
        k.dma("sp", rst64[:, :], i_rst64[:, 0:TM], brst, p.b_in)
        k.op("dve", lambda: nc.vector.memset(onesf[:, :], 1.0), writes=[bonesf])
        ipre, bipre = p.sb(e2, "ipre", [1, TM], F32)
        fpre, bfpre = p.sb(e2, "fpre", [1, TM], F32)
        brow, bbrow = p.sb(e2, "brow", [1, TM], F32)
        urow, burow = p.sb(e2, "urow", [1, TM], F32)
        wsrow, bwsrow = p.sb(e2, "wsrow", [1, TM], F32)
        erow, berow = p.sb(e2, "erow", [1, TM], F32)
        umax, bumax = p.sb(e2, "umax", [1, NCH], F32)
        Mst, bMst = p.sb(e2, "Mst", [1, NCH], F32)
        marr, bmarr = p.sb(e2, "marr", [1, NCH + 1], F32)
        wprow, bwprow = p.sb(e2, "wprow", [1, NCH], F32)
        mcar, bmcar = p.sb(e2, "mcar", [1, 1], F32)
        kT, bkT = p.sb(e2, "kT", [128, 2, TM], BF16)
        qT, bqT = p.sb(e2, "qT", [128, 2, TM], BF16)
        KH, bKH = p.sb(e2, "KH", [128, 2, TM], BF16)
        Vp, bVp = p.sb(e2, "Vp", [64, NCH, 640], BF16)
        WSbc, bWSbc = p.sb(e2, "WSbc", [128, TM], F32)
        Ebc, bEbc = p.sb(e2, "Ebc", [128, TM], F32)
        Wp, bWp = p.sb(e2, "Wp", [128, NCH], F32)
        O, bO = p.sb(e2, "O", [128, 4, TM], F32)
        Ofw, bOfw = p.sb(e2, "Ofw", [128, 4, TM], F32)
        OG, bOG = p.sb(e2, "OG", [128, 4, TM], BF16)
        SQ, bSQ = p.sb(e2, "SQ", [128, 4, TM], BF16)
        OUTb, bOUTb = p.sb(e2, "OUTb", [128, 4, TM], BF16)
        R, bR = p.sb(e2, "R", [128, 512], F32)
        Cst, bCst = p.sb(e2, "Cst", [128, 2, 640], F32)
        Cbf, bCbf = p.sb(e2, "Cbf", [128, 2, 640], BF16)
        Kt, bKt = p.sb(e2, "Kt", [64, 256], BF16)
        A, bA = p.sb(e2, "A", [64, 64], BF16)
        DN, bDN = p.sb(e2, "DN", [128, 64], F32)
        TH, bTH = p.sb(e2, "TH", [128, 4, 64], F32)
        k.op("dve", lambda: nc.vector.memset(Vp[:, :, 512:640], 1.0), writes=[bVp])
        sbs = [(0, 256)] + [(256 + i * TM, min(TM, SEQ - i * TM)) for i in range((SEQ + TM - 1) // TM)]
        KTp = p.ps[0][:, :].bitcast(BF16)
        bKTp = p.bps[0]
        for d in range(2):
            k.op("dve", lambda: nc.vector.memset(Cst[:, :, :], 0.0), writes=[bCst])
            k.op("dve", lambda: nc.vector.memset(mcar[:, :], 0.0), writes=[bmcar])
            order = sbs if d == 0 else [sbs[0]] + sbs[:0:-1]
            for (s0, T) in order:
                nch = T // 64
                k.dma("sp", ipre[:, 0:T], mkap(ml_g, (2 * d) * S + s0, [[S, 1], [1, T]]), bipre, b_mlg_)
                k.dma("sp", fpre[:, 0:T], mkap(ml_g, (2 * d + 1) * S + s0, [[S, 1], [1, T]]), bfpre, b_mlg_)
                k.dma("sp", kT[:, :, 0:T], mkap(ml_k, s0, [[S, 128], [128 * S, 2], [1, T]]), bkT, b_mlk)
                k.dma("sp", qT[:, :, 0:T], mkap(ml_q, s0, [[S, 128], [128 * S, 2], [1, T]]), bqT, b_mlq)
                k.dma("sp", Vp[:, 0:nch, 0:512], mkap(ml_v, s0 * 512, [[512, 64], [64 * 512, nch], [1, 512]]), bVp, b_mlv)
                if d == 1:
                    k.dma("sp", Ofw[:, :, 0:T], mkap(ml_h, s0, [[S, 128], [128 * S, 4], [1, T]]), bOfw, b_mlh)
                    k.dma("sp", OG[:, :, 0:T], mkap(ml_o, s0, [[S, 128], [128 * S, 4], [1, T]]), bOG, b_mlo)
                k.op("act", lambda: nc.scalar.activation(out=fpre[:, 0:T], in_=fpre[:, 0:T], func=AF.Sigmoid, bias=bg[0:1, 2 * d + 1:2 * d + 2], scale=1.0),
                     reads=[bfpre, bbg], writes=[bfpre])
                k.op("act", lambda: nc.scalar.activation(out=fpre[:, 0:T], in_=fpre[:, 0:T], func=AF.Ln), reads=[bfpre], writes=[bfpre])
                k.op("dve", lambda: nc.vector.tensor_tensor_scan(out=brow[:, 0:T], data0=rst64[:, 0:T], data1=fpre[:, 0:T], initial=0.0, op0=ALU.mult, op1=ALU.add),
                     reads=[brst, bfpre], writes=[bbrow])
                b3 = brow[:, 0:T].rearrange("p (c s) -> p c s", s=64)
                if d == 1:
                    k.op("dve", lambda: nc.vector.tensor_tensor(out=urow[:, 0:T], in0=fpre[:, 0:T], in1=brow[:, 0:T], op=ALU.subtract), reads=[bfpre, bbrow], writes=[burow])
                    u3 = urow[:, 0:T].rearrange("p (c s) -> p c s", s=64)
                    k.op("dve", lambda: nc.vector.tensor_tensor(out=erow[:, 0:T].rearrange("p (c s) -> p c s", s=64), in0=u3, in1=b3[:, :, 63:64].broadcast_to([1, nch, 64]), op=ALU.add),
                         reads=[burow, bbrow], writes=[berow])
                    k.op("dve", lambda: nc.vector.tensor_copy(out=brow[:, 0:T], in_=erow[:, 0:T]), reads=[berow], writes=[bbrow])
                    Bview = b3[:, :, 0]
                else:
                    Bview = b3[:, :, 63]
                k.op("dve", lambda: nc.vector.scalar_tensor_tensor(out=urow[:, 0:T], in0=ipre[:, 0:T], scalar=bg[0:1, 2 * d:2 * d + 1], in1=brow[:, 0:T], op0=ALU.add, op1=ALU.subtract),
                     reads=[bipre, bbg, bbrow], writes=[burow])
                u3 = urow[:, 0:T].rearrange("p (c s) -> p c s", s=64)
                k.op("dve", lambda: nc.vector.tensor_reduce(out=umax[:, 0:nch], in_=u3, axis=AX.X, op=ALU.max), reads=[burow], writes=[bumax])
                if d == 0:
                    k.op("dve", lambda: nc.vector.tensor_copy(out=marr[:, 0:1], in_=mcar[:, :]), reads=[bmcar], writes=[bmarr])
                    crange = list(range(nch))
                else:
                    k.op("dve", lambda: nc.vector.tensor_copy(out=marr[:, nch:nch + 1], in_=mcar[:, :]), reads=[bmcar], writes=[bmarr])
                    crange = list(range(nch - 1, -1, -1))
                for c in crange:
                    mi, mo = (c, c + 1) if d == 0 else (c + 1, c)
                    k.op("dve", lambda: nc.vector.tensor_tensor(out=Mst[:, c:c + 1], in0=marr[:, mi:mi + 1], in1=umax[:, c:c + 1], op=ALU.max), reads=[bmarr, bumax], writes=[bMst])
                    k.op("dve", lambda: nc.vector.tensor_tensor(out=marr[:, mo:mo + 1], in0=Mst[:, c:c + 1], in1=Bview[:, c:c + 1], op=ALU.add), reads=[bMst, bbrow], writes=[bmarr])
                mlast = nch if d == 0 else 0
                k.op("dve", lambda: nc.vector.tensor_copy(out=mcar[:, :], in_=marr[:, mlast:mlast + 1]), reads=[bmarr], writes=[bmcar])
                mb0 = 0 if d == 0 else 1
                k.op("dve", lambda: nc.vector.tensor_tensor(out=wprow[:, 0:nch], in0=marr[:, mb0:mb0 + nch], in1=Mst[:, 0:nch], op=ALU.subtract), reads=[bmarr, bMst], writes=[bwprow])
                k.op("act", lambda: nc.scalar.activation(out=wprow[:, 0:nch], in_=wprow[:, 0:nch], func=AF.Exp), reads=[bwprow], writes=[bwprow])
                Mb = Mst[:, 0:nch].unsqueeze(2).broadcast_to([1, nch, 64])
                k.op("dve", lambda: nc.vector.tensor_tensor(out=wsrow[:, 0:T].rearrange("p (c s) -> p c s", s=64), in0=u3, in1=Mb, op=ALU.subtract), reads=[burow, bMst], writes=[bwsrow])
                k.op("act", lambda: nc.scalar.activation(out=wsrow[:, 0:T], in_=wsrow[:, 0:T], func=AF.Exp), reads=[bwsrow], writes=[bwsrow])
                k.op("dve", lambda: nc.vector.tensor_tensor(out=erow[:, 0:T].rearrange("p (c s) -> p c s", s=64), in0=b3, in1=Mb, op=ALU.add), reads=[bbrow, bMst], writes=[berow])
                k.op("act", lambda: nc.scalar.activation(out=erow[:, 0:T], in_=erow[:, 0:T], func=AF.Exp, scale=-1.0), reads=[berow], writes=[berow])
                BP, bBP = p.ps[6], p.bps[6]
                for (row, brw, dst, bdst) in ((wsrow, bwsrow, WSbc, bWSbc), (erow, berow, Ebc, bEbc)):
                    for c0 in range(0, T, 512):
                        cn = min(512, T - c0)
                        k.op("pe", lambda: nc.tensor.matmul(BP[:, 0:cn], lhsT=onesf[0:1, :], rhs=row[0:1, c0:c0 + cn], start=True, stop=True), reads=[bonesf, brw], writes=[bBP])
                        p.evac_copy(dst[:, c0:c0 + cn], BP[:, 0:cn], [bBP], [bdst])
                k.op("pe", lambda: nc.tensor.matmul(BP[:, 0:nch], lhsT=onesf[0:1, :], rhs=wprow[0:1, 0:nch], start=True, stop=True), reads=[bonesf, bwprow], writes=[bBP])
                p.evac_copy(Wp[:, 0:nch], BP[:, 0:nch], [bBP], [bWp])
                k.op("dve", lambda: nc.vector.tensor_tensor(out=KH[:, :, 0:T], in0=kT[:, :, 0:T], in1=WSbc[:, 0:T].unsqueeze(1).broadcast_to([128, 2, T]), op=ALU.mult),
                     reads=[bkT, bWSbc], writes=[bKH])
                moff = 64 if d == 0 else 128
                for c in crange:
                    cs = slice(c * 64, (c + 1) * 64)
                    for dc in range(2):
                        k.op("pe", lambda dc=dc: nc.tensor.transpose(out=KTp[0:64, dc * 128:(dc + 1) * 128], in_=KH[:, dc, cs], identity=p.ident[:, :]),
                             reads=[bKH, p.b_ident], writes=[bKTp], inc=(dc == 1))
                    k.op("act", lambda: nc.scalar.copy(out=Kt[:, :], in_=KTp[0:64, 0:256]), reads=[bKTp], writes=[bKt])
                    ST, bST = p.ps[1], p.bps[1]
                    for dc in range(2):
                        k.op("pe", lambda dc=dc: nc.tensor.matmul(ST[0:64, 0:64], lhsT=KH[:, dc, cs], rhs=qT[:, dc, cs], start=(dc == 0), stop=(dc == 1)),
                             reads=[bKH, bqT], writes=[bST], inc=(dc == 1))
                    k.op("dve", lambda: nc.vector.tensor_tensor(out=A[:, :], in0=ST[0:64, 0:64], in1=p.tri[0:64, moff:moff + 64], op=ALU.mult), reads=[bST, p.b_tri], writes=[bA])
                    k.op("act", lambda: nc.scalar.activation(out=Cbf[:, :, :], in_=Cst[:, :, :], func=AF.Copy, scale=Wp[:, c:c + 1]), reads=[bCst, bWp], writes=[bCbf])
                    NP, bNP = p.ps[2], p.bps[2]
                    for vj in range(5):
                        k.op("pe", lambda vj=vj: nc.tensor.matmul(NP[:, vj * 64:(vj + 1) * 64], lhsT=Vp[:, c, vj * 128:(vj + 1) * 128], rhs=A[:, :], start=True, stop=False),
                             reads=[bVp, bA], writes=[bNP], inc=False)
                        k.op("pe", lambda vj=vj: nc.tensor.matmul(NP[:, vj * 64:(vj + 1) * 64], lhsT=Cbf[:, 0, vj * 128:(vj + 1) * 128], rhs=qT[:, 0, cs], start=False, stop=False),
                             reads=[bCbf, bqT], writes=[bNP], inc=False)
                        k.op("pe", lambda vj=vj: nc.tensor.matmul(NP[:, vj * 64:(vj + 1) * 64], lhsT=Cbf[:, 1, vj * 128:(vj + 1) * 128], rhs=qT[:, 1, cs], start=False, stop=True),
                             reads=[bCbf, bqT], writes=[bNP], inc=(vj == 4))
                    k.op("act", lambda: nc.scalar.activation(out=DN[:, :], in_=NP[:, 256:320], func=AF.Abs), reads=[bNP], writes=[bDN])
                    k.op("dve", lambda: nc.vector.tensor_tensor(out=DN[:, :], in0=DN[:, :], in1=Ebc[:, cs], op=ALU.max), reads=[bDN, bEbc], writes=[bDN])
                    k.op("dve", lambda: nc.vector.reciprocal(out=DN[:, :], in_=DN[:, :]), reads=[bDN], writes=[bDN])
                    np3 = NP[:, 0:256].rearrange("p (v t) -> p v t", t=64)
                    dnb = DN[:, :].unsqueeze(1).broadcast_to([128, 4, 64])
                    if d == 0:
                        k.op("dve", lambda: nc.vector.tensor_tensor(out=O[:, :, cs], in0=np3, in1=dnb, op=ALU.mult), reads=[bNP, bDN], writes=[bO])
                    else:
                        k.op("dve", lambda: nc.vector.tensor_tensor(out=TH[:, :, :], in0=np3, in1=dnb, op=ALU.mult), reads=[bNP, bDN], writes=[bTH])
                        k.op("pool", lambda: nc.gpsimd.tensor_tensor(out=O[:, :, cs], in0=TH[:, :, :], in1=Ofw[:, :, cs], op=ALU.add), reads=[bTH, bOfw], writes=[bO])
                    UV0, bUV0 = p.ps[3], p.bps[3]
                    UV1, bUV1 = p.ps[4], p.bps[4]
                    UN, bUN = p.ps[5], p.bps[5]
                    for dc, (UV, bUV) in enumerate(((UV0, bUV0), (UV1, bUV1))):
                        k.op("pe", lambda dc=dc, UV=UV: nc.tensor.matmul(UV[:, 0:512], lhsT=Kt[:, dc * 128:(dc + 1) * 128], rhs=Vp[:, c, 0:512], start=True, stop=True),
                             reads=[bKt, bVp], writes=[bUV])
                        k.op("pe", lambda dc=dc: nc.tensor.matmul(UN[:, dc * 128:(dc + 1) * 128], lhsT=Kt[:, dc * 128:(dc + 1) * 128], rhs=Vp[:, c, 512:640], start=True, stop=True),
                             reads=[bKt, bVp], writes=[bUN], inc=(dc == 1))
                    for dc, (UV, bUV) in enumerate(((UV0, bUV0), (UV1, bUV1))):
                        k.op("dve", lambda dc=dc, UV=UV: nc.vector.scalar_tensor_tensor(out=Cst[:, dc, 0:512], in0=Cst[:, dc, 0:512], scalar=Wp[:, c:c + 1], in1=UV[:, 0:512],
                                                                                       op0=ALU.mult, op1=ALU.add),
                             reads=[bCst, bWp, bUV], writes=[bCst])
                    k.op("dve", lambda: nc.vector.scalar_tensor_tensor(out=Cst[:, :, 512:640], in0=Cst[:, :, 512:640], scalar=Wp[:, c:c + 1],
                                                                       in1=UN[:, 0:256].rearrange("p (a b) -> p a b", b=128), op0=ALU.mult, op1=ALU.add),
                         reads=[bCst, bWp, bUN], writes=[bCst])
                if d == 0:
                    k.dma("pool", mkap(ml_h, s0, [[S, 128], [128 * S, 4], [1, T]]), O[:, :, 0:T], b_mlh, bO)
                else:
                    p.head_post(O, bO, 4, T, SQ, bSQ, R, bR, OG, bOG, lambda vj: mlg[:, vj:vj + 1], bmlg, OUTb, bOUTb, o_m, b_om, 0, s0)
        k.barrier()
    return p.finish()


def rope_tables(ROWS):
    n_freq = 64
    inv = (10000.0 ** (-np.arange(n_freq, dtype=np.float32) / n_freq)).astype(np.float32)
    sign = np.concatenate([-np.ones(64, np.float32), np.ones(64, np.float32)])
    f2 = np.concatenate([inv, inv])
    rows = np.arange(ROWS, dtype=np.float32)
    cols = np.arange(64, dtype=np.float32)
    angR = (rows[None, :] * f2[:, None]).astype(np.float32)
    angC = (cols[None, :] * f2[:, None]).astype(np.float32)
    ropeR = np.concatenate([np.cos(angR), sign[:, None] * np.sin(angR)], axis=1).astype(np.float32)
    ropeC = np.concatenate([np.cos(angC), sign[:, None] * np.sin(angC)], axis=1).astype(np.float32)
    return ropeR, ropeC


def prep_odd_win(w, h):
    perm = np.concatenate([np.arange(64, 128), np.arange(0, 64), np.arange(192, 256), np.arange(128, 192)])
    q0, k0, v0, o0, g0 = h * 256, 2048 + h * 256, 4096 + h * 512, 8192 + h * 512, 12288
    cols = np.concatenate([q0 + np.arange(256), q0 + perm, k0 + np.arange(256), k0 + perm, o0 + np.arange(512), v0 + np.arange(512)])
    wg = np.ascontiguousarray(w[:, [g0 + h, g0 + 8 + h, g0 + 16 + h, g0 + 24 + h]])
    return np.ascontiguousarray(w[:, cols]), wg


ROWS_FULL = 256
SEQ_FULL = ROWS_FULL * 64
CTX_LEN = 256
S_FULL = SEQ_FULL + CTX_LEN
NCT = 4
FF_FULL = 11008
DEPTH = 4
_PROGS = {}


def _prog(name, fn):
    if name not in _PROGS:
        _PROGS[name] = fn()
    return _PROGS[name]


def _run(nc, in_maps):
    res = run_bass_kernel_spmd(nc, in_maps, core_ids=list(range(len(in_maps))))
    return res.results


def kernel(x, c, ctx, c_ctx, norm_mix_g, norm_ffn_g, mod_w_a, mod_w_b, mod_b,
           even_w_in, even_w_out, na_rpb, hg_lb_logits, hg_norm_g,
           odd_w_in, odd_b_gates, odd_w_out, ml_norm_g,
           ffn_w_up, ffn_w_down, final_norm_g):
    f = lambda a: np.asarray(a, dtype=np.float32)
    x, c, ctx, c_ctx = f(x), f(c), f(ctx), f(c_ctx)
    norm_mix_g, norm_ffn_g, mod_w_a, mod_w_b, mod_b = f(norm_mix_g), f(norm_ffn_g), f(mod_w_a), f(mod_w_b), f(mod_b)
    even_w_in, even_w_out, na_rpb, hg_lb_logits, hg_norm_g = f(even_w_in), f(even_w_out), f(na_rpb), f(hg_lb_logits), f(hg_norm_g)
    odd_w_in, odd_b_gates, odd_w_out, ml_norm_g = f(odd_w_in), f(odd_b_gates), f(odd_w_out), f(ml_norm_g)
    ffn_w_up, ffn_w_down, final_norm_g = f(ffn_w_up), f(ffn_w_down), f(final_norm_g)
    SEQ, S, ROWS = SEQ_FULL, S_FULL, ROWS_FULL
    TL, TC = SEQ // NCT, CTX_LEN // NCT
    tri, rst, rst64, ident = const_tables()
    ropeR, ropeC = rope_tables(ROWS)

    cvec = np.concatenate([col_layout(c[0]), col_layout(c_ctx)], axis=1)
    ncM = _prog("M", build_M)
    rM = _run(ncM, [{"wa": mod_w_a[l], "wb": mod_w_b[l], "modb": col_layout(mod_b[l]), "cvec": cvec} for l in range(DEPTH)])
    modT = [rM[l]["modT"] for l in range(DEPTH)]

    xT = np.ascontiguousarray(x[0].T)
    cT = np.ascontiguousarray(ctx[0].T)
    xloc = [np.ascontiguousarray(np.concatenate([xT[:, t * TL:(t + 1) * TL], cT[:, t * TC:(t + 1) * TC]], axis=1)) for t in range(NCT)]
    del xT, cT

    def gather_hx(hs):
        out = np.empty((D, S), NPBF)
        for t in range(NCT):
            out[:, t * TC:(t + 1) * TC] = hs[t][:, TL:TL + TC]
            out[:, CTX_LEN + t * TL:CTX_LEN + (t + 1) * TL] = hs[t][:, 0:TL]
        return out

    ncA = _prog("A", lambda: build_A(TL, TC, False))
    rA = _run(ncA, [{"xT": xloc[t], "modT": modT[0], "ng": col_layout(norm_mix_g[0])} for t in range(NCT)])
    hx_all = gather_hx([rA[t]["hx"] for t in range(NCT)])
    del rA

    for l in range(DEPTH):
        j = l // 2
        if l % 2 == 0:
            ncP = _prog("PE", lambda: build_PE(ROWS))
            ims = []
            for cc in range(8):
                lbl, jsel, hg = even_small_params(hg_lb_logits, j, hg_norm_g, cc)
                ims.append({"hx": hx_all, "win": prep_even_win(even_w_in[j], cc), "lbl": lbl, "jsel": jsel, "hgg": hg,
                            "nab": na_bias_tables(na_rpb[j], cc, ROWS), "rst": rst, "ident": ident, "tri": tri})
            rP = _run(ncP, ims)
            del ims
            M_all = np.empty((D, S), NPBF)
            for cc in range(8):
                M_all[256 * cc:256 * cc + 256] = rP[cc]["mloc"][0:256]
                M_all[2048 + 256 * cc:2048 + 256 * cc + 256] = rP[cc]["mloc"][256:512]
            wout = even_w_out[j]
        else:
            ncP = _prog("PO", lambda: build_PO(ROWS))
            ims = []
            for h in range(8):
                win, wg = prep_odd_win(odd_w_in[j], h)
                ims.append({"hx": hx_all, "win": win, "wg": wg, "bg": np.ascontiguousarray(odd_b_gates[j][[h, 8 + h, 16 + h, 24 + h]][None]),
                            "mlg": col_layout(ml_norm_g[j][h * 512:(h + 1) * 512]), "ropeR": ropeR, "ropeC": ropeC, "rst64": rst64, "ident": ident, "tri": tri})
            rP = _run(ncP, ims)
            del ims
            M_all = np.empty((D, S), NPBF)
            for h in range(8):
                M_all[512 * h:512 * h + 512] = rP[h]["mloc"]
            wout = odd_w_out[j]
        del rP
        last = (l == DEPTH - 1)
        modn = np.zeros_like(modT[0]) if last else modT[l + 1]
        ngn = col_layout(final_norm_g) if last else col_layout(norm_mix_g[l + 1])
        ncT = _prog("T", lambda: build_T(TL, TC, FF_FULL))
        ims = []
        for t in range(NCT):
            mloc = np.ascontiguousarray(np.concatenate([M_all[:, CTX_LEN + t * TL:CTX_LEN + (t + 1) * TL], M_all[:, t * TC:(t + 1) * TC]], axis=1))
            ims.append({"xT": xloc[t], "mT": mloc, "wout": wout, "wup": ffn_w_up[l], "wdn": ffn_w_down[l], "modT": modT[l],
                        "nfg": col_layout(norm_ffn_g[l]), "modTn": modn, "ngn": ngn})
        del M_all
        rT = _run(ncT, ims)
        del ims
        xloc = [rT[t]["xo"] for t in range(NCT)]
        if not last:
            hx_all = gather_hx([rT[t]["hxo"] for t in range(NCT)])
        del rT

    ncF = _prog("F", lambda: build_A(TL, TC, True))
    zero_mod = np.zeros_like(modT[0])
    rF = _run(ncF, [{"xT": xloc[t], "modT": zero_mod, "ng": col_layout(final_norm_g)} for t in range(NCT)])
    out = np.empty((1, SEQ, D), np.float32)
    for t in range(NCT):
        out[0, t * TL:(t + 1) * TL, :] = rF[t]["hx"][:, 0:TL].T
    return out
```

```python
import numpy as np
import ml_dtypes
from contextlib import ExitStack
import concourse.bass as bass
import concourse.mybir as mybir
from concourse.bass_utils import run_bass_kernel_spmd

F32 = mybir.dt.float32
BF16 = mybir.dt.bfloat16
AF = mybir.ActivationFunctionType
ALU = mybir.AluOpType
AX = mybir.AxisListType
NPBF = ml_dtypes.bfloat16

D = 4096
KC = 32
EPS = 1e-6


class Buf:
    __slots__ = ("name", "w", "r", "dkey")

    def __init__(self, name):
        self.name = name
        self.w = {}
        self.r = {}
        self.dkey = None


class KB:
    def __init__(self, nc):
        self.nc = nc
        self.eng = {"pe": nc.tensor, "act": nc.scalar, "dve": nc.vector, "pool": nc.gpsimd, "sp": nc.sync}
        self.semh = {}
        self.total = {}
        for e in ("pe", "act", "dve", "pool"):
            self.semh[e] = nc.alloc_semaphore(name="sem_" + e)
            self.total[e] = 0
        self.waited = {e: {} for e in self.eng}
        self.pending = {e: ([], []) for e in self.eng}
        self.ninstr = 0

    def buf(self, name):
        return Buf(name)

    def _wait(self, e, key, val):
        if key == "pe" and e == "pe":
            return
        if self.waited[e].get(key, 0) >= val:
            return
        self.eng[e].wait_ge(self.semh[key], val)
        self.waited[e][key] = val
        self.ninstr += 1

    def _deps(self, e, reads, writes):
        for b in reads:
            for k, v in b.w.items():
                self._wait(e, k, v)
        for b in writes:
            for k, v in b.w.items():
                self._wait(e, k, v)
            for k, v in b.r.items():
                self._wait(e, k, v)

    def op(self, e, fn, reads=(), writes=(), inc=True):
        self._deps(e, reads, writes)
        ins = fn()
        self.ninstr += 1
        pr, pw = self.pending[e]
        if not inc:
            pr.extend(reads)
            pw.extend(writes)
            return ins
        self.total[e] += 1
        ins.then_inc(self.semh[e], 1)
        v = self.total[e]
        for b in list(reads) + pr:
            b.r[e] = v
        for b in list(writes) + pw:
            b.w[e] = v
            b.r = {}
        pr.clear()
        pw.clear()
        return ins

    def dma(self, q, out, in_, ob, ib, **kw):
        if ob.dkey is None:
            ob.dkey = "d_" + ob.name
            self.semh[ob.dkey] = self.nc.alloc_semaphore(name=ob.dkey)
            self.total[ob.dkey] = 0
        self._deps(q, [ib], [ob])
        self.eng[q].dma_start(out=out, in_=in_, **kw).then_inc(self.semh[ob.dkey], 16)
        self.ninstr += 1
        self.total[ob.dkey] += 16
        v = self.total[ob.dkey]
        ib.r[ob.dkey] = v
        ob.w[ob.dkey] = v
        ob.r = {}

    def barrier(self):
        for e in self.eng:
            for k, v in self.total.items():
                if v > 0:
                    self._wait(e, k, v)


def mkap(h, off, pat):
    return bass.AP(h, off, [list(p) for p in pat])


class Base:
    def __init__(self):
        nc = bass.Bass("TRN2", target_bir_lowering=False)
        self.nc = nc
        self.k = KB(nc)
        self.b_in = self.k.buf("ext_in")
        self.es = ExitStack()
        self.ps = []
        self.bps = []
        for i in range(8):
            t = self.es.enter_context(nc.psum_tensor(f"ps{i}", [128, 512], F32))
            self.ps.append(t)
            self.bps.append(self.k.buf(f"ps{i}"))
        self.nbuf = 0

    def din(self, name, shape, dt=F32):
        return self.nc.dram_tensor(name, list(shape), dt, kind="ExternalInput")

    def dout(self, name, shape, dt=F32):
        return self.nc.dram_tensor(name, list(shape), dt, kind="ExternalOutput")

    def dint(self, name, shape, dt):
        return self.nc.dram_tensor(name, list(shape), dt), self.k.buf(name)

    def sb(self, es, name, shape, dt):
        self.nbuf += 1
        name = f"s{self.nbuf}_{name}"
        t = es.enter_context(self.nc.sbuf_tensor(name, list(shape), dt))
        return t, self.k.buf(name)

    def consts(self):
        nc, k = self.nc, self.k
        self.ones, self.b_ones = self.sb(self.es, "ones", [128, 128], BF16)
        k.op("dve", lambda: nc.vector.memset(self.ones[:, :], 1.0), writes=[self.b_ones])

    def finish(self):
        self.k.barrier()
        self.es.close()
        return self.nc

    def cast_tiles(self, src_h, K, N, dst_h, b_dst, es):
        nc, k = self.nc, self.k
        nk = K // 128
        CW = 512 if N % 512 == 0 else 128
        G = 8
        ncc = CW // 128
        if not hasattr(self, "_cst"):
            self._cst = [self.sb(es, f"cst{i}", [128, G, 512], F32) for i in range(3)]
            self._csb = [self.sb(es, f"csb{i}", [128, 4, G, 128], BF16) for i in range(3)]
            self._cit = 0
        for kc0 in range(0, nk, G):
            gn = min(G, nk - kc0)
            for c0 in range(0, N, CW):
                it = self._cit
                self._cit += 1
                a, ba = self._cst[it % 3]
                b, bb = self._csb[it % 3]
                k.dma("sp", a[:, 0:gn, 0:CW], mkap(src_h, kc0 * 128 * N + c0, [[N, 128], [128 * N, gn], [1, CW]]), ba, self.b_in)
                src = a[:, 0:gn, 0:CW].rearrange("p g (a c) -> p g a c", c=128)
                dstv = b[:, 0:ncc, 0:gn, :].rearrange("p a g c -> p g a c")
                e = ("act", "dve", "pool")[it % 3]
                if e == "act":
                    k.op("act", lambda: nc.scalar.copy(out=dstv, in_=src), reads=[ba], writes=[bb])
                elif e == "dve":
                    k.op("dve", lambda: nc.vector.tensor_copy(out=dstv, in_=src), reads=[ba], writes=[bb])
                else:
                    k.op("pool", lambda: nc.gpsimd.tensor_copy(out=dstv, in_=src), reads=[ba], writes=[bb])
                cc0 = c0 // 128
                dst = mkap(dst_h, cc0 * 128 * nk * 128 + kc0 * 128, [[nk * 128, 128], [128 * nk * 128, ncc], [1, gn * 128]])
                k.dma("pool", dst, b[:, 0:ncc, 0:gn, :].rearrange("p a g c -> p a (g c)"), b_dst, bb)

    def wtile_view(self, dst_h, nk, cc, k0=0, kn=None):
        kn = nk if kn is None else kn
        return mkap(dst_h, cc * 128 * nk * 128 + k0 * 128, [[nk * 128, 128], [128, kn], [1, 128]])

    def normmod(self, X, bX, SQ, bSQ, OUT, bOUT, R, bR, tmpk, n, gs_ap, sh_ap, bmods):
        nc, k = self.nc, self.k
        P, bP = self.ps[7], self.bps[7]
        k.op("act", lambda: nc.scalar.activation(out=SQ[:, :, 0:n], in_=X[:, :, 0:n], func=AF.Square), reads=[bX], writes=[bSQ])
        for kc in range(KC):
            k.op("pe", lambda kc=kc: nc.tensor.matmul(P[:, 0:n], lhsT=self.ones[:, :], rhs=SQ[:, kc, 0:n], start=(kc == 0), stop=(kc == KC - 1)),
                 reads=[bSQ, self.b_ones], writes=[bP], inc=(kc == KC - 1))
        k.op("dve", lambda: nc.vector.tensor_scalar(out=R[:, 0:n], in0=P[:, 0:n], scalar1=1.0 / D, scalar2=EPS, op0=ALU.mult, op1=ALU.add),
             reads=[bP], writes=[bR])
        k.op("act", lambda: nc.scalar.activation(out=R[:, 0:n], in_=R[:, 0:n], func=AF.Sqrt), reads=[bR], writes=[bR])
        k.op("dve", lambda: nc.vector.reciprocal(out=R[:, 0:n], in_=R[:, 0:n]), reads=[bR], writes=[bR])
        for kc in range(KC):
            tk, btk = tmpk[kc % len(tmpk)]
            k.op("dve", lambda kc=kc, tk=tk: nc.vector.tensor_tensor(out=tk[:, 0:n], in0=X[:, kc, 0:n], in1=R[:, 0:n], op=ALU.mult),
                 reads=[bX, bR], writes=[btk])
            k.op("act", lambda kc=kc, tk=tk: nc.scalar.activation(out=OUT[:, kc, 0:n], in_=tk[:, 0:n], func=AF.Identity,
                                                                  scale=gs_ap(kc), bias=sh_ap(kc)),
                 reads=[btk] + bmods, writes=[bOUT])

    def make_gs(self, gs, bgs, modT, bmod, sc0, ng, bng, tmp, btmp):
        nc, k = self.nc, self.k
        k.op("dve", lambda: nc.vector.tensor_scalar(out=tmp[:, :, :], in0=modT[:, sc0:sc0 + 32, :], scalar1=1.0, scalar2=None, op0=ALU.add),
             reads=[bmod], writes=[btmp])
        k.op("dve", lambda: nc.vector.tensor_tensor(out=gs[:, :, :], in0=tmp[:, :, :], in1=ng[:, 0:32].unsqueeze(2).broadcast_to([128, 32, 2]), op=ALU.mult),
             reads=[btmp, bng], writes=[bgs])


def tblocks(TL, TC, NB=256):
    blocks = []
    for t0 in range(0, TL, NB):
        blocks.append((t0, min(NB, TL - t0), 0))
    blocks.append((TL, TC, 1))
    return blocks


def fm_view(h, ncols_total, t0, n):
    return mkap(h, t0, [[ncols_total, 128], [128 * ncols_total, KC], [1, n]])


def build_M():
    p = Base()
    nc, k = p.nc, p.k
    i_wa = p.din("wa", [D, 256])
    i_wb = p.din("wb", [256, 6 * D])
    i_modb = p.din("modb", [128, 192])
    i_cvec = p.din("cvec", [128, 64])
    o_mod = p.dout("modT", [128, 384])
    b_out = k.buf("o_mod")
    es = p.es
    wa, b_wa = p.sb(es, "wa", [128, 32, 256], F32)
    wb, b_wb = p.sb(es, "wb", [128, 2, 6144], F32)
    cv, b_cv = p.sb(es, "cv", [128, 64], F32)
    cT, b_cT = p.sb(es, "cT", [128, 32, 2], F32)
    hT, b_hT = p.sb(es, "hT", [128, 2, 2], F32)
    mb, b_mb = p.sb(es, "mb", [128, 192], F32)
    mo, b_mo = p.sb(es, "mo", [128, 192, 2], F32)
    k.dma("sp", cv[:, :], i_cvec[:, :], b_cv, p.b_in)
    k.dma("sp", mb[:, :], i_modb[:, :], b_mb, p.b_in)
    k.dma("sp", wa[:, :, :], mkap(i_wa, 0, [[256, 128], [128 * 256, 32], [1, 256]]), b_wa, p.b_in)
    for c in range(2):
        k.op("act", lambda c=c: nc.scalar.activation(out=cT[:, :, c], in_=cv[:, c * 32:(c + 1) * 32], func=AF.Silu), reads=[b_cv], writes=[b_cT])
    P0, bP0 = p.ps[0], p.bps[0]
    P1, bP1 = p.ps[1], p.bps[1]
    for r in range(2):
        for kc in range(32):
            k.op("pe", lambda r=r, kc=kc: nc.tensor.matmul(P0[:, r * 2:(r + 1) * 2], lhsT=wa[:, kc, r * 128:(r + 1) * 128], rhs=cT[:, kc, :],
                                                           start=(kc == 0), stop=(kc == 31)),
                 reads=[b_wa, b_cT], writes=[bP0], inc=(kc == 31))
    k.op("dve", lambda: nc.vector.tensor_copy(out=hT[:, :, :], in_=P0[:, 0:4].rearrange("p (r c) -> p r c", c=2)), reads=[bP0], writes=[b_hT])
    for piece in range(4):
        k.dma("sp", wb[:, :, :], mkap(i_wb, piece * 6144, [[6 * D, 128], [128 * 6 * D, 2], [1, 6144]]), b_wb, p.b_in)
        for jj in range(48):
            jg = piece * 48 + jj
            for r in range(2):
                k.op("pe", lambda r=r, jj=jj, jg=jg: nc.tensor.matmul(P1[:, jg * 2:(jg + 1) * 2], lhsT=wb[:, r, jj * 128:(jj + 1) * 128], rhs=hT[:, r, :],
                                                                      start=(r == 0), stop=(r == 1)),
                     reads=[b_wb, b_hT], writes=[bP1], inc=(r == 1 and jj == 47))
    k.op("dve", lambda: nc.vector.tensor_tensor(out=mo[:, :, :], in0=P1[:, 0:384].rearrange("p (j c) -> p j c", c=2),
                                                in1=mb[:, :].unsqueeze(2).broadcast_to([128, 192, 2]), op=ALU.add),
         reads=[bP1, b_mb], writes=[b_mo])
    k.dma("pool", o_mod[:, :], mo[:, :, :].rearrange("p j c -> p (j c)"), b_out, b_mo)
    return p.finish()


def build_A(TL, TC, out_f32):
    p = Base()
    nc, k = p.nc, p.k
    TLOC = TL + TC
    i_x = p.din("xT", [D, TLOC])
    i_mod = p.din("modT", [128, 384])
    i_ng = p.din("ng", [128, 32])
    odt = F32 if out_f32 else BF16
    o_hx = p.dout("hx", [D, TLOC], odt)
    b_out = k.buf("o_hx")
    es = p.es
    p.consts()
    modT, b_mod = p.sb(es, "modT", [128, 192, 2], F32)
    ng, b_ng = p.sb(es, "ng", [128, 32], F32)
    gs, b_gs = p.sb(es, "gs", [128, 32, 2], F32)
    tmp, b_tmp = p.sb(es, "tmp", [128, 32, 2], F32)
    k.dma("sp", modT[:, :, :].rearrange("p j c -> p (j c)"), i_mod[:, :], b_mod, p.b_in)
    k.dma("sp", ng[:, :], i_ng[:, :], b_ng, p.b_in)
    p.make_gs(gs, b_gs, modT, b_mod, 32, ng, b_ng, tmp, b_tmp)
    Xs = [p.sb(es, f"X{i}", [128, 32, 256], F32) for i in range(2)]
    SQ, bSQ = p.sb(es, "SQ", [128, 32, 256], BF16)
    OUTs = [p.sb(es, f"O{i}", [128, 32, 256], odt) for i in range(2)]
    R, bR = p.sb(es, "R", [128, 256], F32)
    tmpk = [p.sb(es, f"tk{i}", [128, 256], F32) for i in range(3)]
    for bi, (t0, n, cond) in enumerate(tblocks(TL, TC)):
        X, bX = Xs[bi % 2]
        O, bO = OUTs[bi % 2]
        k.dma("sp", X[:, :, 0:n], fm_view(i_x, TLOC, t0, n), bX, p.b_in)
        p.normmod(X, bX, SQ, bSQ, O, bO, R, bR, tmpk, n,
                  lambda kc, cond=cond: gs[:, kc, cond:cond + 1], lambda kc, cond=cond: modT[:, kc, cond:cond + 1], [b_gs, b_mod])
        k.dma("pool", fm_view(o_hx, TLOC, t0, n), O[:, :, 0:n], b_out, bO)
    return p.finish()


def build_T(TL, TC, FF):
    p = Base()
    nc, k = p.nc, p.k
    TLOC = TL + TC
    HC = FF // 128
    i_x = p.din("xT", [D, TLOC])
    i_m = p.din("mT", [D, TLOC], BF16)
    i_wout = p.din("wout", [D, D])
    i_wup = p.din("wup", [D, 2 * FF])
    i_wdn = p.din("wdn", [FF, D])
    i_mod = p.din("modT", [128, 384])
    i_nfg = p.din("nfg", [128, 32])
    i_modn = p.din("modTn", [128, 384])
    i_ngn = p.din("ngn", [128, 32])
    o_x = p.dout("xo", [D, TLOC], F32)
    o_hx = p.dout("hxo", [D, TLOC], BF16)
    b_ox = k.buf("o_x")
    b_ohx = k.buf("o_hx")
    es = p.es
    p.consts()
    woutb, b_woutb = p.dint("woutb", [32 * 128 * 32 * 128], BF16)
    wupb, b_wupb = p.dint("wupb", [(2 * FF // 128) * 128 * 32 * 128], BF16)
    wdnb, b_wdnb = p.dint("wdnb", [32 * 128 * HC * 128], BF16)
    with ExitStack() as es1:
        p.cast_tiles(i_wout, D, D, woutb, b_woutb, es1)
        p.cast_tiles(i_wup, D, 2 * FF, wupb, b_wupb, es1)
        p.cast_tiles(i_wdn, FF, D, wdnb, b_wdnb, es1)
        k.barrier()
    del p._cst
    modT, b_mod = p.sb(es, "modT", [128, 192, 2], F32)
    modn, b_modn = p.sb(es, "modn", [128, 192, 2], F32)
    nfg, b_nfg = p.sb(es, "nfg", [128, 32], F32)
    ngn, b_ngn = p.sb(es, "ngn", [128, 32], F32)
    gs2, b_gs2 = p.sb(es, "gs2", [128, 32, 2], F32)
    gsn, b_gsn = p.sb(es, "gsn", [128, 32, 2], F32)
    tmp, b_tmp = p.sb(es, "tmp", [128, 32, 2], F32)
    k.dma("sp", modT[:, :, :].rearrange("p j c -> p (j c)"), i_mod[:, :], b_mod, p.b_in)
    k.dma("sp", modn[:, :, :].rearrange("p j c -> p (j c)"), i_modn[:, :], b_modn, p.b_in)
    k.dma("sp", nfg[:, :], i_nfg[:, :], b_nfg, p.b_in)
    k.dma("sp", ngn[:, :], i_ngn[:, :], b_ngn, p.b_in)
    p.make_gs(gs2, b_gs2, modT, b_mod, 128, nfg, b_nfg, tmp, b_tmp)
    p.make_gs(gsn, b_gsn, modn, b_modn, 32, ngn, b_ngn, tmp, b_tmp)
    NB = 416
    hh = (HC + 1) // 2
    X, bX = p.sb(es, "X", [128, 32, NB], F32)
    HX, bHX = p.sb(es, "HX", [128, 32, NB], BF16)
    Fh, bF = p.sb(es, "F", [128, max(hh, 32), NB], BF16)
    MT, bMT = Fh, bF
    R, bR = p.sb(es, "R", [128, NB], F32)
    tmpk = [p.sb(es, f"tk{i}", [128, NB], F32) for i in range(3)]
    W8 = [p.sb(es, f"W8_{i}", [128, 32, 128], BF16) for i in range(4)]
    WD = [p.sb(es, f"WD_{i}", [128, hh, 128], BF16) for i in range(2)]
    SG, bSG = p.sb(es, "SG", [128, NB], F32)
    cnt = {"w8": 0, "wd": 0}

    def load_w8(h, bh, cc):
        W, bW = W8[cnt["w8"] % len(W8)]
        cnt["w8"] += 1
        k.dma("sp", W[:, :, :], p.wtile_view(h, 32, cc), bW, bh)
        return W, bW

    blocks = []
    t0 = 0
    while t0 < TLOC:
        n = min(NB, TLOC - t0)
        segs = []
        if t0 < TL:
            segs.append((0, min(n, TL - t0), 0))
        if t0 + n > TL:
            c0 = max(TL - t0, 0)
            segs.append((c0, n - c0, 1))
        blocks.append((t0, n, segs))
        t0 += n

    def resid(P, bP, fo, n, segs, gbase):
        for (c0, cn, cond) in segs:
            k.op("dve", lambda: nc.vector.scalar_tensor_tensor(out=X[:, fo, c0:c0 + cn], in0=P[:, c0:c0 + cn], scalar=modT[:, gbase + fo, cond:cond + 1],
                                                               in1=X[:, fo, c0:c0 + cn], op0=ALU.mult, op1=ALU.add),
                 reads=[bP, b_mod, bX], writes=[bX])

    def normmod_seg(n, segs, gs, bgs, md, bmd, sh0):
        P, bP = p.ps[7], p.bps[7]
        k.op("act", lambda: nc.scalar.activation(out=HX[:, :, 0:n], in_=X[:, :, 0:n], func=AF.Square), reads=[bX], writes=[bHX])
        for kc in range(KC):
            k.op("pe", lambda: nc.tensor.matmul(P[:, 0:n], lhsT=p.ones[:, :], rhs=HX[:, kc, 0:n], start=(kc == 0), stop=(kc == KC - 1)),
                 reads=[bHX, p.b_ones], writes=[bP], inc=(kc == KC - 1))
        k.op("dve", lambda: nc.vector.tensor_scalar(out=R[:, 0:n], in0=P[:, 0:n], scalar1=1.0 / D, scalar2=EPS, op0=ALU.mult, op1=ALU.add), reads=[bP], writes=[bR])
        k.op("act", lambda: nc.scalar.activation(out=R[:, 0:n], in_=R[:, 0:n], func=AF.Sqrt), reads=[bR], writes=[bR])
        k.op("dve", lambda: nc.vector.reciprocal(out=R[:, 0:n], in_=R[:, 0:n]), reads=[bR], writes=[bR])
        for kc in range(KC):
            tk, btk = tmpk[kc % len(tmpk)]
            k.op("dve", lambda: nc.vector.tensor_tensor(out=tk[:, 0:n], in0=X[:, kc, 0:n], in1=R[:, 0:n], op=ALU.mult), reads=[bX, bR], writes=[btk])
            for (c0, cn, cond) in segs:
                k.op("act", lambda: nc.scalar.activation(out=HX[:, kc, c0:c0 + cn], in_=tk[:, c0:c0 + cn], func=AF.Identity,
                                                         scale=gs[:, kc, cond:cond + 1], bias=md[:, sh0 + kc, cond:cond + 1]),
                     reads=[btk, bgs, bmd], writes=[bHX])

    for bi, (t0, n, segs) in enumerate(blocks):
        k.dma("sp", X[:, :, 0:n], fm_view(i_x, TLOC, t0, n), bX, p.b_in)
        k.dma("sp", MT[:, 0:32, 0:n], fm_view(i_m, TLOC, t0, n), bMT, p.b_in)
        for fo in range(32):
            W, bW = load_w8(woutb, b_woutb, fo)
            P, bP = p.ps[fo % 2], p.bps[fo % 2]
            for kc in range(32):
                k.op("pe", lambda: nc.tensor.matmul(P[:, 0:n], lhsT=W[:, kc, :], rhs=MT[:, kc, 0:n], start=(kc == 0), stop=(kc == 31)),
                     reads=[bW, bMT], writes=[bP], inc=(kc == 31))
            resid(P, bP, fo, n, segs, 64)
        normmod_seg(n, segs, gs2, b_gs2, modT, b_mod, 96)
        for half in range(2):
            h0 = half * hh
            hn = min(hh, HC - h0)
            if hn <= 0:
                continue
            for i in range(hn):
                hc = h0 + i
                Wa, bWa = load_w8(wupb, b_wupb, hc)
                Wg, bWg = load_w8(wupb, b_wupb, HC + hc)
                Pa, bPa = p.ps[2 + (hc % 2) * 2], p.bps[2 + (hc % 2) * 2]
                Pg, bPg = p.ps[3 + (hc % 2) * 2], p.bps[3 + (hc % 2) * 2]
                for kc in range(32):
                    k.op("pe", lambda: nc.tensor.matmul(Pa[:, 0:n], lhsT=Wa[:, kc, :], rhs=HX[:, kc, 0:n], start=(kc == 0), stop=(kc == 31)),
                         reads=[bWa, bHX], writes=[bPa], inc=(kc == 31))
                for kc in range(32):
                    k.op("pe", lambda: nc.tensor.matmul(Pg[:, 0:n], lhsT=Wg[:, kc, :], rhs=HX[:, kc, 0:n], start=(kc == 0), stop=(kc == 31)),
                         reads=[bWg, bHX], writes=[bPg], inc=(kc == 31))
                k.op("act", lambda: nc.scalar.activation(out=SG[:, 0:n], in_=Pg[:, 0:n], func=AF.Silu), reads=[bPg], writes=[bSG])
                k.op("dve", lambda: nc.vector.tensor_tensor(out=Fh[:, i, 0:n], in0=Pa[:, 0:n], in1=SG[:, 0:n], op=ALU.mult), reads=[bPa, bSG], writes=[bF])
            for fo in range(32):
                P, bP = p.ps[fo % 2], p.bps[fo % 2]
                W, bW = WD[cnt["wd"] % len(WD)]
                cnt["wd"] += 1
                k.dma("sp", W[:, 0:hn, :], p.wtile_view(wdnb, HC, fo, h0, hn), bW, b_wdnb)
                for i in range(hn):
                    k.op("pe", lambda: nc.tensor.matmul(P[:, 0:n], lhsT=W[:, i, :], rhs=Fh[:, i, 0:n], start=(i == 0), stop=(i == hn - 1)),
                         reads=[bW, bF], writes=[bP], inc=(i == hn - 1))
                resid(P, bP, fo, n, segs, 160)
        k.dma("pool", fm_view(o_x, TLOC, t0, n), X[:, :, 0:n], b_ox, bX)
        normmod_seg(n, segs, gsn, b_gsn, modn, b_modn, 0)
        k.dma("pool", fm_view(o_hx, TLOC, t0, n), HX[:, :, 0:n], b_ohx, bHX)
    return p.finish()


def pblocks(S):
    blocks = [(0, 256)]
    s = 256
    while s < S:
        n = min(512, S - s)
        blocks.append((s, n))
        s += n
    return blocks


class PBase(Base):
    def setup(self, S):
        nc, k = self.nc, self.k
        self.S = S
        self.consts()
        self.i_ident = self.din("ident", [128, 128], BF16)
        self.i_tri = self.din("tri", [64, 256])
        self.ident, self.b_ident = self.sb(self.es, "ident", [128, 128], BF16)
        self.tri, self.b_tri = self.sb(self.es, "tri", [64, 256], F32)
        k.dma("sp", self.ident[:, :], self.i_ident[:, :], self.b_ident, self.b_in)
        k.dma("sp", self.tri[:, :], self.i_tri[:, :], self.b_tri, self.b_in)
        self.rr = 0

    def proj_fm(self, W, bW, ci, HXB, bHXB, n, P, bP, M=128):
        nc, k = self.nc, self.k
        for kc in range(KC):
            k.op("pe", lambda kc=kc: nc.tensor.matmul(P[0:M, 0:n], lhsT=W[:, kc, ci * 128:ci * 128 + M], rhs=HXB[:, kc, 0:n], start=(kc == 0), stop=(kc == KC - 1)),
                 reads=[bW, bHXB], writes=[bP], inc=(kc == KC - 1))

    def proj_tm(self, W, bW, c0, ncols, HXB, bHXB, ts, P, bP):
        nc, k = self.nc, self.k
        for kc in range(KC):
            k.op("pe", lambda kc=kc: nc.tensor.matmul(P[:, 0:ncols], lhsT=HXB[:, kc, ts * 128:(ts + 1) * 128], rhs=W[:, kc, c0:c0 + ncols], start=(kc == 0), stop=(kc == KC - 1)),
                 reads=[bW, bHXB], writes=[bP], inc=(kc == KC - 1))

    def evac_copy(self, out, in_, reads, writes, scale=None, func=None):
        nc, k = self.nc, self.k
        self.rr += 1
        if func is not None or scale is not None or self.rr % 2 == 0:
            f = func if func is not None else AF.Copy
            if scale is None:
                k.op("act", lambda: nc.scalar.activation(out=out, in_=in_, func=f), reads=reads, writes=writes)
            else:
                k.op("act", lambda: nc.scalar.activation(out=out, in_=in_, func=f, scale=scale), reads=reads, writes=writes)
        else:
            k.op("dve", lambda: nc.vector.tensor_copy(out=out, in_=in_), reads=reads, writes=writes)

    def head_post(self, H, bH, nv, T, SQ, bSQ, R, bR, G, bG, gain_ap, b_gain, OUTb, bOUTb, o_m, b_om, row0, s0):
        nc, k = self.nc, self.k
        S = self.S
        P, bP = self.ps[7], self.bps[7]
        k.op("act", lambda: nc.scalar.activation(out=SQ[:, :, 0:T], in_=H[:, :, 0:T], func=AF.Square), reads=[bH], writes=[bSQ])
        for c0 in range(0, T, 512):
            cn = min(512, T - c0)
            for vj in range(nv):
                k.op("pe", lambda vj=vj: nc.tensor.matmul(P[:, 0:cn], lhsT=self.ones[:, :], rhs=SQ[:, vj, c0:c0 + cn], start=(vj == 0), stop=(vj == nv - 1)),
                     reads=[bSQ, self.b_ones], writes=[bP], inc=(vj == nv - 1))
            k.op("dve", lambda: nc.vector.tensor_scalar(out=R[:, 0:cn], in0=P[:, 0:cn], scalar1=1.0 / (128 * nv), scalar2=EPS, op0=ALU.mult, op1=ALU.add),
                 reads=[bP], writes=[bR])
            k.op("act", lambda: nc.scalar.activation(out=R[:, 0:cn], in_=R[:, 0:cn], func=AF.Sqrt), reads=[bR], writes=[bR])
            k.op("dve", lambda: nc.vector.reciprocal(out=R[:, 0:cn], in_=R[:, 0:cn]), reads=[bR], writes=[bR])
            for vj in range(nv):
                k.op("dve", lambda vj=vj: nc.vector.tensor_tensor(out=H[:, vj, c0:c0 + cn], in0=H[:, vj, c0:c0 + cn], in1=R[:, 0:cn], op=ALU.mult),
                     reads=[bH, bR], writes=[bH])
                k.op("dve", lambda vj=vj: nc.vector.scalar_tensor_tensor(out=OUTb[:, vj, c0:c0 + cn], in0=H[:, vj, c0:c0 + cn], scalar=gain_ap(vj),
                                                                        in1=G[:, vj, c0:c0 + cn], op0=ALU.mult, op1=ALU.mult),
                     reads=[bH, bG, b_gain], writes=[bOUTb])
        for vj in range(nv):
            k.dma("pool", mkap(o_m, (row0 + vj * 128) * S + s0, [[S, 128], [1, T]]), OUTb[:, vj, 0:T], b_om, bOUTb)


def build_PE(ROWS):
    p = PBase()
    nc, k = p.nc, p.k
    SEQ = ROWS * 64
    S = SEQ + 256
    NT = ROWS // 2
    i_hx = p.din("hx", [D, S], BF16)
    i_win = p.din("win", [D, 2048])
    i_lbl = p.din("lbl", [128, 8])
    i_jsel = p.din("jsel", [128, 1])
    i_hgg = p.din("hgg", [128, 2])
    i_nab = p.din("nab", [128, 2 * 5 * 640])
    i_rst = p.din("rst", [128, 2048])
    o_m = p.dout("mloc", [512, S], BF16)
    b_om = k.buf("o_m")
    p.setup(S)
    es = p.es
    wbf, b_wbf = p.dint("wbf", [16 * 128 * 32 * 128], BF16)
    na_q, b_naq = p.dint("na_q", [2 * 128 * S], BF16)
    na_k, b_nak = p.dint("na_k", [2 * 128 * S], BF16)
    na_v, b_nav = p.dint("na_v", [S * 256], BF16)
    hg_q, b_hgq = p.dint("hg_q", [2 * 128 * S], F32)
    hg_ff, b_hgff = p.dint("hg_ff", [2 * 128 * S], F32)
    hg_fb, b_hgfb = p.dint("hg_fb", [2 * 128 * S], F32)
    hg_g, b_hgg_ = p.dint("hg_g", [2 * 128 * S], BF16)
    hg_i, b_hgi = p.dint("hg_i", [S * 256], BF16)
    hg_o, b_hgo = p.dint("hg_o", [2 * 128 * S], F32)
    with ExitStack() as e0:
        p.cast_tiles(i_win, D, 2048, wbf, b_wbf, e0)
        k.barrier()
    del p._cst

    def rows_view(h, hidx, s0, n):
        return mkap(h, hidx * 128 * S + s0, [[S, 128], [1, n]])

    with ExitStack() as e1:
        Wfm, bWfm = p.sb(e1, "Wfm", [128, 32, 512], BF16)
        Wtm, bWtm = p.sb(e1, "Wtm", [128, 32, 256], BF16)
        for i in range(4):
            k.dma("sp", Wfm[:, :, i * 128:(i + 1) * 128], p.wtile_view(wbf, 32, i), bWfm, b_wbf)
        for i in range(2):
            k.dma("sp", Wtm[:, :, i * 128:(i + 1) * 128], p.wtile_view(wbf, 32, 4 + i), bWtm, b_wbf)
        HXBs = [p.sb(e1, f"HXB{i}", [128, 32, 512], BF16) for i in range(2)]
        stg = [p.sb(e1, f"stg{i}", [128, 512], BF16) for i in range(4)]
        si = 0
        for bi, (s0, n) in enumerate(pblocks(S)):
            HXB, bHXB = HXBs[bi % 2]
            k.dma("sp", HXB[:, :, 0:n], fm_view(i_hx, S, s0, n), bHXB, p.b_in)
            for ci in range(4):
                P, bP = p.ps[ci % 4], p.bps[ci % 4]
                p.proj_fm(Wfm, bWfm, ci, HXB, bHXB, n, P, bP)
                st, bst = stg[si % 4]
                si += 1
                p.evac_copy(st[:, 0:n], P[:, 0:n], [bP], [bst], scale=(128 ** -0.5 if ci < 2 else None))
                dh, bd = (na_q, b_naq) if ci < 2 else (na_k, b_nak)
                k.dma("pool", rows_view(dh, ci % 2, s0, n), st[:, 0:n], bd, bst)
            for ts in range(n // 128):
                P, bP = p.ps[4 + ts % 2], p.bps[4 + ts % 2]
                p.proj_tm(Wtm, bWtm, 0, 256, HXB, bHXB, ts, P, bP)
                st, bst = stg[si % 4]
                si += 1
                p.evac_copy(st[:, 0:256], P[:, 0:256], [bP], [bst])
                k.dma("pool", mkap(na_v, (s0 + ts * 128) * 256, [[256, 128], [1, 256]]), st[:, 0:256], b_nav, bst)
        k.barrier()

    with ExitStack() as e2:
        kT, bkT = p.sb(e2, "kT", [128, S], BF16)
        qT, bqT = p.sb(e2, "qT", [128, S], BF16)
        V, bV = p.sb(e2, "V", [128, S // 128, 128], BF16)
        bias, bbias = p.sb(e2, "bias", [128, 5, 640], F32)
        Sc, bSc = p.sb(e2, "Sc", [128, 896], F32)
        Pn, bPn = p.sb(e2, "Pn", [128, 896], BF16)
        PT, bPT = p.sb(e2, "PT", [128, 7, 128], BF16)
        mx, bmx = p.sb(e2, "mx", [128, 1], F32)
        sm, bsm = p.sb(e2, "sm", [128, 1], F32)
        ostg = [p.sb(e2, f"ostg{i}", [128, 128], BF16) for i in range(3)]
        oi = 0
        PTp = p.ps[2][:, :].bitcast(BF16)
        bPTp = p.bps[2]

        def softmax_pv(nk, vtiles, h, q0):
            nonlocal oi
            k.op("dve", lambda: nc.vector.tensor_reduce(out=mx[:, :], in_=Sc[:, 0:nk], axis=AX.X, op=ALU.max, negate=True), reads=[bSc], writes=[bmx])
            k.op("act", lambda: nc.scalar.activation(out=Sc[:, 0:nk], in_=Sc[:, 0:nk], func=AF.Exp, bias=mx[:, 0:1], scale=1.0), reads=[bSc, bmx], writes=[bSc])
            k.op("dve", lambda: nc.vector.tensor_reduce(out=sm[:, :], in_=Sc[:, 0:nk], axis=AX.X, op=ALU.add), reads=[bSc], writes=[bsm])
            k.op("dve", lambda: nc.vector.reciprocal(out=sm[:, :], in_=sm[:, :]), reads=[bsm], writes=[bsm])
            k.op("dve", lambda: nc.vector.tensor_scalar(out=Pn[:, 0:nk], in0=Sc[:, 0:nk], scalar1=sm[:, 0:1], scalar2=None, op0=ALU.mult), reads=[bSc, bsm], writes=[bPn])
            nt = nk // 128
            for i in range(nt):
                k.op("pe", lambda i=i: nc.tensor.transpose(out=PTp[:, i * 128:(i + 1) * 128], in_=Pn[:, i * 128:(i + 1) * 128], identity=p.ident[:, :]),
                     reads=[bPn, p.b_ident], writes=[bPTp], inc=(i == nt - 1))
            p.evac_copy(PT[:, 0:nt, :], PTp[:, 0:nt * 128].rearrange("p (a c) -> p a c", c=128), [bPTp], [bPT])
            OA, bOA = p.ps[3], p.bps[3]
            for i, vt in enumerate(vtiles):
                k.op("pe", lambda i=i, vt=vt: nc.tensor.matmul(OA[:, 0:128], lhsT=V[:, vt, :], rhs=PT[:, i, :], start=(i == 0), stop=(i == nt - 1)),
                     reads=[bV, bPT], writes=[bOA], inc=(i == nt - 1))
            st, bst = ostg[oi % 3]
            oi += 1
            p.evac_copy(st[:, :], OA[:, 0:128], [bOA], [bst])
            k.dma("pool", mkap(o_m, h * 128 * S + q0, [[S, 128], [1, 128]]), st[:, :], b_om, bst)

        for h in range(2):
            k.dma("sp", kT[:, :], rows_view(na_k, h, 0, S), bkT, b_nak)
            k.dma("sp", qT[:, :], rows_view(na_q, h, 0, S), bqT, b_naq)
            k.dma("sp", V[:, :, :], mkap(na_v, h * 128, [[256, 128], [128 * 256, S // 128], [1, 128]]), bV, b_nav)
            k.dma("sp", bias[:, :, :].rearrange("p a b -> p (a b)"), i_nab[:, h * 3200:(h + 1) * 3200], bbias, p.b_in)
            SA, bSA = p.ps[0], p.bps[0]
            SB, bSB = p.ps[1], p.bps[1]
            for qb in range(2):
                q0 = qb * 128
                k.op("pe", lambda: nc.tensor.matmul(SB[:, 0:256], lhsT=qT[:, q0:q0 + 128], rhs=kT[:, 0:256], start=True, stop=True),
                     reads=[bqT, bkT], writes=[bSB])
                p.evac_copy(Sc[:, 0:256], SB[:, 0:256], [bSB], [bSc])
                softmax_pv(256, [0, 1], h, q0)
            for j in range(NT):
                ts = min(max(j - 2, 0), NT - 5)
                typ = 0 if j == 0 else (1 if j == 1 else (3 if j == NT - 2 else (4 if j == NT - 1 else 2)))
                q0 = 256 + j * 128
                kl0 = 256 + ts * 128
                k.op("pe", lambda: nc.tensor.matmul(SA[:, 0:512], lhsT=qT[:, q0:q0 + 128], rhs=kT[:, kl0:kl0 + 512], start=True, stop=True),
                     reads=[bqT, bkT], writes=[bSA])
                k.op("pe", lambda: nc.tensor.matmul(SB[:, 0:128], lhsT=qT[:, q0:q0 + 128], rhs=kT[:, kl0 + 512:kl0 + 640], start=True, stop=True),
                     reads=[bqT, bkT], writes=[bSB], inc=False)
                k.op("pe", lambda: nc.tensor.matmul(SB[:, 128:384], lhsT=qT[:, q0:q0 + 128], rhs=kT[:, 0:256], start=True, stop=True),
                     reads=[bqT, bkT], writes=[bSB])
                k.op("dve", lambda: nc.vector.tensor_tensor(out=Sc[:, 0:512], in0=SA[:, 0:512], in1=bias[:, typ, 0:512], op=ALU.add), reads=[bSA, bbias], writes=[bSc])
                k.op("dve", lambda: nc.vector.tensor_tensor(out=Sc[:, 512:640], in0=SB[:, 0:128], in1=bias[:, typ, 512:640], op=ALU.add), reads=[bSB, bbias], writes=[bSc])
                k.op("act", lambda: nc.scalar.copy(out=Sc[:, 640:896], in_=SB[:, 128:384]), reads=[bSB], writes=[bSc])
                softmax_pv(896, [2 + ts + i for i in range(5)] + [0, 1], h, q0)
        k.barrier()

    with ExitStack() as e3:
        Wfm, bWfm = p.sb(e3, "Wfm3", [128, 32, 1024], BF16)
        Wtm, bWtm = p.sb(e3, "Wtm3", [128, 32, 256], BF16)
        for i, cc in enumerate([6, 7, 8, 9, 10, 11, 14, 15]):
            k.dma("sp", Wfm[:, :, i * 128:(i + 1) * 128], p.wtile_view(wbf, 32, cc), bWfm, b_wbf)
        for i in range(2):
            k.dma("sp", Wtm[:, :, i * 128:(i + 1) * 128], p.wtile_view(wbf, 32, 12 + i), bWtm, b_wbf)
        HXBs = [p.sb(e3, f"HXB3{i}", [128, 32, 512], BF16) for i in range(2)]
        stgf = [p.sb(e3, f"stgf{i}", [128, 512], F32) for i in range(4)]
        stgb = [p.sb(e3, f"stgb{i}", [128, 512], BF16) for i in range(3)]
        si = 0
        sj = 0
        dests = [(hg_q, b_hgq), (hg_ff, b_hgff), (hg_fb, b_hgfb), (hg_g, b_hgg_)]
        for bi, (s0, n) in enumerate(pblocks(S)):
            HXB, bHXB = HXBs[bi % 2]
            k.dma("sp", HXB[:, :, 0:n], fm_view(i_hx, S, s0, n), bHXB, p.b_in)
            for ci in range(8):
                P, bP = p.ps[ci % 4], p.bps[ci % 4]
                p.proj_fm(Wfm, bWfm, ci, HXB, bHXB, n, P, bP)
                dh, bd = dests[ci // 2]
                if ci < 6:
                    st, bst = stgf[si % 4]
                    si += 1
                    p.evac_copy(st[:, 0:n], P[:, 0:n], [bP], [bst])
                else:
                    st, bst = stgb[sj % 3]
                    sj += 1
                    p.evac_copy(st[:, 0:n], P[:, 0:n], [bP], [bst], func=AF.Silu)
                k.dma("pool", rows_view(dh, ci % 2, s0, n), st[:, 0:n], bd, bst)
            for ts in range(n // 128):
                P, bP = p.ps[4 + ts % 2], p.bps[4 + ts % 2]
                p.proj_tm(Wtm, bWtm, 0, 256, HXB, bHXB, ts, P, bP)
                st, bst = stgb[sj % 3]
                sj += 1
                p.evac_copy(st[:, 0:256], P[:, 0:256], [bP], [bst])
                k.dma("pool", mkap(hg_i, (s0 + ts * 128) * 256, [[256, 128], [1, 256]]), st[:, 0:256], b_hgi, bst)
        k.barrier()

    with ExitStack() as e4:
        TM = 2048
        lbl, blbl = p.sb(e4, "lbl", [128, 8], F32)
        jsel, bjsel = p.sb(e4, "jsel", [128, 1], F32)
        hgg, bhgg = p.sb(e4, "hgg", [128, 2], F32)
        rst, brst = p.sb(e4, "rst", [128, TM], F32)
        lb, blb = p.sb(e4, "lb", [128, 4], F32)
        oml, boml = p.sb(e4, "oml", [128, 4], F32)
        k.dma("sp", lbl[:, :], i_lbl[:, :], blbl, p.b_in)
        k.dma("sp", jsel[:, :], i_jsel[:, :], bjsel, p.b_in)
        k.dma("sp", hgg[:, :], i_hgg[:, :], bhgg, p.b_in)
        k.dma("sp", rst[:, :], i_rst[:, :], brst, p.b_in)
        k.op("dve", lambda: nc.vector.tensor_tensor(out=lb[:, :], in0=lbl[:, 4:8], in1=lbl[:, 0:4], op=ALU.subtract), reads=[blbl], writes=[blb])
        k.op("act", lambda: nc.scalar.activation(out=lb[:, :], in_=lb[:, :], func=AF.Sigmoid), reads=[blb], writes=[blb])
        k.op("dve", lambda: nc.vector.tensor_scalar(out=lb[:, :], in0=lb[:, :], scalar1=jsel[:, 0:1], scalar2=None, op0=ALU.mult), reads=[blb, bjsel], writes=[blb])
        k.op("dve", lambda: nc.vector.tensor_scalar(out=oml[:, :], in0=lb[:, :], scalar1=-1.0, scalar2=1.0, op0=ALU.mult, op1=ALU.add), reads=[blb], writes=[boml])
        q, bq = p.sb(e4, "q", [128, TM], F32)
        fp, bfp = p.sb(e4, "fp", [128, TM], F32)
        F1, bF1 = p.sb(e4, "F1", [128, TM], F32)
        K1, bK1 = p.sb(e4, "K1", [128, TM], F32)
        Bc, bBc = p.sb(e4, "Bc", [128, TM], F32)
        CUM, bCUM = p.sb(e4, "CUM", [128, TM], F32)
        EX, bEX = p.sb(e4, "EX", [128, TM], F32)
        QT, bQT = p.sb(e4, "QT", [128, TM], BF16)
        KTb, bKTb = p.sb(e4, "KTb", [128, TM], BF16)
        KH, bKH = p.sb(e4, "KH", [128, TM], BF16)
        EL, bEL = p.sb(e4, "EL", [128, TM // 32], F32)
        Vt, bVt = p.sb(e4, "Vt", [32, TM // 32, 128], BF16)
        O, bO = p.sb(e4, "O", [128, 1, TM], F32)
        Ofw, bOfw = p.sb(e4, "Ofw", [128, TM], F32)
        G, bG = p.sb(e4, "G", [128, 1, TM], BF16)
        SQ, bSQ = p.sb(e4, "SQ4", [128, 1, TM], BF16)
        OUTb, bOUTb = p.sb(e4, "OUTb", [128, 1, TM], BF16)
        R, bR = p.sb(e4, "R4", [128, 512], F32)
        St, bSt = p.sb(e4, "St", [128, 128], F32)
        Sbf, bSbf = p.sb(e4, "Sbf", [128, 128], BF16)
        A, bA = p.sb(e4, "A", [32, 32], BF16)
        KHt, bKHt = p.sb(e4, "KHt", [32, 128], BF16)
        sbs = [(0, 256)] + [(256 + i * TM, min(TM, SEQ - i * TM)) for i in range((SEQ + TM - 1) // TM)]
        KHp = p.ps[1][:, :].bitcast(BF16)
        for h in range(2):
            for d in range(2):
                col = d * 2 + h
                k.op("dve", lambda: nc.vector.memset(St[:, :], 0.0), writes=[bSt])
                k.op("dve", lambda: nc.vector.memset(Sbf[:, :], 0.0), writes=[bSbf])
                order = sbs if d == 0 else [sbs[0]] + sbs[:0:-1]
                for (s0, T) in order:
                    nch = T // 32
                    fph, bfph = (hg_ff, b_hgff) if d == 0 else (hg_fb, b_hgfb)
                    k.dma("sp", q[:, 0:T], rows_view(hg_q, h, s0, T), bq, b_hgq)
                    k.dma("sp", fp[:, 0:T], rows_view(fph, h, s0, T), bfp, bfph)
                    k.dma("sp", Vt[:, 0:nch, :], mkap(hg_i, s0 * 256 + h * 128, [[256, 32], [32 * 256, nch], [1, 128]]), bVt, b_hgi)
                    if d == 1:
                        k.dma("sp", Ofw[:, 0:T], rows_view(hg_o, h, s0, T), bOfw, b_hgo)
                        k.dma("sp", G[:, 0, 0:T], rows_view(hg_g, h, s0, T), bG, b_hgg_)
                    k.op("act", lambda: nc.scalar.activation(out=F1[:, 0:T], in_=fp[:, 0:T], func=AF.Sigmoid), reads=[bfp], writes=[bF1])
                    k.op("dve", lambda: nc.vector.tensor_scalar(out=F1[:, 0:T], in0=F1[:, 0:T], scalar1=oml[:, col:col + 1], scalar2=lb[:, col:col + 1], op0=ALU.mult, op1=ALU.add),
                         reads=[bF1, boml, blb], writes=[bF1])
                    k.op("act", lambda: nc.scalar.activation(out=K1[:, 0:T], in_=fp[:, 0:T], func=AF.Sigmoid, scale=-1.0), reads=[bfp], writes=[bK1])
                    k.op("pool", lambda: nc.gpsimd.tensor_scalar(out=K1[:, 0:T], in0=K1[:, 0:T], scalar1=oml[:, col:col + 1], scalar2=None, op0=ALU.mult),
                         reads=[bK1, boml], writes=[bK1])
                    k.op("act", lambda: nc.scalar.activation(out=F1[:, 0:T], in_=F1[:, 0:T], func=AF.Ln), reads=[bF1], writes=[bF1])
                    k.op("dve", lambda: nc.vector.tensor_tensor_scan(out=Bc[:, 0:T], data0=rst[:, 0:T], data1=F1[:, 0:T], initial=0.0, op0=ALU.mult, op1=ALU.add),
                         reads=[brst, bF1], writes=[bBc])
                    B3 = Bc[:, 0:T].rearrange("p (c s) -> p c s", s=32)
                    if d == 0:
                        cum = Bc
                        bcum = bBc
                        C3 = B3
                        last_ap = B3[:, :, 31]
                    else:
                        k.op("dve", lambda: nc.vector.tensor_tensor(out=CUM[:, 0:T], in0=F1[:, 0:T], in1=Bc[:, 0:T], op=ALU.subtract), reads=[bF1, bBc], writes=[bCUM])
                        C3 = CUM[:, 0:T].rearrange("p (c s) -> p c s", s=32)
                        k.op("dve", lambda: nc.vector.tensor_tensor(out=C3, in0=C3, in1=B3[:, :, 31:32].broadcast_to([128, nch, 32]), op=ALU.add), reads=[bCUM, bBc], writes=[bCUM])
                        cum = CUM
                        bcum = bCUM
                        last_ap = C3[:, :, 0]
                    k.op("act", lambda: nc.scalar.activation(out=EX[:, 0:T], in_=cum[:, 0:T], func=AF.Exp), reads=[bcum], writes=[bEX])
                    k.op("dve", lambda: nc.vector.tensor_tensor(out=QT[:, 0:T], in0=q[:, 0:T], in1=EX[:, 0:T], op=ALU.mult), reads=[bq, bEX], writes=[bQT])
                    k.op("act", lambda: nc.scalar.activation(out=EX[:, 0:T], in_=cum[:, 0:T], func=AF.Exp, scale=-1.0), reads=[bcum], writes=[bEX])
                    k.op("dve", lambda: nc.vector.tensor_tensor(out=K1[:, 0:T], in0=K1[:, 0:T], in1=EX[:, 0:T], op=ALU.mult), reads=[bK1, bEX], writes=[bK1])
                    k.op("act", lambda: nc.scalar.copy(out=KTb[:, 0:T], in_=K1[:, 0:T]), reads=[bK1], writes=[bKTb])
                    k.op("act", lambda: nc.scalar.activation(out=EL[:, 0:nch], in_=last_ap, func=AF.Exp), reads=[bcum], writes=[bEL])
                    k.op("dve", lambda: nc.vector.tensor_tensor(out=KH[:, 0:T].rearrange("p (c s) -> p c s", s=32), in0=K1[:, 0:T].rearrange("p (c s) -> p c s", s=32),
                                                                in1=EL[:, 0:nch].unsqueeze(2).broadcast_to([128, nch, 32]), op=ALU.mult),
                         reads=[bK1, bEL], writes=[bKH])
                    crange = range(nch) if d == 0 else range(nch - 1, -1, -1)
                    moff = 0 if d == 0 else 32
                    for c in crange:
                        cs = slice(c * 32, (c + 1) * 32)
                        AT, bAT = p.ps[0], p.bps[0]
                        k.op("pe", lambda: nc.tensor.matmul(AT[0:32, 0:32], lhsT=KTb[:, cs], rhs=QT[:, cs], start=True, stop=True), reads=[bKTb, bQT], writes=[bAT])
                        k.op("dve", lambda: nc.vector.tensor_tensor(out=A[:, :], in0=AT[0:32, 0:32], in1=p.tri[0:32, moff:moff + 32], op=ALU.mult), reads=[bAT, p.b_tri], writes=[bA])
                        k.op("pe", lambda: nc.tensor.transpose(out=KHp[0:32, 0:128], in_=KH[:, cs], identity=p.ident[:, :]), reads=[bKH, p.b_ident], writes=[p.bps[1]])
                        k.op("act", lambda: nc.scalar.copy(out=KHt[:, :], in_=KHp[0:32, 0:128]), reads=[p.bps[1]], writes=[bKHt])
                        OP, bOP = p.ps[2], p.bps[2]
                        k.op("pe", lambda: nc.tensor.matmul(OP[:, 0:32], lhsT=Vt[:, c, :], rhs=A[:, :], start=True, stop=False), reads=[bVt, bA], writes=[bOP], inc=False)
                        k.op("pe", lambda: nc.tensor.matmul(OP[:, 0:32], lhsT=Sbf[:, :], rhs=QT[:, cs], start=False, stop=True), reads=[bSbf, bQT], writes=[bOP])
                        UP, bUP = p.ps[3], p.bps[3]
                        k.op("pe", lambda: nc.tensor.matmul(UP[:, 0:128], lhsT=KHt[:, :], rhs=Vt[:, c, :], start=True, stop=True), reads=[bKHt, bVt], writes=[bUP])
                        if d == 0:
                            k.op("act", lambda: nc.scalar.copy(out=O[:, 0, cs], in_=OP[:, 0:32]), reads=[bOP], writes=[bO])
                        else:
                            k.op("pool" if False else "dve", lambda: nc.vector.tensor_tensor(out=O[:, 0, cs], in0=OP[:, 0:32], in1=Ofw[:, cs], op=ALU.add), reads=[bOP, bOfw], writes=[bO])
                        k.op("dve", lambda: nc.vector.scalar_tensor_tensor(out=St[:, :], in0=St[:, :], scalar=EL[:, c:c + 1], in1=UP[:, 0:128], op0=ALU.mult, op1=ALU.add),
                             reads=[bSt, bEL, bUP], writes=[bSt])
                        k.op("act", lambda: nc.scalar.copy(out=Sbf[:, :], in_=St[:, :]), reads=[bSt], writes=[bSbf])
                    if d == 0:
                        k.dma("pool", rows_view(hg_o, h, s0, T), O[:, 0, 0:T], b_hgo, bO)
                    else:
                        p.head_post(O, bO, 1, T, SQ, bSQ, R, bR, G, bG, lambda vj: hgg[:, h:h + 1], bhgg, OUTb, bOUTb, o_m, b_om, 256 + h * 128, s0)
        k.barrier()
    return p.finish()


def col_layout(v):
    v = np.asarray(v)
    return np.ascontiguousarray(v.reshape(-1, 128).T)


def const_tables():
    s = np.arange(64)
    tri = np.zeros((64, 256), np.float32)
    f32_ = (s[:32, None] <= s[None, :32]).astype(np.float32)
    tri[:32, 0:32] = f32_
    tri[:32, 32:64] = f32_.T
    f64_ = (s[:, None] <= s[None, :]).astype(np.float32)
    tri[:, 64:128] = f64_
    tri[:, 128:192] = f64_.T
    rst = np.ones((128, 2048), np.float32)
    rst[:, ::32] = 0.0
    rst64 = np.ones((1, 2048), np.float32)
    rst64[:, ::64] = 0.0
    ident = np.eye(128, dtype=np.float32).astype(NPBF)
    return tri, rst, rst64, ident


def prep_even_win(w, c):
    return np.ascontiguousarray(np.concatenate([w[:, sec * 2048 + 256 * c: sec * 2048 + 256 * c + 256] for sec in range(8)], axis=1))


def na_bias_tables(rpb, c, ROWS):
    NT = ROWS // 2
    out = np.empty((2, 5, 128, 640), np.float32)
    qi = np.arange(128)
    ki = np.arange(640)
    for ti, j in enumerate([0, 1, 2, NT - 2, NT - 1]):
        ts = min(max(j - 2, 0), NT - 5)
        r = 2 * j + qi // 64
        cq = qi % 64
        kr = 2 * ts + ki // 64
        kc = ki % 64
        r0 = np.clip(r - 4, 0, ROWS - 8)
        c0 = np.clip(cq - 8, 0, 64 - 16)
        inwin = ((kr[None, :] >= r0[:, None]) & (kr[None, :] < r0[:, None] + 8) &
                 (kc[None, :] >= c0[:, None]) & (kc[None, :] < c0[:, None] + 16))
        di = np.clip(kr[None, :] - r[:, None] + 7, 0, 14)
        dj = np.clip(kc[None, :] - cq[:, None] + 15, 0, 30)
        for h in range(2):
            g = rpb[2 * c + h][di, dj]
            out[h, ti] = np.where(inwin, g, np.float32(-30000.0))
    return np.ascontiguousarray(out.transpose(2, 0, 1, 3).reshape(128, 2 * 5 * 640))


def even_small_params(lbl_all, j, hgg_all, c):
    lbl = np.empty((128, 8), np.float32)
    hg = np.empty((128, 2), np.float32)
    for d in range(2):
        for h in range(2):
            sl = slice((2 * c + h) * 128, (2 * c + h + 1) * 128)
            lbl[:, d * 2 + h] = lbl_all[d, 0, sl]
            lbl[:, 4 + d * 2 + h] = lbl_all[d, 1, sl]
    for h in range(2):
        hg[:, h] = hgg_all[j, (2 * c + h) * 128:(2 * c + h + 1) * 128]
    jsel = np.full((128, 1), float(j), np.float32)
    return lbl, jsel, hg


def build_PO(ROWS):
    p = PBase()
    nc, k = p.nc, p.k
    SEQ = ROWS * 64
    S = SEQ + 256
    i_hx = p.din("hx", [D, S], BF16)
    i_win = p.din("win", [D, 2048])
    i_wg = p.din("wg", [D, 4])
    i_bg = p.din("bg", [1, 4])
    i_mlg = p.din("mlg", [128, 4])
    i_ropeR = p.din("ropeR", [128, 2 * ROWS])
    i_ropeC = p.din("ropeC", [128, 128])
    i_rst64 = p.din("rst64", [1, 2048])
    o_m = p.dout("mloc", [512, S], BF16)
    b_om = k.buf("o_m")
    p.setup(S)
    es = p.es
    wbf, b_wbf = p.dint("wbf", [16 * 128 * 32 * 128], BF16)
    ml_q, b_mlq = p.dint("ml_q", [2 * 128 * S], BF16)
    ml_k, b_mlk = p.dint("ml_k", [2 * 128 * S], BF16)
    ml_o, b_mlo = p.dint("ml_o", [4 * 128 * S], BF16)
    ml_v, b_mlv = p.dint("ml_v", [S * 512], BF16)
    ml_g, b_mlg_ = p.dint("ml_g", [4 * S], F32)
    ml_h, b_mlh = p.dint("ml_h", [4 * 128 * S], F32)
    with ExitStack() as e0:
        p.cast_tiles(i_win, D, 2048, wbf, b_wbf, e0)
        k.barrier()
    del p._cst

    def rows_view(h, hidx, s0, n):
        return mkap(h, hidx * 128 * S + s0, [[S, 128], [1, n]])

    with ExitStack() as e1:
        Wfm, bWfm = p.sb(e1, "Wfm", [128, 32, 1024], BF16)
        for i in range(8):
            k.dma("sp", Wfm[:, :, i * 128:(i + 1) * 128], p.wtile_view(wbf, 32, i), bWfm, b_wbf)
        wg32, bwg32 = p.sb(e1, "wg32", [128, 32, 4], F32)
        Wg, bWg = p.sb(e1, "Wg", [128, 32, 4], BF16)
        k.dma("sp", wg32[:, :, :], mkap(i_wg, 0, [[4, 128], [128 * 4, 32], [1, 4]]), bwg32, p.b_in)
        k.op("dve", lambda: nc.vector.tensor_copy(out=Wg[:, :, :], in_=wg32[:, :, :]), reads=[bwg32], writes=[bWg])
        ropeR, bropeR = p.sb(e1, "ropeR", [128, 2 * ROWS], F32)
        ropeC, bropeC = p.sb(e1, "ropeC", [128, 128], F32)
        k.dma("sp", ropeR[:, :], i_ropeR[:, :], bropeR, p.b_in)
        k.dma("sp", ropeC[:, :], i_ropeC[:, :], bropeC, p.b_in)
        HXBs = [p.sb(e1, f"HXB{i}", [128, 32, 512], BF16) for i in range(2)]
        stg = [p.sb(e1, f"stg{i}", [128, 512], BF16) for i in range(4)]
        T1, bT1 = p.sb(e1, "T1", [128, 512], F32)
        T2, bT2 = p.sb(e1, "T2", [128, 512], F32)
        gst, bgst = p.sb(e1, "gst", [4, 512], F32)
        si = 0
        for bi, (s0, n) in enumerate(pblocks(S)):
            HXB, bHXB = HXBs[bi % 2]
            k.dma("sp", HXB[:, :, 0:n], fm_view(i_hx, S, s0, n), bHXB, p.b_in)
            for which in range(2):
                scale = 1.0 if which == 0 else 256 ** -0.5
                dh, bd = (ml_q, b_mlq) if which == 0 else (ml_k, b_mlk)
                for c in range(2):
                    cc = which * 4 + c
                    P1, bP1 = p.ps[(which * 2 + c) % 2 * 2], p.bps[(which * 2 + c) % 2 * 2]
                    p.proj_fm(Wfm, bWfm, cc, HXB, bHXB, n, P1, bP1)
                    st, bst = stg[si % 4]
                    si += 1
                    if s0 < 256:
                        p.evac_copy(st[:, 0:n], P1[:, 0:n], [bP1], [bst], scale=scale)
                    else:
                        P2, bP2 = p.ps[(which * 2 + c) % 2 * 2 + 1], p.bps[(which * 2 + c) % 2 * 2 + 1]
                        p.proj_fm(Wfm, bWfm, cc + 2, HXB, bHXB, n, P2, bP2)
                        t0 = s0 - 256
                        r0, nr = t0 // 64, n // 64
                        if c == 0:
                            cosb = ropeR[:, r0:r0 + nr].unsqueeze(2).broadcast_to([128, nr, 64])
                            sinb = ropeR[:, ROWS + r0:ROWS + r0 + nr].unsqueeze(2).broadcast_to([128, nr, 64])
                        else:
                            cosb = ropeC[:, 0:64].unsqueeze(1).broadcast_to([128, nr, 64])
                            sinb = ropeC[:, 64:128].unsqueeze(1).broadcast_to([128, nr, 64])
                        v3 = lambda t: t[:, 0:n].rearrange("p (r c) -> p r c", c=64)
                        k.op("dve", lambda: nc.vector.tensor_tensor(out=v3(T1), in0=v3(P1), in1=cosb, op=ALU.mult), reads=[bP1, bropeR, bropeC], writes=[bT1])
                        k.op("dve", lambda: nc.vector.tensor_tensor(out=v3(T2), in0=v3(P2), in1=sinb, op=ALU.mult), reads=[bP2, bropeR, bropeC], writes=[bT2])
                        k.op("pool", lambda: nc.gpsimd.tensor_tensor(out=T1[:, 0:n], in0=T1[:, 0:n], in1=T2[:, 0:n], op=ALU.add), reads=[bT1, bT2], writes=[bT1])
                        k.op("act", lambda: nc.scalar.activation(out=st[:, 0:n], in_=T1[:, 0:n], func=AF.Copy, scale=scale), reads=[bT1], writes=[bst])
                    k.dma("pool", rows_view(dh, c, s0, n), st[:, 0:n], bd, bst)
            PG, bPG = p.ps[6], p.bps[6]
            p.proj_fm(Wg, bWg, 0, HXB, bHXB, n, PG, bPG, M=4)
            k.op("dve", lambda: nc.vector.tensor_copy(out=gst[:, 0:n], in_=PG[0:4, 0:n]), reads=[bPG], writes=[bgst])
            k.dma("pool", mkap(ml_g, s0, [[S, 4], [1, n]]), gst[:, 0:n], b_mlg_, bgst)
        k.barrier()

    with ExitStack() as e1:
        Wfm, bWfm = p.sb(e1, "Wfmb", [128, 32, 512], BF16)
        Wtm, bWtm = p.sb(e1, "Wtmb", [128, 32, 512], BF16)
        for i in range(4):
            k.dma("sp", Wfm[:, :, i * 128:(i + 1) * 128], p.wtile_view(wbf, 32, 8 + i), bWfm, b_wbf)
            k.dma("sp", Wtm[:, :, i * 128:(i + 1) * 128], p.wtile_view(wbf, 32, 12 + i), bWtm, b_wbf)
        HXBs = [p.sb(e1, f"HXBb{i}", [128, 32, 512], BF16) for i in range(2)]
        stg = [p.sb(e1, f"stgb{i}", [128, 512], BF16) for i in range(4)]
        si = 0
        for bi, (s0, n) in enumerate(pblocks(S)):
            HXB, bHXB = HXBs[bi % 2]
            k.dma("sp", HXB[:, :, 0:n], fm_view(i_hx, S, s0, n), bHXB, p.b_in)
            for ci in range(4):
                P, bP = p.ps[ci % 4], p.bps[ci % 4]
                p.proj_fm(Wfm, bWfm, ci, HXB, bHXB, n, P, bP)
                st, bst = stg[si % 4]
                si += 1
                p.evac_copy(st[:, 0:n], P[:, 0:n], [bP], [bst], func=AF.Sigmoid)
                k.dma("pool", rows_view(ml_o, ci, s0, n), st[:, 0:n], b_mlo, bst)
            for ts in range(n // 128):
                P, bP = p.ps[4 + ts % 2], p.bps[4 + ts % 2]
                p.proj_tm(Wtm, bWtm, 0, 512, HXB, bHXB, ts, P, bP)
                st, bst = stg[si % 4]
                si += 1
                p.evac_copy(st[:, 0:512], P[:, 0:512], [bP], [bst])
                k.dma("pool", mkap(ml_v, (s0 + ts * 128) * 512, [[512, 128], [1, 512]]), st[:, 0:512], b_mlv, bst)
        k.barrier()

    with ExitStack() as e2:
        TM = 1024
        NCH = TM // 64
        bg, bbg = p.sb(e2, "bg", [1, 4], F32)
        mlg, bmlg = p.sb(e2, "mlg", [128, 4], F32)
        rst64, brst = p.sb(e2, "rst64", [1, TM], F32)
        onesf, bonesf = p.sb(e2, "onesf", [1, 128], F32)
        k.dma("sp", bg[:, :], i_bg[:, :], bbg, p.b_in)
        k.dma("sp", mlg[:, :], i_mlg[:, :], bmlg, p.b_in)
        k.dma("sp", rst64[:, :], i_rst64[:, 0:TM], brst, p.b_in)
        k.op("dve", lambda: nc.vector.memset(onesf[:, :], 1.0), writes=[bonesf])
        ipre, bipre = p.sb(e2, "ipre", [1, TM], F32)
        fpre, bfpre = p.sb(e2, "fpre", [1, TM], F32)
        brow, bbrow = p.sb(e2, "brow", [1, TM], F32)
        urow, burow = p.sb(e2, "urow", [1, TM], F32)
        wsrow, bwsrow = p.sb(e2, "wsrow", [1, TM], F32)
        erow, berow = p.sb(e2, "erow", [1, TM], F32)
        umax, bumax = p.sb(e2, "umax", [1, NCH], F32)
        Mst, bMst = p.sb(e2, "Mst", [1, NCH], F32)
        marr, bmarr = p.sb(e2, "marr", [1, NCH + 1], F32)
        wprow, bwprow = p.sb(e2, "wprow", [1, NCH], F32)
        mcar, bmcar = p.sb(e2, "mcar", [1, 1], F32)
        kT, bkT = p.sb(e2, "kT", [128, 2, TM], BF16)
        qT, bqT = p.sb(e2, "qT", [128, 2, TM], BF16)
        KH, bKH = p.sb(e2, "KH", [128, 2, TM], BF16)
        Vp, bVp = p.sb(e2, "Vp", [64, NCH, 640], BF16)
        WSbc, bWSbc = p.sb(e2, "WSbc", [128, TM], F32)
        Ebc, bEbc = p.sb(e2, "Ebc", [128, TM], F32)
        Wp, bWp = p.sb(e2, "Wp", [128, NCH], F32)
        O, bO = p.sb(e2, "O", [128, 4, TM], F32)
        Ofw, bOfw = p.sb(e2, "Ofw", [128, 4, TM], F32)
        OG, bOG = p.sb(e2, "OG", [128, 4, TM], BF16)
        SQ, bSQ = p.sb(e2, "SQ", [128, 4, TM], BF16)
        OUTb, bOUTb = p.sb(e2, "OUTb", [128, 4, TM], BF16)
        R, bR = p.sb(e2, "R", [128, 512], F32)
        Cst, bCst = p.sb(e2, "Cst", [128, 2, 640], F32)
        Cbf, bCbf = p.sb(e2, "Cbf", [128, 2, 640], BF16)
        Kt, bKt = p.sb(e2, "Kt", [64, 256], BF16)
        A, bA = p.sb(e2, "A", [64, 64], BF16)
        DN, bDN = p.sb(e2, "DN", [128, 64], F32)
        TH, bTH = p.sb(e2, "TH", [128, 4, 64], F32)
        k.op("dve", lambda: nc.vector.memset(Vp[:, :, 512:640], 1.0), writes=[bVp])
        sbs = [(0, 256)] + [(256 + i * TM, min(TM, SEQ - i * TM)) for i in range((SEQ + TM - 1) // TM)]
        KTp = p.ps[0][:, :].bitcast(BF16)
        bKTp = p.bps[0]
        for d in range(2):
            k.op("dve", lambda: nc.vector.memset(Cst[:, :, :], 0.0), writes=[bCst])
            k.op("dve", lambda: nc.vector.memset(mcar[:, :], 0.0), writes=[bmcar])
            order = sbs if d == 0 else [sbs[0]] + sbs[:0:-1]
            for (s0, T) in order:
                nch = T // 64
                k.dma("sp", ipre[:, 0:T], mkap(ml_g, (2 * d) * S + s0, [[S, 1], [1, T]]), bipre, b_mlg_)
                k.dma("sp", fpre[:, 0:T], mkap(ml_g, (2 * d + 1) * S + s0, [[S, 1], [1, T]]), bfpre, b_mlg_)
                k.dma("sp", kT[:, :, 0:T], mkap(ml_k, s0, [[S, 128], [128 * S, 2], [1, T]]), bkT, b_mlk)
                k.dma("sp", qT[:, :, 0:T], mkap(ml_q, s0, [[S, 128], [128 * S, 2], [1, T]]), bqT, b_mlq)
                k.dma("sp", Vp[:, 0:nch, 0:512], mkap(ml_v, s0 * 512, [[512, 64], [64 * 512, nch], [1, 512]]), bVp, b_mlv)
                if d == 1:
                    k.dma("sp", Ofw[:, :, 0:T], mkap(ml_h, s0, [[S, 128], [128 * S, 4], [1, T]]), bOfw, b_mlh)
                    k.dma("sp", OG[:, :, 0:T], mkap(ml_o, s0, [[S, 128], [128 * S, 4], [1, T]]), bOG, b_mlo)
                k.op("act", lambda: nc.scalar.activation(out=fpre[:, 0:T], in_=fpre[:, 0:T], func=AF.Sigmoid, bias=bg[0:1, 2 * d + 1:2 * d + 2], scale=1.0),
                     reads=[bfpre, bbg], writes=[bfpre])
                k.op("act", lambda: nc.scalar.activation(out=fpre[:, 0:T], in_=fpre[:, 0:T], func=AF.Ln), reads=[bfpre], writes=[bfpre])
                k.op("dve", lambda: nc.vector.tensor_tensor_scan(out=brow[:, 0:T], data0=rst64[:, 0:T], data1=fpre[:, 0:T], initial=0.0, op0=ALU.mult, op1=ALU.add),
                     reads=[brst, bfpre], writes=[bbrow])
                b3 = brow[:, 0:T].rearrange("p (c s) -> p c s", s=64)
                if d == 1:
                    k.op("dve", lambda: nc.vector.tensor_tensor(out=urow[:, 0:T], in0=fpre[:, 0:T], in1=brow[:, 0:T], op=ALU.subtract), reads=[bfpre, bbrow], writes=[burow])
                    u3 = urow[:, 0:T].rearrange("p (c s) -> p c s", s=64)
                    k.op("dve", lambda: nc.vector.tensor_tensor(out=erow[:, 0:T].rearrange("p (c s) -> p c s", s=64), in0=u3, in1=b3[:, :, 63:64].broadcast_to([1, nch, 64]), op=ALU.add),
                         reads=[burow, bbrow], writes=[berow])
                    k.op("dve", lambda: nc.vector.tensor_copy(out=brow[:, 0:T], in_=erow[:, 0:T]), reads=[berow], writes=[bbrow])
                    Bview = b3[:, :, 0]
                else:
                    Bview = b3[:, :, 63]
                k.op("dve", lambda: nc.vector.scalar_tensor_tensor(out=urow[:, 0:T], in0=ipre[:, 0:T], scalar=bg[0:1, 2 * d:2 * d + 1], in1=brow[:, 0:T], op0=ALU.add, op1=ALU.subtract),
                     reads=[bipre, bbg, bbrow], writes=[burow])
                u3 = urow[:, 0:T].rearrange("p (c s) -> p c s", s=64)
                k.op("dve", lambda: nc.vector.tensor_reduce(out=umax[:, 0:nch], in_=u3, axis=AX.X, op=ALU.max), reads=[burow], writes=[bumax])
                if d == 0:
                    k.op("dve", lambda: nc.vector.tensor_copy(out=marr[:, 0:1], in_=mcar[:, :]), reads=[bmcar], writes=[bmarr])
                    crange = list(range(nch))
                else:
                    k.op("dve", lambda: nc.vector.tensor_copy(out=marr[:, nch:nch + 1], in_=mcar[:, :]), reads=[bmcar], writes=[bmarr])
                    crange = list(range(nch - 1, -1, -1))
                for c in crange:
                    mi, mo = (c, c + 1) if d == 0 else (c + 1, c)
                    k.op("dve", lambda: nc.vector.tensor_tensor(out=Mst[:, c:c + 1], in0=marr[:, mi:mi + 1], in1=umax[:, c:c + 1], op=ALU.max), reads=[bmarr, bumax], writes=[bMst])
                    k.op("dve", lambda: nc.vector.tensor_tensor(out=marr[:, mo:mo + 1], in0=Mst[:, c:c + 1], in1=Bview[:, c:c + 1], op=ALU.add), reads=[bMst, bbrow], writes=[bmarr])
                mlast = nch if d == 0 else 0
                k.op("dve", lambda: nc.vector.tensor_copy(out=mcar[:, :], in_=marr[:, mlast:mlast + 1]), reads=[bmarr], writes=[bmcar])
                mb0 = 0 if d == 0 else 1
                k.op("dve", lambda: nc.vector.tensor_tensor(out=wprow[:, 0:nch], in0=marr[:, mb0:mb0 + nch], in1=Mst[:, 0:nch], op=ALU.subtract), reads=[bmarr, bMst], writes=[bwprow])
                k.op("act", lambda: nc.scalar.activation(out=wprow[:, 0:nch], in_=wprow[:, 0:nch], func=AF.Exp), reads=[bwprow], writes=[bwprow])
                Mb = Mst[:, 0:nch].unsqueeze(2).broadcast_to([1, nch, 64])
                k.op("dve", lambda: nc.vector.tensor_tensor(out=wsrow[:, 0:T].rearrange("p (c s) -> p c s", s=64), in0=u3, in1=Mb, op=ALU.subtract), reads=[burow, bMst], writes=[bwsrow])
                k.op("act", lambda: nc.scalar.activation(out=wsrow[:, 0:T], in_=wsrow[:, 0:T], func=AF.Exp), reads=[bwsrow], writes=[bwsrow])
                k.op("dve", lambda: nc.vector.tensor_tensor(out=erow[:, 0:T].rearrange("p (c s) -> p c s", s=64), in0=b3, in1=Mb, op=ALU.add), reads=[bbrow, bMst], writes=[berow])
                k.op("act", lambda: nc.scalar.activation(out=erow[:, 0:T], in_=erow[:, 0:T], func=AF.Exp, scale=-1.0), reads=[berow], writes=[berow])
                BP, bBP = p.ps[6], p.bps[6]
                for (row, brw, dst, bdst) in ((wsrow, bwsrow, WSbc, bWSbc), (erow, berow, Ebc, bEbc)):
                    for c0 in range(0, T, 512):
                        cn = min(512, T - c0)
                        k.op("pe", lambda: nc.tensor.matmul(BP[:, 0:cn], lhsT=onesf[0:1, :], rhs=row[0:1, c0:c0 + cn], start=True, stop=True), reads=[bonesf, brw], writes=[bBP])
                        p.evac_copy(dst[:, c0:c0 + cn], BP[:, 0:cn], [bBP], [bdst])
                k.op("pe", lambda: nc.tensor.matmul(BP[:, 0:nch], lhsT=onesf[0:1, :], rhs=wprow[0:1, 0:nch], start=True, stop=True), reads=[bonesf, bwprow], writes=[bBP])
                p.evac_copy(Wp[:, 0:nch], BP[:, 0:nch], [bBP], [bWp])
                k.op("dve", lambda: nc.vector.tensor_tensor(out=KH[:, :, 0:T], in0=kT[:, :, 0:T], in1=WSbc[:, 0:T].unsqueeze(1).broadcast_to([128, 2, T]), op=ALU.mult),
                     reads=[bkT, bWSbc], writes=[bKH])
                moff = 64 if d == 0 else 128
                for c in crange:
                    cs = slice(c * 64, (c + 1) * 64)
                    for dc in range(2):
                        k.op("pe", lambda dc=dc: nc.tensor.transpose(out=KTp[0:64, dc * 128:(dc + 1) * 128], in_=KH[:, dc, cs], identity=p.ident[:, :]),
                             reads=[bKH, p.b_ident], writes=[bKTp], inc=(dc == 1))
                    k.op("act", lambda: nc.scalar.copy(out=Kt[:, :], in_=KTp[0:64, 0:256]), reads=[bKTp], writes=[bKt])
                    ST, bST = p.ps[1], p.bps[1]
                    for dc in range(2):
                        k.op("pe", lambda dc=dc: nc.tensor.matmul(ST[0:64, 0:64], lhsT=KH[:, dc, cs], rhs=qT[:, dc, cs], start=(dc == 0), stop=(dc == 1)),
                             reads=[bKH, bqT], writes=[bST], inc=(dc == 1))
                    k.op("dve", lambda: nc.vector.tensor_tensor(out=A[:, :], in0=ST[0:64, 0:64], in1=p.tri[0:64, moff:moff + 64], op=ALU.mult), reads=[bST, p.b_tri], writes=[bA])
                    k.op("act", lambda: nc.scalar.activation(out=Cbf[:, :, :], in_=Cst[:, :, :], func=AF.Copy, scale=Wp[:, c:c + 1]), reads=[bCst, bWp], writes=[bCbf])
                    NP, bNP = p.ps[2], p.bps[2]
                    for vj in range(5):
                        k.op("pe", lambda vj=vj: nc.tensor.matmul(NP[:, vj * 64:(vj + 1) * 64], lhsT=Vp[:, c, vj * 128:(vj + 1) * 128], rhs=A[:, :], start=True, stop=False),
                             reads=[bVp, bA], writes=[bNP], inc=False)
                        k.op("pe", lambda vj=vj: nc.tensor.matmul(NP[:, vj * 64:(vj + 1) * 64], lhsT=Cbf[:, 0, vj * 128:(vj + 1) * 128], rhs=qT[:, 0, cs], start=False, stop=False),
                             reads=[bCbf, bqT], writes=[bNP], inc=False)
                        k.op("pe", lambda vj=vj: nc.tensor.matmul(NP[:, vj * 64:(vj + 1) * 64], lhsT=Cbf[:, 1, vj * 128:(vj + 1) * 128], rhs=qT[:, 1, cs], start=False, stop=True),
                             reads=[bCbf, bqT], writes=[bNP], inc=(vj == 4))
                    k.op("act", lambda: nc.scalar.activation(out=DN[:, :], in_=NP[:, 256:320], func=AF.Abs), reads=[bNP], writes=[bDN])
                    k.op("dve", lambda: nc.vector.tensor_tensor(out=DN[:, :], in0=DN[:, :], in1=Ebc[:, cs], op=ALU.max), reads=[bDN, bEbc], writes=[bDN])
                    k.op("dve", lambda: nc.vector.reciprocal(out=DN[:, :], in_=DN[:, :]), reads=[bDN], writes=[bDN])
                    np3 = NP[:, 0:256].rearrange("p (v t) -> p v t", t=64)
                    dnb = DN[:, :].unsqueeze(1).broadcast_to([128, 4, 64])
                    if d == 0:
                        k.op("dve", lambda: nc.vector.tensor_tensor(out=O[:, :, cs], in0=np3, in1=dnb, op=ALU.mult), reads=[bNP, bDN], writes=[bO])
                    else:
                        k.op("dve", lambda: nc.vector.tensor_tensor(out=TH[:, :, :], in0=np3, in1=dnb, op=ALU.mult), reads=[bNP, bDN], writes=[bTH])
                        k.op("pool", lambda: nc.gpsimd.tensor_tensor(out=O[:, :, cs], in0=TH[:, :, :], in1=Ofw[:, :, cs], op=ALU.add), reads=[bTH, bOfw], writes=[bO])
                    UV0, bUV0 = p.ps[3], p.bps[3]
                    UV1, bUV1 = p.ps[4], p.bps[4]
                    UN, bUN = p.ps[5], p.bps[5]
                    for dc, (UV, bUV) in enumerate(((UV0, bUV0), (UV1, bUV1))):
                        k.op("pe", lambda dc=dc, UV=UV: nc.tensor.matmul(UV[:, 0:512], lhsT=Kt[:, dc * 128:(dc + 1) * 128], rhs=Vp[:, c, 0:512], start=True, stop=True),
                             reads=[bKt, bVp], writes=[bUV])
                        k.op("pe", lambda dc=dc: nc.tensor.matmul(UN[:, dc * 128:(dc + 1) * 128], lhsT=Kt[:, dc * 128:(dc + 1) * 128], rhs=Vp[:, c, 512:640], start=True, stop=True),
                             reads=[bKt, bVp], writes=[bUN], inc=(dc == 1))
                    for dc, (UV, bUV) in enumerate(((UV0, bUV0), (UV1, bUV1))):
                        k.op("dve", lambda dc=dc, UV=UV: nc.vector.scalar_tensor_tensor(out=Cst[:, dc, 0:512], in0=Cst[:, dc, 0:512], scalar=Wp[:, c:c + 1], in1=UV[:, 0:512],
                                                                                       op0=ALU.mult, op1=ALU.add),
                             reads=[bCst, bWp, bUV], writes=[bCst])
                    k.op("dve", lambda: nc.vector.scalar_tensor_tensor(out=Cst[:, :, 512:640], in0=Cst[:, :, 512:640], scalar=Wp[:, c:c + 1],
                                                                       in1=UN[:, 0:256].rearrange("p (a b) -> p a b", b=128), op0=ALU.mult, op1=ALU.add),
                         reads=[bCst, bWp, bUN], writes=[bCst])
                if d == 0:
                    k.dma("pool", mkap(ml_h, s0, [[S, 128], [128 * S, 4], [1, T]]), O[:, :, 0:T], b_mlh, bO)
                else:
                    p.head_post(O, bO, 4, T, SQ, bSQ, R, bR, OG, bOG, lambda vj: mlg[:, vj:vj + 1], bmlg, OUTb, bOUTb, o_m, b_om, 0, s0)
        k.barrier()
    return p.finish()


def rope_tables(ROWS):
    n_freq = 64
    inv = (10000.0 ** (-np.arange(n_freq, dtype=np.float32) / n_freq)).astype(np.float32)
    sign = np.concatenate([-np.ones(64, np.float32), np.ones(64, np.float32)])
    f2 = np.concatenate([inv, inv])
    rows = np.arange(ROWS, dtype=np.float32)
    cols = np.arange(64, dtype=np.float32)
    angR = (rows[None, :] * f2[:, None]).astype(np.float32)
    angC = (cols[None, :] * f2[:, None]).astype(np.float32)
    ropeR = np.concatenate([np.cos(angR), sign[:, None] * np.sin(angR)], axis=1).astype(np.float32)
    ropeC = np.concatenate([np.cos(angC), sign[:, None] * np.sin(angC)], axis=1).astype(np.float32)
    return ropeR, ropeC


def prep_odd_win(w, h):
    perm = np.concatenate([np.arange(64, 128), np.arange(0, 64), np.arange(192, 256), np.arange(128, 192)])
    q0, k0, v0, o0, g0 = h * 256, 2048 + h * 256, 4096 + h * 512, 8192 + h * 512, 12288
    cols = np.concatenate([q0 + np.arange(256), q0 + perm, k0 + np.arange(256), k0 + perm, o0 + np.arange(512), v0 + np.arange(512)])
    wg = np.ascontiguousarray(w[:, [g0 + h, g0 + 8 + h, g0 + 16 + h, g0 + 24 + h]])
    return np.ascontiguousarray(w[:, cols]), wg


ROWS_FULL = 256
SEQ_FULL = ROWS_FULL * 64
CTX_LEN = 256
S_FULL = SEQ_FULL + CTX_LEN
NCT = 8
FF_FULL = 11008
DEPTH = 4
_PROGS = {}


def _prog(name, fn):
    if name not in _PROGS:
        _PROGS[name] = fn()
    return _PROGS[name]


def _run(nc, in_maps):
    res = run_bass_kernel_spmd(nc, in_maps, core_ids=list(range(len(in_maps))))
    return res.results


def kernel(x, c, ctx, c_ctx, norm_mix_g, norm_ffn_g, mod_w_a, mod_w_b, mod_b,
           even_w_in, even_w_out, na_rpb, hg_lb_logits, hg_norm_g,
           odd_w_in, odd_b_gates, odd_w_out, ml_norm_g,
           ffn_w_up, ffn_w_down, final_norm_g):
    f = lambda a: np.asarray(a, dtype=np.float32)
    x, c, ctx, c_ctx = f(x), f(c), f(ctx), f(c_ctx)
    norm_mix_g, norm_ffn_g, mod_w_a, mod_w_b, mod_b = f(norm_mix_g), f(norm_ffn_g), f(mod_w_a), f(mod_w_b), f(mod_b)
    even_w_in, even_w_out, na_rpb, hg_lb_logits, hg_norm_g = f(even_w_in), f(even_w_out), f(na_rpb), f(hg_lb_logits), f(hg_norm_g)
    odd_w_in, odd_b_gates, odd_w_out, ml_norm_g = f(odd_w_in), f(odd_b_gates), f(odd_w_out), f(ml_norm_g)
    ffn_w_up, ffn_w_down, final_norm_g = f(ffn_w_up), f(ffn_w_down), f(final_norm_g)
    SEQ, S, ROWS = SEQ_FULL, S_FULL, ROWS_FULL
    TL, TC = SEQ // NCT, CTX_LEN // NCT
    tri, rst, rst64, ident = const_tables()
    ropeR, ropeC = rope_tables(ROWS)

    cvec = np.concatenate([col_layout(c[0]), col_layout(c_ctx)], axis=1)
    ncM = _prog("M", build_M)
    rM = _run(ncM, [{"wa": mod_w_a[l], "wb": mod_w_b[l], "modb": col_layout(mod_b[l]), "cvec": cvec} for l in range(DEPTH)])
    modT = [rM[l]["modT"] for l in range(DEPTH)]

    xT = np.ascontiguousarray(x[0].T)
    cT = np.ascontiguousarray(ctx[0].T)
    xloc = [np.ascontiguousarray(np.concatenate([xT[:, t * TL:(t + 1) * TL], cT[:, t * TC:(t + 1) * TC]], axis=1)) for t in range(NCT)]
    del xT, cT

    def gather_hx(hs):
        out = np.empty((D, S), NPBF)
        for t in range(NCT):
            out[:, t * TC:(t + 1) * TC] = hs[t][:, TL:TL + TC]
            out[:, CTX_LEN + t * TL:CTX_LEN + (t + 1) * TL] = hs[t][:, 0:TL]
        return out

    ncA = _prog("A", lambda: build_A(TL, TC, False))
    rA = _run(ncA, [{"xT": xloc[t], "modT": modT[0], "ng": col_layout(norm_mix_g[0])} for t in range(NCT)])
    hx_all = gather_hx([rA[t]["hx"] for t in range(NCT)])
    del rA

    for l in range(DEPTH):
        j = l // 2
        if l % 2 == 0:
            ncP = _prog("PE", lambda: build_PE(ROWS))
            ims = []
            for cc in range(8):
                lbl, jsel, hg = even_small_params(hg_lb_logits, j, hg_norm_g, cc)
                ims.append({"hx": hx_all, "win": prep_even_win(even_w_in[j], cc), "lbl": lbl, "jsel": jsel, "hgg": hg,
                            "nab": na_bias_tables(na_rpb[j], cc, ROWS), "rst": rst, "ident": ident, "tri": tri})
            rP = _run(ncP, ims)
            del ims
            M_all = np.empty((D, S), NPBF)
            for cc in range(8):
                M_all[256 * cc:256 * cc + 256] = rP[cc]["mloc"][0:256]
                M_all[2048 + 256 * cc:2048 + 256 * cc + 256] = rP[cc]["mloc"][256:512]
            wout = even_w_out[j]
        else:
            ncP = _prog("PO", lambda: build_PO(ROWS))
            ims = []
            for h in range(8):
                win, wg = prep_odd_win(odd_w_in[j], h)
                ims.append({"hx": hx_all, "win": win, "wg": wg, "bg": np.ascontiguousarray(odd_b_gates[j][[h, 8 + h, 16 + h, 24 + h]][None]),
                            "mlg": col_layout(ml_norm_g[j][h * 512:(h + 1) * 512]), "ropeR": ropeR, "ropeC": ropeC, "rst64": rst64, "ident": ident, "tri": tri})
            rP = _run(ncP, ims)
            del ims
            M_all = np.empty((D, S), NPBF)
            for h in range(8):
                M_all[512 * h:512 * h + 512] = rP[h]["mloc"]
            wout = odd_w_out[j]
        del rP
        last = (l == DEPTH - 1)
        modn = np.zeros_like(modT[0]) if last else modT[l + 1]
        ngn = col_layout(final_norm_g) if last else col_layout(norm_mix_g[l + 1])
        ncT = _prog("T", lambda: build_T(TL, TC, FF_FULL))
        ims = []
        for t in range(NCT):
            mloc = np.ascontiguousarray(np.concatenate([M_all[:, CTX_LEN + t * TL:CTX_LEN + (t + 1) * TL], M_all[:, t * TC:(t + 1) * TC]], axis=1))
            ims.append({"xT": xloc[t], "mT": mloc, "wout": wout, "wup": ffn_w_up[l], "wdn": ffn_w_down[l], "modT": modT[l],
                        "nfg": col_layout(norm_ffn_g[l]), "modTn": modn, "ngn": ngn})
        del M_all
        rT = _run(ncT, ims)
        del ims
        xloc = [rT[t]["xo"] for t in range(NCT)]
        if not last:
            hx_all = gather_hx([rT[t]["hxo"] for t in range(NCT)])
        del rT

    ncF = _prog("F", lambda: build_A(TL, TC, True))
    zero_mod = np.zeros_like(modT[0])
    rF = _run(ncF, [{"xT": xloc[t], "modT": zero_mod, "ng": col_layout(final_norm_g)} for t in range(NCT)])
    out = np.empty((1, SEQ, D), np.float32)
    for t in range(NCT):
        out[0, t * TL:(t + 1) * TL, :] = rF[t]["hx"][:, 0:TL].T
    return out
```

```python
import numpy as np
import ml_dtypes
from contextlib import ExitStack
import concourse.bass as bass
import concourse.mybir as mybir
from concourse.bass_utils import run_bass_kernel_spmd

F32 = mybir.dt.float32
BF16 = mybir.dt.bfloat16
AF = mybir.ActivationFunctionType
ALU = mybir.AluOpType
AX = mybir.AxisListType
NPBF = ml_dtypes.bfloat16

D = 4096
KC = 32
EPS = 1e-6
import os as _os
SKIP_SELF_WAIT = _os.environ.get('KB_NOSELF', '0') == '1'


class Buf:
    __slots__ = ("name", "w", "r", "dkey")

    def __init__(self, name):
        self.name = name
        self.w = {}
        self.r = {}
        self.dkey = None


class KB:
    def __init__(self, nc):
        self.nc = nc
        self.eng = {"pe": nc.tensor, "act": nc.scalar, "dve": nc.vector, "pool": nc.gpsimd, "sp": nc.sync}
        self.semh = {}
        self.total = {}
        for e in ("pe", "act", "dve", "pool"):
            self.semh[e] = nc.alloc_semaphore(name="sem_" + e)
            self.total[e] = 0
        self.waited = {e: {} for e in self.eng}
        self.pending = {e: ([], []) for e in self.eng}
        self.ninstr = 0

    def buf(self, name):
        return Buf(name)

    def _wait(self, e, key, val):
        if key == e and (e == "pe" or SKIP_SELF_WAIT):
            return
        if self.waited[e].get(key, 0) >= val:
            return
        self.eng[e].wait_ge(self.semh[key], val)
        self.waited[e][key] = val
        self.ninstr += 1

    def _deps(self, e, reads, writes):
        for b in reads:
            for k, v in b.w.items():
                self._wait(e, k, v)
        for b in writes:
            for k, v in b.w.items():
                self._wait(e, k, v)
            for k, v in b.r.items():
                self._wait(e, k, v)

    def op(self, e, fn, reads=(), writes=(), inc=True):
        self._deps(e, reads, writes)
        ins = fn()
        self.ninstr += 1
        pr, pw = self.pending[e]
        if not inc:
            pr.extend(reads)
            pw.extend(writes)
            return ins
        self.total[e] += 1
        ins.then_inc(self.semh[e], 1)
        v = self.total[e]
        for b in list(reads) + pr:
            b.r[e] = v
        for b in list(writes) + pw:
            b.w[e] = v
            b.r = {}
        pr.clear()
        pw.clear()
        return ins

    def dma(self, q, out, in_, ob, ib, **kw):
        if ob.dkey is None:
            ob.dkey = "d_" + ob.name
            self.semh[ob.dkey] = self.nc.alloc_semaphore(name=ob.dkey)
            self.total[ob.dkey] = 0
        self._deps(q, [ib], [ob])
        self.eng[q].dma_start(out=out, in_=in_, **kw).then_inc(self.semh[ob.dkey], 16)
        self.ninstr += 1
        self.total[ob.dkey] += 16
        v = self.total[ob.dkey]
        ib.r[ob.dkey] = v
        ob.w[ob.dkey] = v
        ob.r = {}

    def barrier(self):
        for e in self.eng:
            for k, v in self.total.items():
                if v > 0:
                    self._wait(e, k, v)


def mkap(h, off, pat):
    return bass.AP(h, off, [list(p) for p in pat])


class Base:
    def __init__(self):
        nc = bass.Bass("TRN2", target_bir_lowering=False)
        self.nc = nc
        self.k = KB(nc)
        self.b_in = self.k.buf("ext_in")
        self.es = ExitStack()
        self.ps = []
        self.bps = []
        for i in range(8):
            t = self.es.enter_context(nc.psum_tensor(f"ps{i}", [128, 512], F32))
            self.ps.append(t)
            self.bps.append(self.k.buf(f"ps{i}"))
        self.nbuf = 0

    def din(self, name, shape, dt=F32):
        return self.nc.dram_tensor(name, list(shape), dt, kind="ExternalInput")

    def dout(self, name, shape, dt=F32):
        return self.nc.dram_tensor(name, list(shape), dt, kind="ExternalOutput")

    def dint(self, name, shape, dt):
        return self.nc.dram_tensor(name, list(shape), dt), self.k.buf(name)

    def sb(self, es, name, shape, dt):
        self.nbuf += 1
        name = f"s{self.nbuf}_{name}"
        t = es.enter_context(self.nc.sbuf_tensor(name, list(shape), dt))
        return t, self.k.buf(name)

    def consts(self):
        nc, k = self.nc, self.k
        self.ones, self.b_ones = self.sb(self.es, "ones", [128, 128], BF16)
        k.op("dve", lambda: nc.vector.memset(self.ones[:, :], 1.0), writes=[self.b_ones])

    def finish(self):
        self.k.barrier()
        self.es.close()
        return self.nc

    def cast_tiles(self, src_h, K, N, dst_h, b_dst, es):
        nc, k = self.nc, self.k
        nk = K // 128
        CW = 512 if N % 512 == 0 else 128
        G = 8
        ncc = CW // 128
        if not hasattr(self, "_cst"):
            self._cst = [self.sb(es, f"cst{i}", [128, G, 512], F32) for i in range(3)]
            self._csb = [self.sb(es, f"csb{i}", [128, 4, G, 128], BF16) for i in range(3)]
            self._cit = 0
        for kc0 in range(0, nk, G):
            gn = min(G, nk - kc0)
            for c0 in range(0, N, CW):
                it = self._cit
                self._cit += 1
                a, ba = self._cst[it % 3]
                b, bb = self._csb[it % 3]
                k.dma("sp", a[:, 0:gn, 0:CW], mkap(src_h, kc0 * 128 * N + c0, [[N, 128], [128 * N, gn], [1, CW]]), ba, self.b_in)
                src = a[:, 0:gn, 0:CW].rearrange("p g (a c) -> p g a c", c=128)
                dstv = b[:, 0:ncc, 0:gn, :].rearrange("p a g c -> p g a c")
                e = ("act", "dve", "pool")[it % 3]
                if e == "act":
                    k.op("act", lambda: nc.scalar.copy(out=dstv, in_=src), reads=[ba], writes=[bb])
                elif e == "dve":
                    k.op("dve", lambda: nc.vector.tensor_copy(out=dstv, in_=src), reads=[ba], writes=[bb])
                else:
                    k.op("pool", lambda: nc.gpsimd.tensor_copy(out=dstv, in_=src), reads=[ba], writes=[bb])
                cc0 = c0 // 128
                dst = mkap(dst_h, cc0 * 128 * nk * 128 + kc0 * 128, [[nk * 128, 128], [128 * nk * 128, ncc], [1, gn * 128]])
                k.dma("pool", dst, b[:, 0:ncc, 0:gn, :].rearrange("p a g c -> p a (g c)"), b_dst, bb)

    def wtile_view(self, dst_h, nk, cc, k0=0, kn=None):
        kn = nk if kn is None else kn
        return mkap(dst_h, cc * 128 * nk * 128 + k0 * 128, [[nk * 128, 128], [128, kn], [1, 128]])

    def normmod(self, X, bX, SQ, bSQ, OUT, bOUT, R, bR, tmpk, n, gs_ap, sh_ap, bmods):
        nc, k = self.nc, self.k
        P, bP = self.ps[7], self.bps[7]
        k.op("act", lambda: nc.scalar.activation(out=SQ[:, :, 0:n], in_=X[:, :, 0:n], func=AF.Square), reads=[bX], writes=[bSQ])
        for kc in range(KC):
            k.op("pe", lambda kc=kc: nc.tensor.matmul(P[:, 0:n], lhsT=self.ones[:, :], rhs=SQ[:, kc, 0:n], start=(kc == 0), stop=(kc == KC - 1)),
                 reads=[bSQ, self.b_ones], writes=[bP], inc=(kc == KC - 1))
        k.op("dve", lambda: nc.vector.tensor_scalar(out=R[:, 0:n], in0=P[:, 0:n], scalar1=1.0 / D, scalar2=EPS, op0=ALU.mult, op1=ALU.add),
             reads=[bP], writes=[bR])
        k.op("act", lambda: nc.scalar.activation(out=R[:, 0:n], in_=R[:, 0:n], func=AF.Sqrt), reads=[bR], writes=[bR])
        k.op("dve", lambda: nc.vector.reciprocal(out=R[:, 0:n], in_=R[:, 0:n]), reads=[bR], writes=[bR])
        for kc in range(KC):
            tk, btk = tmpk[kc % len(tmpk)]
            k.op("dve", lambda kc=kc, tk=tk: nc.vector.tensor_tensor(out=tk[:, 0:n], in0=X[:, kc, 0:n], in1=R[:, 0:n], op=ALU.mult),
                 reads=[bX, bR], writes=[btk])
            k.op("act", lambda kc=kc, tk=tk: nc.scalar.activation(out=OUT[:, kc, 0:n], in_=tk[:, 0:n], func=AF.Identity,
                                                                  scale=gs_ap(kc), bias=sh_ap(kc)),
                 reads=[btk] + bmods, writes=[bOUT])

    def make_gs(self, gs, bgs, modT, bmod, sc0, ng, bng, tmp, btmp):
        nc, k = self.nc, self.k
        k.op("dve", lambda: nc.vector.tensor_scalar(out=tmp[:, :, :], in0=modT[:, sc0:sc0 + 32, :], scalar1=1.0, scalar2=None, op0=ALU.add),
             reads=[bmod], writes=[btmp])
        k.op("dve", lambda: nc.vector.tensor_tensor(out=gs[:, :, :], in0=tmp[:, :, :], in1=ng[:, 0:32].unsqueeze(2).broadcast_to([128, 32, 2]), op=ALU.mult),
             reads=[btmp, bng], writes=[bgs])


def tblocks(TL, TC, NB=256):
    blocks = []
    for t0 in range(0, TL, NB):
        blocks.append((t0, min(NB, TL - t0), 0))
    blocks.append((TL, TC, 1))
    return blocks


def fm_view(h, ncols_total, t0, n):
    return mkap(h, t0, [[ncols_total, 128], [128 * ncols_total, KC], [1, n]])


def build_M():
    p = Base()
    nc, k = p.nc, p.k
    i_wa = p.din("wa", [D, 256])
    i_wb = p.din("wb", [256, 6 * D])
    i_modb = p.din("modb", [128, 192])
    i_cvec = p.din("cvec", [128, 64])
    o_mod = p.dout("modT", [128, 384])
    b_out = k.buf("o_mod")
    es = p.es
    wa, b_wa = p.sb(es, "wa", [128, 32, 256], F32)
    wb, b_wb = p.sb(es, "wb", [128, 2, 6144], F32)
    cv, b_cv = p.sb(es, "cv", [128, 64], F32)
    cT, b_cT = p.sb(es, "cT", [128, 32, 2], F32)
    hT, b_hT = p.sb(es, "hT", [128, 2, 2], F32)
    mb, b_mb = p.sb(es, "mb", [128, 192], F32)
    mo, b_mo = p.sb(es, "mo", [128, 192, 2], F32)
    k.dma("sp", cv[:, :], i_cvec[:, :], b_cv, p.b_in)
    k.dma("sp", mb[:, :], i_modb[:, :], b_mb, p.b_in)
    k.dma("sp", wa[:, :, :], mkap(i_wa, 0, [[256, 128], [128 * 256, 32], [1, 256]]), b_wa, p.b_in)
    for c in range(2):
        k.op("act", lambda c=c: nc.scalar.activation(out=cT[:, :, c], in_=cv[:, c * 32:(c + 1) * 32], func=AF.Silu), reads=[b_cv], writes=[b_cT])
    P0, bP0 = p.ps[0], p.bps[0]
    P1, bP1 = p.ps[1], p.bps[1]
    for r in range(2):
        for kc in range(32):
            k.op("pe", lambda r=r, kc=kc: nc.tensor.matmul(P0[:, r * 2:(r + 1) * 2], lhsT=wa[:, kc, r * 128:(r + 1) * 128], rhs=cT[:, kc, :],
                                                           start=(kc == 0), stop=(kc == 31)),
                 reads=[b_wa, b_cT], writes=[bP0], inc=(kc == 31))
    k.op("dve", lambda: nc.vector.tensor_copy(out=hT[:, :, :], in_=P0[:, 0:4].rearrange("p (r c) -> p r c", c=2)), reads=[bP0], writes=[b_hT])
    for piece in range(4):
        k.dma("sp", wb[:, :, :], mkap(i_wb, piece * 6144, [[6 * D, 128], [128 * 6 * D, 2], [1, 6144]]), b_wb, p.b_in)
        for jj in range(48):
            jg = piece * 48 + jj
            for r in range(2):
                k.op("pe", lambda r=r, jj=jj, jg=jg: nc.tensor.matmul(P1[:, jg * 2:(jg + 1) * 2], lhsT=wb[:, r, jj * 128:(jj + 1) * 128], rhs=hT[:, r, :],
                                                                      start=(r == 0), stop=(r == 1)),
                     reads=[b_wb, b_hT], writes=[bP1], inc=(r == 1 and jj == 47))
    k.op("dve", lambda: nc.vector.tensor_tensor(out=mo[:, :, :], in0=P1[:, 0:384].rearrange("p (j c) -> p j c", c=2),
                                                in1=mb[:, :].unsqueeze(2).broadcast_to([128, 192, 2]), op=ALU.add),
         reads=[bP1, b_mb], writes=[b_mo])
    k.dma("pool", o_mod[:, :], mo[:, :, :].rearrange("p j c -> p (j c)"), b_out, b_mo)
    return p.finish()


def build_A(TL, TC, out_f32):
    p = Base()
    nc, k = p.nc, p.k
    TLOC = TL + TC
    i_x = p.din("xT", [D, TLOC])
    i_mod = p.din("modT", [128, 384])
    i_ng = p.din("ng", [128, 32])
    odt = F32 if out_f32 else BF16
    o_hx = p.dout("hx", [D, TLOC], odt)
    b_out = k.buf("o_hx")
    es = p.es
    p.consts()
    modT, b_mod = p.sb(es, "modT", [128, 192, 2], F32)
    ng, b_ng = p.sb(es, "ng", [128, 32], F32)
    gs, b_gs = p.sb(es, "gs", [128, 32, 2], F32)
    tmp, b_tmp = p.sb(es, "tmp", [128, 32, 2], F32)
    k.dma("sp", modT[:, :, :].rearrange("p j c -> p (j c)"), i_mod[:, :], b_mod, p.b_in)
    k.dma("sp", ng[:, :], i_ng[:, :], b_ng, p.b_in)
    p.make_gs(gs, b_gs, modT, b_mod, 32, ng, b_ng, tmp, b_tmp)
    Xs = [p.sb(es, f"X{i}", [128, 32, 256], F32) for i in range(2)]
    SQ, bSQ = p.sb(es, "SQ", [128, 32, 256], BF16)
    OUTs = [p.sb(es, f"O{i}", [128, 32, 256], odt) for i in range(2)]
    R, bR = p.sb(es, "R", [128, 256], F32)
    tmpk = [p.sb(es, f"tk{i}", [128, 256], F32) for i in range(3)]
    for bi, (t0, n, cond) in enumerate(tblocks(TL, TC)):
        X, bX = Xs[bi % 2]
        O, bO = OUTs[bi % 2]
        k.dma("sp", X[:, :, 0:n], fm_view(i_x, TLOC, t0, n), bX, p.b_in)
        p.normmod(X, bX, SQ, bSQ, O, bO, R, bR, tmpk, n,
                  lambda kc, cond=cond: gs[:, kc, cond:cond + 1], lambda kc, cond=cond: modT[:, kc, cond:cond + 1], [b_gs, b_mod])
        k.dma("pool", fm_view(o_hx, TLOC, t0, n), O[:, :, 0:n], b_out, bO)
    return p.finish()


def build_T(TL, TC, FF):
    p = Base()
    nc, k = p.nc, p.k
    TLOC = TL + TC
    HC = FF // 128
    i_x = p.din("xT", [D, TLOC])
    i_m = p.din("mT", [D, TLOC], BF16)
    i_wout = p.din("wout", [D, D])
    i_wup = p.din("wup", [D, 2 * FF])
    i_wdn = p.din("wdn", [FF, D])
    i_mod = p.din("modT", [128, 384])
    i_nfg = p.din("nfg", [128, 32])
    i_modn = p.din("modTn", [128, 384])
    i_ngn = p.din("ngn", [128, 32])
    o_x = p.dout("xo", [D, TLOC], F32)
    o_hx = p.dout("hxo", [D, TLOC], BF16)
    b_ox = k.buf("o_x")
    b_ohx = k.buf("o_hx")
    es = p.es
    p.consts()
    woutb, b_woutb = p.dint("woutb", [32 * 128 * 32 * 128], BF16)
    wupb, b_wupb = p.dint("wupb", [(2 * FF // 128) * 128 * 32 * 128], BF16)
    wdnb, b_wdnb = p.dint("wdnb", [32 * 128 * HC * 128], BF16)
    with ExitStack() as es1:
        p.cast_tiles(i_wout, D, D, woutb, b_woutb, es1)
        p.cast_tiles(i_wup, D, 2 * FF, wupb, b_wupb, es1)
        p.cast_tiles(i_wdn, FF, D, wdnb, b_wdnb, es1)
        k.barrier()
    del p._cst
    modT, b_mod = p.sb(es, "modT", [128, 192, 2], F32)
    modn, b_modn = p.sb(es, "modn", [128, 192, 2], F32)
    nfg, b_nfg = p.sb(es, "nfg", [128, 32], F32)
    ngn, b_ngn = p.sb(es, "ngn", [128, 32], F32)
    gs2, b_gs2 = p.sb(es, "gs2", [128, 32, 2], F32)
    gsn, b_gsn = p.sb(es, "gsn", [128, 32, 2], F32)
    tmp, b_tmp = p.sb(es, "tmp", [128, 32, 2], F32)
    k.dma("sp", modT[:, :, :].rearrange("p j c -> p (j c)"), i_mod[:, :], b_mod, p.b_in)
    k.dma("sp", modn[:, :, :].rearrange("p j c -> p (j c)"), i_modn[:, :], b_modn, p.b_in)
    k.dma("sp", nfg[:, :], i_nfg[:, :], b_nfg, p.b_in)
    k.dma("sp", ngn[:, :], i_ngn[:, :], b_ngn, p.b_in)
    p.make_gs(gs2, b_gs2, modT, b_mod, 128, nfg, b_nfg, tmp, b_tmp)
    p.make_gs(gsn, b_gsn, modn, b_modn, 32, ngn, b_ngn, tmp, b_tmp)
    NB = 416
    hh = (HC + 1) // 2
    X, bX = p.sb(es, "X", [128, 32, NB], F32)
    HX, bHX = p.sb(es, "HX", [128, 32, NB], BF16)
    Fh, bF = p.sb(es, "F", [128, max(hh, 32), NB], BF16)
    MT, bMT = Fh, bF
    R, bR = p.sb(es, "R", [128, NB], F32)
    tmpk = [p.sb(es, f"tk{i}", [128, NB], F32) for i in range(3)]
    W8 = [p.sb(es, f"W8_{i}", [128, 32, 128], BF16) for i in range(4)]
    WD = [p.sb(es, f"WD_{i}", [128, hh, 128], BF16) for i in range(2)]
    SG, bSG = p.sb(es, "SG", [128, NB], F32)
    cnt = {"w8": 0, "wd": 0}

    def load_w8(h, bh, cc):
        W, bW = W8[cnt["w8"] % len(W8)]
        cnt["w8"] += 1
        k.dma("sp", W[:, :, :], p.wtile_view(h, 32, cc), bW, bh)
        return W, bW

    blocks = []
    t0 = 0
    while t0 < TLOC:
        n = min(NB, TLOC - t0)
        segs = []
        if t0 < TL:
            segs.append((0, min(n, TL - t0), 0))
        if t0 + n > TL:
            c0 = max(TL - t0, 0)
            segs.append((c0, n - c0, 1))
        blocks.append((t0, n, segs))
        t0 += n

    def resid(P, bP, fo, n, segs, gbase):
        for (c0, cn, cond) in segs:
            k.op("dve", lambda: nc.vector.scalar_tensor_tensor(out=X[:, fo, c0:c0 + cn], in0=P[:, c0:c0 + cn], scalar=modT[:, gbase + fo, cond:cond + 1],
                                                               in1=X[:, fo, c0:c0 + cn], op0=ALU.mult, op1=ALU.add),
                 reads=[bP, b_mod, bX], writes=[bX])

    def normmod_seg(n, segs, gs, bgs, md, bmd, sh0):
        P, bP = p.ps[7], p.bps[7]
        k.op("act", lambda: nc.scalar.activation(out=HX[:, :, 0:n], in_=X[:, :, 0:n], func=AF.Square), reads=[bX], writes=[bHX])
        for kc in range(KC):
            k.op("pe", lambda: nc.tensor.matmul(P[:, 0:n], lhsT=p.ones[:, :], rhs=HX[:, kc, 0:n], start=(kc == 0), stop=(kc == KC - 1)),
                 reads=[bHX, p.b_ones], writes=[bP], inc=(kc == KC - 1))
        k.op("dve", lambda: nc.vector.tensor_scalar(out=R[:, 0:n], in0=P[:, 0:n], scalar1=1.0 / D, scalar2=EPS, op0=ALU.mult, op1=ALU.add), reads=[bP], writes=[bR])
        k.op("act", lambda: nc.scalar.activation(out=R[:, 0:n], in_=R[:, 0:n], func=AF.Sqrt), reads=[bR], writes=[bR])
        k.op("dve", lambda: nc.vector.reciprocal(out=R[:, 0:n], in_=R[:, 0:n]), reads=[bR], writes=[bR])
        for kc in range(KC):
            tk, btk = tmpk[kc % len(tmpk)]
            k.op("dve", lambda: nc.vector.tensor_tensor(out=tk[:, 0:n], in0=X[:, kc, 0:n], in1=R[:, 0:n], op=ALU.mult), reads=[bX, bR], writes=[btk])
            for (c0, cn, cond) in segs:
                k.op("act", lambda: nc.scalar.activation(out=HX[:, kc, c0:c0 + cn], in_=tk[:, c0:c0 + cn], func=AF.Identity,
                                                         scale=gs[:, kc, cond:cond + 1], bias=md[:, sh0 + kc, cond:cond + 1]),
                     reads=[btk, bgs, bmd], writes=[bHX])

    for bi, (t0, n, segs) in enumerate(blocks):
        k.dma("sp", X[:, :, 0:n], fm_view(i_x, TLOC, t0, n), bX, p.b_in)
        k.dma("sp", MT[:, 0:32, 0:n], fm_view(i_m, TLOC, t0, n), bMT, p.b_in)
        for fo in range(32):
            W, bW = load_w8(woutb, b_woutb, fo)
            P, bP = p.ps[fo % 2], p.bps[fo % 2]
            for kc in range(32):
                k.op("pe", lambda: nc.tensor.matmul(P[:, 0:n], lhsT=W[:, kc, :], rhs=MT[:, kc, 0:n], start=(kc == 0), stop=(kc == 31)),
                     reads=[bW, bMT], writes=[bP], inc=(kc == 31))
            resid(P, bP, fo, n, segs, 64)
        normmod_seg(n, segs, gs2, b_gs2, modT, b_mod, 96)
        for half in range(2):
            h0 = half * hh
            hn = min(hh, HC - h0)
            if hn <= 0:
                continue
            for i in range(hn):
                hc = h0 + i
                Wa, bWa = load_w8(wupb, b_wupb, hc)
                Wg, bWg = load_w8(wupb, b_wupb, HC + hc)
                Pa, bPa = p.ps[2 + (hc % 2) * 2], p.bps[2 + (hc % 2) * 2]
                Pg, bPg = p.ps[3 + (hc % 2) * 2], p.bps[3 + (hc % 2) * 2]
                for kc in range(32):
                    k.op("pe", lambda: nc.tensor.matmul(Pa[:, 0:n], lhsT=Wa[:, kc, :], rhs=HX[:, kc, 0:n], start=(kc == 0), stop=(kc == 31)),
                         reads=[bWa, bHX], writes=[bPa], inc=(kc == 31))
                for kc in range(32):
                    k.op("pe", lambda: nc.tensor.matmul(Pg[:, 0:n], lhsT=Wg[:, kc, :], rhs=HX[:, kc, 0:n], start=(kc == 0), stop=(kc == 31)),
                         reads=[bWg, bHX], writes=[bPg], inc=(kc == 31))
                k.op("act", lambda: nc.scalar.activation(out=SG[:, 0:n], in_=Pg[:, 0:n], func=AF.Silu), reads=[bPg], writes=[bSG])
                k.op("dve", lambda: nc.vector.tensor_tensor(out=Fh[:, i, 0:n], in0=Pa[:, 0:n], in1=SG[:, 0:n], op=ALU.mult), reads=[bPa, bSG], writes=[bF])
            for fo in range(32):
                P, bP = p.ps[fo % 2], p.bps[fo % 2]
                W, bW = WD[cnt["wd"] % len(WD)]
                cnt["wd"] += 1
                k.dma("sp", W[:, 0:hn, :], p.wtile_view(wdnb, HC, fo, h0, hn), bW, b_wdnb)
                for i in range(hn):
                    k.op("pe", lambda: nc.tensor.matmul(P[:, 0:n], lhsT=W[:, i, :], rhs=Fh[:, i, 0:n], start=(i == 0), stop=(i == hn - 1)),
                         reads=[bW, bF], writes=[bP], inc=(i == hn - 1))
                resid(P, bP, fo, n, segs, 160)
        k.dma("pool", fm_view(o_x, TLOC, t0, n), X[:, :, 0:n], b_ox, bX)
        normmod_seg(n, segs, gsn, b_gsn, modn, b_modn, 0)
        k.dma("pool", fm_view(o_hx, TLOC, t0, n), HX[:, :, 0:n], b_ohx, bHX)
    return p.finish()


def pblocks(S):
    blocks = [(0, 256)]
    s = 256
    while s < S:
        n = min(512, S - s)
        blocks.append((s, n))
        s += n
    return blocks


class PBase(Base):
    def setup(self, S):
        nc, k = self.nc, self.k
        self.S = S
        self.consts()
        self.i_ident = self.din("ident", [128, 128], BF16)
        self.i_tri = self.din("tri", [64, 256])
        self.ident, self.b_ident = self.sb(self.es, "ident", [128, 128], BF16)
        self.tri, self.b_tri = self.sb(self.es, "tri", [64, 256], F32)
        k.dma("sp", self.ident[:, :], self.i_ident[:, :], self.b_ident, self.b_in)
        k.dma("sp", self.tri[:, :], self.i_tri[:, :], self.b_tri, self.b_in)
        self.rr = 0

    def proj_fm(self, W, bW, ci, HXB, bHXB, n, P, bP, M=128):
        nc, k = self.nc, self.k
        for kc in range(KC):
            k.op("pe", lambda kc=kc: nc.tensor.matmul(P[0:M, 0:n], lhsT=W[:, kc, ci * 128:ci * 128 + M], rhs=HXB[:, kc, 0:n], start=(kc == 0), stop=(kc == KC - 1)),
                 reads=[bW, bHXB], writes=[bP], inc=(kc == KC - 1))

    def proj_tm(self, W, bW, c0, ncols, HXB, bHXB, ts, P, bP):
        nc, k = self.nc, self.k
        for kc in range(KC):
            k.op("pe", lambda kc=kc: nc.tensor.matmul(P[:, 0:ncols], lhsT=HXB[:, kc, ts * 128:(ts + 1) * 128], rhs=W[:, kc, c0:c0 + ncols], start=(kc == 0), stop=(kc == KC - 1)),
                 reads=[bW, bHXB], writes=[bP], inc=(kc == KC - 1))

    def evac_copy(self, out, in_, reads, writes, scale=None, func=None):
        nc, k = self.nc, self.k
        self.rr += 1
        if func is not None or scale is not None or self.rr % 2 == 0:
            f = func if func is not None else AF.Copy
            if scale is None:
                k.op("act", lambda: nc.scalar.activation(out=out, in_=in_, func=f), reads=reads, writes=writes)
            else:
                k.op("act", lambda: nc.scalar.activation(out=out, in_=in_, func=f, scale=scale), reads=reads, writes=writes)
        else:
            k.op("dve", lambda: nc.vector.tensor_copy(out=out, in_=in_), reads=reads, writes=writes)

    def head_post(self, H, bH, nv, T, SQ, bSQ, R, bR, G, bG, gain_ap, b_gain, OUTb, bOUTb, o_m, b_om, row0, s0):
        nc, k = self.nc, self.k
        S = self.S
        P, bP = self.ps[7], self.bps[7]
        k.op("act", lambda: nc.scalar.activation(out=SQ[:, :, 0:T], in_=H[:, :, 0:T], func=AF.Square), reads=[bH], writes=[bSQ])
        for c0 in range(0, T, 512):
            cn = min(512, T - c0)
            for vj in range(nv):
                k.op("pe", lambda vj=vj: nc.tensor.matmul(P[:, 0:cn], lhsT=self.ones[:, :], rhs=SQ[:, vj, c0:c0 + cn], start=(vj == 0), stop=(vj == nv - 1)),
                     reads=[bSQ, self.b_ones], writes=[bP], inc=(vj == nv - 1))
            k.op("dve", lambda: nc.vector.tensor_scalar(out=R[:, 0:cn], in0=P[:, 0:cn], scalar1=1.0 / (128 * nv), scalar2=EPS, op0=ALU.mult, op1=ALU.add),
                 reads=[bP], writes=[bR])
            k.op("act", lambda: nc.scalar.activation(out=R[:, 0:cn], in_=R[:, 0:cn], func=AF.Sqrt), reads=[bR], writes=[bR])
            k.op("dve", lambda: nc.vector.reciprocal(out=R[:, 0:cn], in_=R[:, 0:cn]), reads=[bR], writes=[bR])
            for vj in range(nv):
                k.op("dve", lambda vj=vj: nc.vector.tensor_tensor(out=H[:, vj, c0:c0 + cn], in0=H[:, vj, c0:c0 + cn], in1=R[:, 0:cn], op=ALU.mult),
                     reads=[bH, bR], writes=[bH])
                k.op("dve", lambda vj=vj: nc.vector.scalar_tensor_tensor(out=OUTb[:, vj, c0:c0 + cn], in0=H[:, vj, c0:c0 + cn], scalar=gain_ap(vj),
                                                                        in1=G[:, vj, c0:c0 + cn], op0=ALU.mult, op1=ALU.mult),
                     reads=[bH, bG, b_gain], writes=[bOUTb])
        for vj in range(nv):
            k.dma("pool", mkap(o_m, (row0 + vj * 128) * S + s0, [[S, 128], [1, T]]), OUTb[:, vj, 0:T], b_om, bOUTb)


def build_PE(ROWS):
    p = PBase()
    nc, k = p.nc, p.k
    SEQ = ROWS * 64
    S = SEQ + 256
    NT = ROWS // 2
    i_hx = p.din("hx", [D, S], BF16)
    i_win = p.din("win", [D, 2048])
    i_lbl = p.din("lbl", [128, 8])
    i_jsel = p.din("jsel", [128, 1])
    i_hgg = p.din("hgg", [128, 2])
    i_nab = p.din("nab", [128, 2 * 5 * 640])
    i_rst = p.din("rst", [128, 2048])
    o_m = p.dout("mloc", [512, S], BF16)
    b_om = k.buf("o_m")
    p.setup(S)
    es = p.es
    wbf, b_wbf = p.dint("wbf", [16 * 128 * 32 * 128], BF16)
    na_q, b_naq = p.dint("na_q", [2 * 128 * S], BF16)
    na_k, b_nak = p.dint("na_k", [2 * 128 * S], BF16)
    na_v, b_nav = p.dint("na_v", [S * 256], BF16)
    hg_q, b_hgq = p.dint("hg_q", [2 * 128 * S], F32)
    hg_ff, b_hgff = p.dint("hg_ff", [2 * 128 * S], F32)
    hg_fb, b_hgfb = p.dint("hg_fb", [2 * 128 * S], F32)
    hg_g, b_hgg_ = p.dint("hg_g", [2 * 128 * S], BF16)
    hg_i, b_hgi = p.dint("hg_i", [S * 256], BF16)
    hg_o, b_hgo = p.dint("hg_o", [2 * 128 * S], F32)
    with ExitStack() as e0:
        p.cast_tiles(i_win, D, 2048, wbf, b_wbf, e0)
        k.barrier()
    del p._cst

    def rows_view(h, hidx, s0, n):
        return mkap(h, hidx * 128 * S + s0, [[S, 128], [1, n]])

    with ExitStack() as e1:
        Wfm, bWfm = p.sb(e1, "Wfm", [128, 32, 512], BF16)
        Wtm, bWtm = p.sb(e1, "Wtm", [128, 32, 256], BF16)
        for i in range(4):
            k.dma("sp", Wfm[:, :, i * 128:(i + 1) * 128], p.wtile_view(wbf, 32, i), bWfm, b_wbf)
        for i in range(2):
            k.dma("sp", Wtm[:, :, i * 128:(i + 1) * 128], p.wtile_view(wbf, 32, 4 + i), bWtm, b_wbf)
        HXBs = [p.sb(e1, f"HXB{i}", [128, 32, 512], BF16) for i in range(2)]
        stg = [p.sb(e1, f"stg{i}", [128, 512], BF16) for i in range(4)]
        si = 0
        for bi, (s0, n) in enumerate(pblocks(S)):
            HXB, bHXB = HXBs[bi % 2]
            k.dma("sp", HXB[:, :, 0:n], fm_view(i_hx, S, s0, n), bHXB, p.b_in)
            for ci in range(4):
                P, bP = p.ps[ci % 4], p.bps[ci % 4]
                p.proj_fm(Wfm, bWfm, ci, HXB, bHXB, n, P, bP)
                st, bst = stg[si % 4]
                si += 1
                p.evac_copy(st[:, 0:n], P[:, 0:n], [bP], [bst], scale=(128 ** -0.5 if ci < 2 else None))
                dh, bd = (na_q, b_naq) if ci < 2 else (na_k, b_nak)
                k.dma("pool", rows_view(dh, ci % 2, s0, n), st[:, 0:n], bd, bst)
            for ts in range(n // 128):
                P, bP = p.ps[4 + ts % 2], p.bps[4 + ts % 2]
                p.proj_tm(Wtm, bWtm, 0, 256, HXB, bHXB, ts, P, bP)
                st, bst = stg[si % 4]
                si += 1
                p.evac_copy(st[:, 0:256], P[:, 0:256], [bP], [bst])
                k.dma("pool", mkap(na_v, (s0 + ts * 128) * 256, [[256, 128], [1, 256]]), st[:, 0:256], b_nav, bst)
        k.barrier()

    with ExitStack() as e2:
        kT, bkT = p.sb(e2, "kT", [128, S], BF16)
        qT, bqT = p.sb(e2, "qT", [128, S], BF16)
        V, bV = p.sb(e2, "V", [128, S // 128, 128], BF16)
        bias, bbias = p.sb(e2, "bias", [128, 5, 640], F32)
        sets = []
        for par in range(2):
            Sc_, bSc_ = p.sb(e2, f"Sc{par}", [128, 896], F32)
            Pn_, bPn_ = p.sb(e2, f"Pn{par}", [128, 896], BF16)
            PT_, bPT_ = p.sb(e2, f"PT{par}", [128, 7, 128], BF16)
            mx_, bmx_ = p.sb(e2, f"mx{par}", [128, 1], F32)
            sm_, bsm_ = p.sb(e2, f"sm{par}", [128, 1], F32)
            sets.append(dict(Sc=Sc_, bSc=bSc_, Pn=Pn_, bPn=bPn_, PT=PT_, bPT=bPT_, mx=mx_, bmx=bmx_, sm=sm_, bsm=bsm_,
                             SA=p.ps[0 + 4 * par], bSA=p.bps[0 + 4 * par], SB=p.ps[1 + 4 * par], bSB=p.bps[1 + 4 * par],
                             PTp=p.ps[2 + 4 * par][:, :].bitcast(BF16), bPTp=p.bps[2 + 4 * par], OA=p.ps[3 + 4 * par], bOA=p.bps[3 + 4 * par]))
        ostg = [p.sb(e2, f"ostg{i}", [128, 128], BF16) for i in range(3)]
        oi = 0
        blk = 0

        def softmax_pv(z, nk, vtiles, h, q0):
            nonlocal oi
            Sc, bSc, Pn, bPn, PT, bPT, mx, bmx, sm, bsm = z["Sc"], z["bSc"], z["Pn"], z["bPn"], z["PT"], z["bPT"], z["mx"], z["bmx"], z["sm"], z["bsm"]
            PTp, bPTp, OA, bOA = z["PTp"], z["bPTp"], z["OA"], z["bOA"]
            k.op("dve", lambda: nc.vector.tensor_reduce(out=mx[:, :], in_=Sc[:, 0:nk], axis=AX.X, op=ALU.max, negate=True), reads=[bSc], writes=[bmx])
            k.op("act", lambda: nc.scalar.activation(out=Sc[:, 0:nk], in_=Sc[:, 0:nk], func=AF.Exp, bias=mx[:, 0:1], scale=1.0), reads=[bSc, bmx], writes=[bSc])
            k.op("dve", lambda: nc.vector.tensor_reduce(out=sm[:, :], in_=Sc[:, 0:nk], axis=AX.X, op=ALU.add), reads=[bSc], writes=[bsm])
            k.op("dve", lambda: nc.vector.reciprocal(out=sm[:, :], in_=sm[:, :]), reads=[bsm], writes=[bsm])
            k.op("dve", lambda: nc.vector.tensor_scalar(out=Pn[:, 0:nk], in0=Sc[:, 0:nk], scalar1=sm[:, 0:1], scalar2=None, op0=ALU.mult), reads=[bSc, bsm], writes=[bPn])
            nt = nk // 128
            for i in range(nt):
                k.op("pe", lambda i=i: nc.tensor.transpose(out=PTp[:, i * 128:(i + 1) * 128], in_=Pn[:, i * 128:(i + 1) * 128], identity=p.ident[:, :]),
                     reads=[bPn, p.b_ident], writes=[bPTp], inc=(i == nt - 1))
            p.evac_copy(PT[:, 0:nt, :], PTp[:, 0:nt * 128].rearrange("p (a c) -> p a c", c=128), [bPTp], [bPT])
            for i, vt in enumerate(vtiles):
                k.op("pe", lambda i=i, vt=vt: nc.tensor.matmul(OA[:, 0:128], lhsT=V[:, vt, :], rhs=PT[:, i, :], start=(i == 0), stop=(i == nt - 1)),
                     reads=[bV, bPT], writes=[bOA], inc=(i == nt - 1))
            st, bst = ostg[oi % 3]
            oi += 1
            p.evac_copy(st[:, :], OA[:, 0:128], [bOA], [bst])
            k.dma("pool", mkap(o_m, h * 128 * S + q0, [[S, 128], [1, 128]]), st[:, :], b_om, bst)

        for h in range(2):
            k.dma("sp", kT[:, :], rows_view(na_k, h, 0, S), bkT, b_nak)
            k.dma("sp", qT[:, :], rows_view(na_q, h, 0, S), bqT, b_naq)
            k.dma("sp", V[:, :, :], mkap(na_v, h * 128, [[256, 128], [128 * 256, S // 128], [1, 128]]), bV, b_nav)
            k.dma("sp", bias[:, :, :].rearrange("p a b -> p (a b)"), i_nab[:, h * 3200:(h + 1) * 3200], bbias, p.b_in)
            for qb in range(2):
                z = sets[blk % 2]
                blk += 1
                Sc, bSc, SB, bSB = z["Sc"], z["bSc"], z["SB"], z["bSB"]
                q0 = qb * 128
                k.op("pe", lambda: nc.tensor.matmul(SB[:, 0:256], lhsT=qT[:, q0:q0 + 128], rhs=kT[:, 0:256], start=True, stop=True),
                     reads=[bqT, bkT], writes=[bSB])
                p.evac_copy(Sc[:, 0:256], SB[:, 0:256], [bSB], [bSc])
                softmax_pv(z, 256, [0, 1], h, q0)
            for j in range(NT):
                z = sets[blk % 2]
                blk += 1
                Sc, bSc, SA, bSA, SB, bSB = z["Sc"], z["bSc"], z["SA"], z["bSA"], z["SB"], z["bSB"]
                ts = min(max(j - 2, 0), NT - 5)
                typ = 0 if j == 0 else (1 if j == 1 else (3 if j == NT - 2 else (4 if j == NT - 1 else 2)))
                q0 = 256 + j * 128
                kl0 = 256 + ts * 128
                k.op("pe", lambda: nc.tensor.matmul(SA[:, 0:512], lhsT=qT[:, q0:q0 + 128], rhs=kT[:, kl0:kl0 + 512], start=True, stop=True),
                     reads=[bqT, bkT], writes=[bSA])
                k.op("pe", lambda: nc.tensor.matmul(SB[:, 0:128], lhsT=qT[:, q0:q0 + 128], rhs=kT[:, kl0 + 512:kl0 + 640], start=True, stop=True),
                     reads=[bqT, bkT], writes=[bSB], inc=False)
                k.op("pe", lambda: nc.tensor.matmul(SB[:, 128:384], lhsT=qT[:, q0:q0 + 128], rhs=kT[:, 0:256], start=True, stop=True),
                     reads=[bqT, bkT], writes=[bSB])
                k.op("dve", lambda: nc.vector.tensor_tensor(out=Sc[:, 0:512], in0=SA[:, 0:512], in1=bias[:, typ, 0:512], op=ALU.add), reads=[bSA, bbias], writes=[bSc])
                k.op("dve", lambda: nc.vector.tensor_tensor(out=Sc[:, 512:640], in0=SB[:, 0:128], in1=bias[:, typ, 512:640], op=ALU.add), reads=[bSB, bbias], writes=[bSc])
                k.op("act", lambda: nc.scalar.copy(out=Sc[:, 640:896], in_=SB[:, 128:384]), reads=[bSB], writes=[bSc])
                softmax_pv(z, 896, [2 + ts + i for i in range(5)] + [0, 1], h, q0)
        k.barrier()

    with ExitStack() as e3:
        Wfm, bWfm = p.sb(e3, "Wfm3", [128, 32, 1024], BF16)
        Wtm, bWtm = p.sb(e3, "Wtm3", [128, 32, 256], BF16)
        for i, cc in enumerate([6, 7, 8, 9, 10, 11, 14, 15]):
            k.dma("sp", Wfm[:, :, i * 128:(i + 1) * 128], p.wtile_view(wbf, 32, cc), bWfm, b_wbf)
        for i in range(2):
            k.dma("sp", Wtm[:, :, i * 128:(i + 1) * 128], p.wtile_view(wbf, 32, 12 + i), bWtm, b_wbf)
        HXBs = [p.sb(e3, f"HXB3{i}", [128, 32, 512], BF16) for i in range(2)]
        stgf = [p.sb(e3, f"stgf{i}", [128, 512], F32) for i in range(4)]
        stgb = [p.sb(e3, f"stgb{i}", [128, 512], BF16) for i in range(3)]
        si = 0
        sj = 0
        dests = [(hg_q, b_hgq), (hg_ff, b_hgff), (hg_fb, b_hgfb), (hg_g, b_hgg_)]
        for bi, (s0, n) in enumerate(pblocks(S)):
            HXB, bHXB = HXBs[bi % 2]
            k.dma("sp", HXB[:, :, 0:n], fm_view(i_hx, S, s0, n), bHXB, p.b_in)
            for ci in range(8):
                P, bP = p.ps[ci % 4], p.bps[ci % 4]
                p.proj_fm(Wfm, bWfm, ci, HXB, bHXB, n, P, bP)
                dh, bd = dests[ci // 2]
                if ci < 6:
                    st, bst = stgf[si % 4]
                    si += 1
                    p.evac_copy(st[:, 0:n], P[:, 0:n], [bP], [bst])
                else:
                    st, bst = stgb[sj % 3]
                    sj += 1
                    p.evac_copy(st[:, 0:n], P[:, 0:n], [bP], [bst], func=AF.Silu)
                k.dma("pool", rows_view(dh, ci % 2, s0, n), st[:, 0:n], bd, bst)
            for ts in range(n // 128):
                P, bP = p.ps[4 + ts % 2], p.bps[4 + ts % 2]
                p.proj_tm(Wtm, bWtm, 0, 256, HXB, bHXB, ts, P, bP)
                st, bst = stgb[sj % 3]
                sj += 1
                p.evac_copy(st[:, 0:256], P[:, 0:256], [bP], [bst])
                k.dma("pool", mkap(hg_i, (s0 + ts * 128) * 256, [[256, 128], [1, 256]]), st[:, 0:256], b_hgi, bst)
        k.barrier()

    with ExitStack() as e4:
        TM = 1024
        lbl, blbl = p.sb(e4, "lbl", [128, 8], F32)
        jsel, bjsel = p.sb(e4, "jsel", [128, 1], F32)
        hgg, bhgg = p.sb(e4, "hgg", [128, 2], F32)
        rst, brst = p.sb(e4, "rst", [128, TM], F32)
        lb, blb = p.sb(e4, "lb", [128, 4], F32)
        oml, boml = p.sb(e4, "oml", [128, 4], F32)
        k.dma("sp", lbl[:, :], i_lbl[:, :], blbl, p.b_in)
        k.dma("sp", jsel[:, :], i_jsel[:, :], bjsel, p.b_in)
        k.dma("sp", hgg[:, :], i_hgg[:, :], bhgg, p.b_in)
        k.dma("sp", rst[:, :], i_rst[:, 0:TM], brst, p.b_in)
        k.op("dve", lambda: nc.vector.tensor_tensor(out=lb[:, :], in0=lbl[:, 4:8], in1=lbl[:, 0:4], op=ALU.subtract), reads=[blbl], writes=[blb])
        k.op("act", lambda: nc.scalar.activation(out=lb[:, :], in_=lb[:, :], func=AF.Sigmoid), reads=[blb], writes=[blb])
        k.op("dve", lambda: nc.vector.tensor_scalar(out=lb[:, :], in0=lb[:, :], scalar1=jsel[:, 0:1], scalar2=None, op0=ALU.mult), reads=[blb, bjsel], writes=[blb])
        k.op("dve", lambda: nc.vector.tensor_scalar(out=oml[:, :], in0=lb[:, :], scalar1=-1.0, scalar2=1.0, op0=ALU.mult, op1=ALU.add), reads=[blb], writes=[boml])
        HT = []
        for h in range(2):
            t = {}
            for nm in ("q", "fp", "F1", "K1", "Bc", "CUM", "EX", "Ofw"):
                t[nm], t["b" + nm] = p.sb(e4, f"{nm}{h}", [128, TM], F32)
            for nm in ("QT", "KTb", "KH"):
                t[nm], t["b" + nm] = p.sb(e4, f"{nm}{h}", [128, TM], BF16)
            for nm in ("G", "SQ", "OUTb"):
                t[nm], t["b" + nm] = p.sb(e4, f"{nm}{h}", [128, 1, TM], BF16)
            t["O"], t["bO"] = p.sb(e4, f"O{h}", [128, 1, TM], F32)
            t["EL"], t["bEL"] = p.sb(e4, f"EL{h}", [128, TM // 32], F32)
            t["Vt"], t["bVt"] = p.sb(e4, f"Vt{h}", [32, TM // 32, 128], BF16)
            t["St"], t["bSt"] = p.sb(e4, f"St{h}", [128, 128], F32)
            t["Sbf"], t["bSbf"] = p.sb(e4, f"Sbf{h}", [128, 128], BF16)
            t["A"], t["bA"] = p.sb(e4, f"A{h}", [32, 32], BF16)
            t["KHt"], t["bKHt"] = p.sb(e4, f"KHt{h}", [32, 128], BF16)
            t["KHp"] = p.ps[4 * h + 1][:, :].bitcast(BF16)
            HT.append(t)
        R, bR = p.sb(e4, "R4", [128, 512], F32)
        sbs = [(0, 256)] + [(256 + i * TM, min(TM, SEQ - i * TM)) for i in range((SEQ + TM - 1) // TM)]
        for d in range(2):
            for h in range(2):
                t = HT[h]
                k.op("dve", lambda: nc.vector.memset(t["St"][:, :], 0.0), writes=[t["bSt"]])
                k.op("dve", lambda: nc.vector.memset(t["Sbf"][:, :], 0.0), writes=[t["bSbf"]])
            order = sbs if d == 0 else [sbs[0]] + sbs[:0:-1]
            for (s0, T) in order:
                nch = T // 32
                for h in range(2):
                    t = HT[h]
                    col = d * 2 + h
                    q, bq, fp, bfp, F1, bF1, K1, bK1, Bc, bBc, CUM, bCUM, EX, bEX = (t["q"], t["bq"], t["fp"], t["bfp"], t["F1"], t["bF1"], t["K1"], t["bK1"],
                                                                                      t["Bc"], t["bBc"], t["CUM"], t["bCUM"], t["EX"], t["bEX"])
                    QT, bQT, KTb, bKTb, KH, bKH, EL, bEL, Vt, bVt = t["QT"], t["bQT"], t["KTb"], t["bKTb"], t["KH"], t["bKH"], t["EL"], t["bEL"], t["Vt"], t["bVt"]
                    fph, bfph = (hg_ff, b_hgff) if d == 0 else (hg_fb, b_hgfb)
                    k.dma("sp", q[:, 0:T], rows_view(hg_q, h, s0, T), bq, b_hgq)
                    k.dma("sp", fp[:, 0:T], rows_view(fph, h, s0, T), bfp, bfph)
                    k.dma("sp", Vt[:, 0:nch, :], mkap(hg_i, s0 * 256 + h * 128, [[256, 32], [32 * 256, nch], [1, 128]]), bVt, b_hgi)
                    if d == 1:
                        k.dma("sp", t["Ofw"][:, 0:T], rows_view(hg_o, h, s0, T), t["bOfw"], b_hgo)
                        k.dma("sp", t["G"][:, 0, 0:T], rows_view(hg_g, h, s0, T), t["bG"], b_hgg_)
                    k.op("act", lambda: nc.scalar.activation(out=F1[:, 0:T], in_=fp[:, 0:T], func=AF.Sigmoid), reads=[bfp], writes=[bF1])
                    k.op("dve", lambda: nc.vector.tensor_scalar(out=F1[:, 0:T], in0=F1[:, 0:T], scalar1=oml[:, col:col + 1], scalar2=lb[:, col:col + 1], op0=ALU.mult, op1=ALU.add),
                         reads=[bF1, boml, blb], writes=[bF1])
                    k.op("act", lambda: nc.scalar.activation(out=K1[:, 0:T], in_=fp[:, 0:T], func=AF.Sigmoid, scale=-1.0), reads=[bfp], writes=[bK1])
                    k.op("pool", lambda: nc.gpsimd.tensor_scalar(out=K1[:, 0:T], in0=K1[:, 0:T], scalar1=oml[:, col:col + 1], scalar2=None, op0=ALU.mult),
                         reads=[bK1, boml], writes=[bK1])
                    k.op("act", lambda: nc.scalar.activation(out=F1[:, 0:T], in_=F1[:, 0:T], func=AF.Ln), reads=[bF1], writes=[bF1])
                    k.op("dve", lambda: nc.vector.tensor_tensor_scan(out=Bc[:, 0:T], data0=rst[:, 0:T], data1=F1[:, 0:T], initial=0.0, op0=ALU.mult, op1=ALU.add),
                         reads=[brst, bF1], writes=[bBc])
                    B3 = Bc[:, 0:T].rearrange("p (c s) -> p c s", s=32)
                    if d == 0:
                        cum, bcum = Bc, bBc
                        last_ap = B3[:, :, 31]
                    else:
                        k.op("dve", lambda: nc.vector.tensor_tensor(out=CUM[:, 0:T], in0=F1[:, 0:T], in1=Bc[:, 0:T], op=ALU.subtract), reads=[bF1, bBc], writes=[bCUM])
                        C3 = CUM[:, 0:T].rearrange("p (c s) -> p c s", s=32)
                        k.op("dve", lambda: nc.vector.tensor_tensor(out=C3, in0=C3, in1=B3[:, :, 31:32].broadcast_to([128, nch, 32]), op=ALU.add), reads=[bCUM, bBc], writes=[bCUM])
                        cum, bcum = CUM, bCUM
                        last_ap = C3[:, :, 0]
                    k.op("act", lambda: nc.scalar.activation(out=EX[:, 0:T], in_=cum[:, 0:T], func=AF.Exp), reads=[bcum], writes=[bEX])
                    k.op("dve", lambda: nc.vector.tensor_tensor(out=QT[:, 0:T], in0=q[:, 0:T], in1=EX[:, 0:T], op=ALU.mult), reads=[bq, bEX], writes=[bQT])
                    k.op("act", lambda: nc.scalar.activation(out=EX[:, 0:T], in_=cum[:, 0:T], func=AF.Exp, scale=-1.0), reads=[bcum], writes=[bEX])
                    k.op("dve", lambda: nc.vector.tensor_tensor(out=K1[:, 0:T], in0=K1[:, 0:T], in1=EX[:, 0:T], op=ALU.mult), reads=[bK1, bEX], writes=[bK1])
                    k.op("act", lambda: nc.scalar.copy(out=KTb[:, 0:T], in_=K1[:, 0:T]), reads=[bK1], writes=[bKTb])
                    k.op("act", lambda: nc.scalar.activation(out=EL[:, 0:nch], in_=last_ap, func=AF.Exp), reads=[bcum], writes=[bEL])
                    k.op("dve", lambda: nc.vector.tensor_tensor(out=KH[:, 0:T].rearrange("p (c s) -> p c s", s=32), in0=K1[:, 0:T].rearrange("p (c s) -> p c s", s=32),
                                                                in1=EL[:, 0:nch].unsqueeze(2).broadcast_to([128, nch, 32]), op=ALU.mult),
                         reads=[bK1, bEL], writes=[bKH])
                crange = range(nch) if d == 0 else range(nch - 1, -1, -1)
                moff = 0 if d == 0 else 32
                for c in crange:
                    cs = slice(c * 32, (c + 1) * 32)
                    for h in range(2):
                        t = HT[h]
                        QT, bQT, KTb, bKTb, KH, bKH, EL, bEL, Vt, bVt = t["QT"], t["bQT"], t["KTb"], t["bKTb"], t["KH"], t["bKH"], t["EL"], t["bEL"], t["Vt"], t["bVt"]
                        St, bSt, Sbf, bSbf, A, bA, KHt, bKHt, O, bO = t["St"], t["bSt"], t["Sbf"], t["bSbf"], t["A"], t["bA"], t["KHt"], t["bKHt"], t["O"], t["bO"]
                        KHp = t["KHp"]
                        AT, bAT = p.ps[4 * h], p.bps[4 * h]
                        bKHp = p.bps[4 * h + 1]
                        OP, bOP = p.ps[4 * h + 2], p.bps[4 * h + 2]
                        UP, bUP = p.ps[4 * h + 3], p.bps[4 * h + 3]
                        k.op("pe", lambda: nc.tensor.matmul(AT[0:32, 0:32], lhsT=KTb[:, cs], rhs=QT[:, cs], start=True, stop=True), reads=[bKTb, bQT], writes=[bAT])
                        k.op("dve", lambda: nc.vector.tensor_tensor(out=A[:, :], in0=AT[0:32, 0:32], in1=p.tri[0:32, moff:moff + 32], op=ALU.mult), reads=[bAT, p.b_tri], writes=[bA])
                        k.op("pe", lambda: nc.tensor.transpose(out=KHp[0:32, 0:128], in_=KH[:, cs], identity=p.ident[:, :]), reads=[bKH, p.b_ident], writes=[bKHp])
                        k.op("act", lambda: nc.scalar.copy(out=KHt[:, :], in_=KHp[0:32, 0:128]), reads=[bKHp], writes=[bKHt])
                        k.op("pe", lambda: nc.tensor.matmul(OP[:, 0:32], lhsT=Vt[:, c, :], rhs=A[:, :], start=True, stop=False), reads=[bVt, bA], writes=[bOP], inc=False)
                        k.op("pe", lambda: nc.tensor.matmul(OP[:, 0:32], lhsT=Sbf[:, :], rhs=QT[:, cs], start=False, stop=True), reads=[bSbf, bQT], writes=[bOP])
                        k.op("pe", lambda: nc.tensor.matmul(UP[:, 0:128], lhsT=KHt[:, :], rhs=Vt[:, c, :], start=True, stop=True), reads=[bKHt, bVt], writes=[bUP])
                        if d == 0:
                            k.op("act", lambda: nc.scalar.copy(out=O[:, 0, cs], in_=OP[:, 0:32]), reads=[bOP], writes=[bO])
                        else:
                            k.op("pool", lambda: nc.gpsimd.tensor_copy(out=O[:, 0, cs], in_=t["Ofw"][:, cs]), reads=[t["bOfw"]], writes=[bO]) if False else None
                            k.op("dve", lambda: nc.vector.tensor_tensor(out=O[:, 0, cs], in0=OP[:, 0:32], in1=t["Ofw"][:, cs], op=ALU.add), reads=[bOP, t["bOfw"]], writes=[bO])
                        k.op("dve", lambda: nc.vector.scalar_tensor_tensor(out=St[:, :], in0=St[:, :], scalar=EL[:, c:c + 1], in1=UP[:, 0:128], op0=ALU.mult, op1=ALU.add),
                             reads=[bSt, bEL, bUP], writes=[bSt])
                        k.op("act", lambda: nc.scalar.copy(out=Sbf[:, :], in_=St[:, :]), reads=[bSt], writes=[bSbf])
                for h in range(2):
                    t = HT[h]
                    if d == 0:
                        k.dma("pool", rows_view(hg_o, h, s0, T), t["O"][:, 0, 0:T], b_hgo, t["bO"])
                    else:
                        p.head_post(t["O"], t["bO"], 1, T, t["SQ"], t["bSQ"], R, bR, t["G"], t["bG"], lambda vj, h=h: hgg[:, h:h + 1], bhgg,
                                    t["OUTb"], t["bOUTb"], o_m, b_om, 256 + h * 128, s0)
        k.barrier()
    return p.finish()


def col_layout(v):
    v = np.asarray(v)
    return np.ascontiguousarray(v.reshape(-1, 128).T)


def const_tables():
    s = np.arange(64)
    tri = np.zeros((64, 256), np.float32)
    f32_ = (s[:32, None] <= s[None, :32]).astype(np.float32)
    tri[:32, 0:32] = f32_
    tri[:32, 32:64] = f32_.T
    f64_ = (s[:, None] <= s[None, :]).astype(np.float32)
    tri[:, 64:128] = f64_
    tri[:, 128:192] = f64_.T
    rst = np.ones((128, 2048), np.float32)
    rst[:, ::32] = 0.0
    rst64 = np.ones((1, 2048), np.float32)
    rst64[:, ::64] = 0.0
    ident = np.eye(128, dtype=np.float32).astype(NPBF)
    return tri, rst, rst64, ident


def prep_even_win(w, c):
    return np.ascontiguousarray(np.concatenate([w[:, sec * 2048 + 256 * c: sec * 2048 + 256 * c + 256] for sec in range(8)], axis=1))


def na_bias_tables(rpb, c, ROWS):
    NT = ROWS // 2
    out = np.empty((2, 5, 128, 640), np.float32)
    qi = np.arange(128)
    ki = np.arange(640)
    for ti, j in enumerate([0, 1, 2, NT - 2, NT - 1]):
        ts = min(max(j - 2, 0), NT - 5)
        r = 2 * j + qi // 64
        cq = qi % 64
        kr = 2 * ts + ki // 64
        kc = ki % 64
        r0 = np.clip(r - 4, 0, ROWS - 8)
        c0 = np.clip(cq - 8, 0, 64 - 16)
        inwin = ((kr[None, :] >= r0[:, None]) & (kr[None, :] < r0[:, None] + 8) &
                 (kc[None, :] >= c0[:, None]) & (kc[None, :] < c0[:, None] + 16))
        di = np.clip(kr[None, :] - r[:, None] + 7, 0, 14)
        dj = np.clip(kc[None, :] - cq[:, None] + 15, 0, 30)
        for h in range(2):
            g = rpb[2 * c + h][di, dj]
            out[h, ti] = np.where(inwin, g, np.float32(-30000.0))
    return np.ascontiguousarray(out.transpose(2, 0, 1, 3).reshape(128, 2 * 5 * 640))


def even_small_params(lbl_all, j, hgg_all, c):
    lbl = np.empty((128, 8), np.float32)
    hg = np.empty((128, 2), np.float32)
    for d in range(2):
        for h in range(2):
            sl = slice((2 * c + h) * 128, (2 * c + h + 1) * 128)
            lbl[:, d * 2 + h] = lbl_all[d, 0, sl]
            lbl[:, 4 + d * 2 + h] = lbl_all[d, 1, sl]
    for h in range(2):
        hg[:, h] = hgg_all[j, (2 * c + h) * 128:(2 * c + h + 1) * 128]
    jsel = np.full((128, 1), float(j), np.float32)
    return lbl, jsel, hg


def build_PO(ROWS):
    p = PBase()
    nc, k = p.nc, p.k
    SEQ = ROWS * 64
    S = SEQ + 256
    i_hx = p.din("hx", [D, S], BF16)
    i_win = p.din("win", [D, 2048])
    i_wg = p.din("wg", [D, 4])
    i_bg = p.din("bg", [1, 4])
    i_mlg = p.din("mlg", [128, 4])
    i_ropeR = p.din("ropeR", [128, 2 * ROWS])
    i_ropeC = p.din("ropeC", [128, 128])
    i_rst64 = p.din("rst64", [1, 2048])
    o_m = p.dout("mloc", [512, S], BF16)
    b_om = k.buf("o_m")
    p.setup(S)
    es = p.es
    wbf, b_wbf = p.dint("wbf", [16 * 128 * 32 * 128], BF16)
    ml_q, b_mlq = p.dint("ml_q", [2 * 128 * S], BF16)
    ml_k, b_mlk = p.dint("ml_k", [2 * 128 * S], BF16)
    ml_o, b_mlo = p.dint("ml_o", [4 * 128 * S], BF16)
    ml_v, b_mlv = p.dint("ml_v", [S * 512], BF16)
    ml_g, b_mlg_ = p.dint("ml_g", [4 * S], F32)
    ml_h, b_mlh = p.dint("ml_h", [4 * 128 * S], F32)
    with ExitStack() as e0:
        p.cast_tiles(i_win, D, 2048, wbf, b_wbf, e0)
        k.barrier()
    del p._cst

    def rows_view(h, hidx, s0, n):
        return mkap(h, hidx * 128 * S + s0, [[S, 128], [1, n]])

    with ExitStack() as e1:
        Wfm, bWfm = p.sb(e1, "Wfm", [128, 32, 1024], BF16)
        for i in range(8):
            k.dma("sp", Wfm[:, :, i * 128:(i + 1) * 128], p.wtile_view(wbf, 32, i), bWfm, b_wbf)
        wg32, bwg32 = p.sb(e1, "wg32", [128, 32, 4], F32)
        Wg, bWg = p.sb(e1, "Wg", [128, 32, 4], BF16)
        k.dma("sp", wg32[:, :, :], mkap(i_wg, 0, [[4, 128], [128 * 4, 32], [1, 4]]), bwg32, p.b_in)
        k.op("dve", lambda: nc.vector.tensor_copy(out=Wg[:, :, :], in_=wg32[:, :, :]), reads=[bwg32], writes=[bWg])
        ropeR, bropeR = p.sb(e1, "ropeR", [128, 2 * ROWS], F32)
        ropeC, bropeC = p.sb(e1, "ropeC", [128, 128], F32)
        k.dma("sp", ropeR[:, :], i_ropeR[:, :], bropeR, p.b_in)
        k.dma("sp", ropeC[:, :], i_ropeC[:, :], bropeC, p.b_in)
        HXBs = [p.sb(e1, f"HXB{i}", [128, 32, 512], BF16) for i in range(2)]
        stg = [p.sb(e1, f"stg{i}", [128, 512], BF16) for i in range(4)]
        T1, bT1 = p.sb(e1, "T1", [128, 512], F32)
        T2, bT2 = p.sb(e1, "T2", [128, 512], F32)
        gst, bgst = p.sb(e1, "gst", [4, 512], F32)
        si = 0
        for bi, (s0, n) in enumerate(pblocks(S)):
            HXB, bHXB = HXBs[bi % 2]
            k.dma("sp", HXB[:, :, 0:n], fm_view(i_hx, S, s0, n), bHXB, p.b_in)
            for which in range(2):
                scale = 1.0 if which == 0 else 256 ** -0.5
                dh, bd = (ml_q, b_mlq) if which == 0 else (ml_k, b_mlk)
                for c in range(2):
                    cc = which * 4 + c
                    P1, bP1 = p.ps[(which * 2 + c) % 2 * 2], p.bps[(which * 2 + c) % 2 * 2]
                    p.proj_fm(Wfm, bWfm, cc, HXB, bHXB, n, P1, bP1)
                    st, bst = stg[si % 4]
                    si += 1
                    if s0 < 256:
                        p.evac_copy(st[:, 0:n], P1[:, 0:n], [bP1], [bst], scale=scale)
                    else:
                        P2, bP2 = p.ps[(which * 2 + c) % 2 * 2 + 1], p.bps[(which * 2 + c) % 2 * 2 + 1]
                        p.proj_fm(Wfm, bWfm, cc + 2, HXB, bHXB, n, P2, bP2)
                        t0 = s0 - 256
                        r0, nr = t0 // 64, n // 64
                        if c == 0:
                            cosb = ropeR[:, r0:r0 + nr].unsqueeze(2).broadcast_to([128, nr, 64])
                            sinb = ropeR[:, ROWS + r0:ROWS + r0 + nr].unsqueeze(2).broadcast_to([128, nr, 64])
                        else:
                            cosb = ropeC[:, 0:64].unsqueeze(1).broadcast_to([128, nr, 64])
                            sinb = ropeC[:, 64:128].unsqueeze(1).broadcast_to([128, nr, 64])
                        v3 = lambda t: t[:, 0:n].rearrange("p (r c) -> p r c", c=64)
                        k.op("dve", lambda: nc.vector.tensor_tensor(out=v3(T1), in0=v3(P1), in1=cosb, op=ALU.mult), reads=[bP1, bropeR, bropeC], writes=[bT1])
                        k.op("dve", lambda: nc.vector.tensor_tensor(out=v3(T2), in0=v3(P2), in1=sinb, op=ALU.mult), reads=[bP2, bropeR, bropeC], writes=[bT2])
                        k.op("pool", lambda: nc.gpsimd.tensor_tensor(out=T1[:, 0:n], in0=T1[:, 0:n], in1=T2[:, 0:n], op=ALU.add), reads=[bT1, bT2], writes=[bT1])
                        k.op("act", lambda: nc.scalar.activation(out=st[:, 0:n], in_=T1[:, 0:n], func=AF.Copy, scale=scale), reads=[bT1], writes=[bst])
                    k.dma("pool", rows_view(dh, c, s0, n), st[:, 0:n], bd, bst)
            PG, bPG = p.ps[6], p.bps[6]
            p.proj_fm(Wg, bWg, 0, HXB, bHXB, n, PG, bPG, M=4)
            k.op("dve", lambda: nc.vector.tensor_copy(out=gst[:, 0:n], in_=PG[0:4, 0:n]), reads=[bPG], writes=[bgst])
            k.dma("pool", mkap(ml_g, s0, [[S, 4], [1, n]]), gst[:, 0:n], b_mlg_, bgst)
        k.barrier()

    with ExitStack() as e1:
        Wfm, bWfm = p.sb(e1, "Wfmb", [128, 32, 512], BF16)
        Wtm, bWtm = p.sb(e1, "Wtmb", [128, 32, 512], BF16)
        for i in range(4):
            k.dma("sp", Wfm[:, :, i * 128:(i + 1) * 128], p.wtile_view(wbf, 32, 8 + i), bWfm, b_wbf)
            k.dma("sp", Wtm[:, :, i * 128:(i + 1) * 128], p.wtile_view(wbf, 32, 12 + i), bWtm, b_wbf)
        HXBs = [p.sb(e1, f"HXBb{i}", [128, 32, 512], BF16) for i in range(2)]
        stg = [p.sb(e1, f"stgb{i}", [128, 512], BF16) for i in range(4)]
        si = 0
        for bi, (s0, n) in enumerate(pblocks(S)):
            HXB, bHXB = HXBs[bi % 2]
            k.dma("sp", HXB[:, :, 0:n], fm_view(i_hx, S, s0, n), bHXB, p.b_in)
            for ci in range(4):
                P, bP = p.ps[ci % 4], p.bps[ci % 4]
                p.proj_fm(Wfm, bWfm, ci, HXB, bHXB, n, P, bP)
                st, bst = stg[si % 4]
                si += 1
                p.evac_copy(st[:, 0:n], P[:, 0:n], [bP], [bst], func=AF.Sigmoid)
                k.dma("pool", rows_view(ml_o, ci, s0, n), st[:, 0:n], b_mlo, bst)
            for ts in range(n // 128):
                P, bP = p.ps[4 + ts % 2], p.bps[4 + ts % 2]
                p.proj_tm(Wtm, bWtm, 0, 512, HXB, bHXB, ts, P, bP)
                st, bst = stg[si % 4]
                si += 1
                p.evac_copy(st[:, 0:512], P[:, 0:512], [bP], [bst])
                k.dma("pool", mkap(ml_v, (s0 + ts * 128) * 512, [[512, 128], [1, 512]]), st[:, 0:512], b_mlv, bst)
        k.barrier()

    with ExitStack() as e2:
        TM = 1024
        NCH = TM // 64
        bg, bbg = p.sb(e2, "bg", [1, 4], F32)
        mlg, bmlg = p.sb(e2, "mlg", [128, 4], F32)
        rst64, brst = p.sb(e2, "rst64", [1, TM], F32)
        onesf, bonesf = p.sb(e2, "onesf", [1, 128], F32)
        k.dma("sp", bg[:, :], i_bg[:, :], bbg, p.b_in)
        k.dma("sp", mlg[:, :], i_mlg[:, :], bmlg, p.b_in)
        k.dma("sp", rst64[:, :], i_rst64[:, 0:TM], brst, p.b_in)
        k.op("dve", lambda: nc.vector.memset(onesf[:, :], 1.0), writes=[bonesf])
        ipre, bipre = p.sb(e2, "ipre", [1, TM], F32)
        fpre, bfpre = p.sb(e2, "fpre", [1, TM], F32)
        brow, bbrow = p.sb(e2, "brow", [1, TM], F32)
        urow, burow = p.sb(e2, "urow", [1, TM], F32)
        wsrow, bwsrow = p.sb(e2, "wsrow", [1, TM], F32)
        erow, berow = p.sb(e2, "erow", [1, TM], F32)
        umax, bumax = p.sb(e2, "umax", [1, NCH], F32)
        Mst, bMst = p.sb(e2, "Mst", [1, NCH], F32)
        marr, bmarr = p.sb(e2, "marr", [1, NCH + 1], F32)
        wprow, bwprow = p.sb(e2, "wprow", [1, NCH], F32)
        mcar, bmcar = p.sb(e2, "mcar", [1, 1], F32)
        kT, bkT = p.sb(e2, "kT", [128, 2, TM], BF16)
        qT, bqT = p.sb(e2, "qT", [128, 2, TM], BF16)
        KH, bKH = p.sb(e2, "KH", [128, 2, TM], BF16)
        Vp, bVp = p.sb(e2, "Vp", [64, NCH, 640], BF16)
        WSbc, bWSbc = p.sb(e2, "WSbc", [128, TM], F32)
        Ebc, bEbc = p.sb(e2, "Ebc", [128, TM], F32)
        Wp, bWp = p.sb(e2, "Wp", [128, NCH], F32)
        O, bO = p.sb(e2, "O", [128, 4, TM], F32)
        Ofw, bOfw = p.sb(e2, "Ofw", [128, 4, TM], F32)
        OG, bOG = p.sb(e2, "OG", [128, 4, TM], BF16)
        SQ, bSQ = p.sb(e2, "SQ", [128, 4, TM], BF16)
        OUTb, bOUTb = p.sb(e2, "OUTb", [128, 4, TM], BF16)
        R, bR = p.sb(e2, "R", [128, 512], F32)
        Cst, bCst = p.sb(e2, "Cst", [128, 2, 640], F32)
        Cbfs = [p.sb(e2, f"Cbf{i}", [128, 2, 640], BF16) for i in range(2)]
        Kts = [p.sb(e2, f"Kt{i}", [64, 256], BF16) for i in range(2)]
        As = [p.sb(e2, f"A{i}", [64, 64], BF16) for i in range(2)]
        cpar = 0
        DNs = [p.sb(e2, f"DN{i}", [128, 64], F32) for i in range(2)]
        TH, bTH = p.sb(e2, "TH", [128, 4, 64], F32)
        k.op("dve", lambda: nc.vector.memset(Vp[:, :, 512:640], 1.0), writes=[bVp])
        sbs = [(0, 256)] + [(256 + i * TM, min(TM, SEQ - i * TM)) for i in range((SEQ + TM - 1) // TM)]
        KTps = [(p.ps[0][:, :].bitcast(BF16), p.bps[0]), (p.ps[6][:, :].bitcast(BF16), p.bps[6])]
        STs = [(p.ps[1], p.bps[1]), (p.ps[7], p.bps[7])]
        for d in range(2):
            k.op("dve", lambda: nc.vector.memset(Cst[:, :, :], 0.0), writes=[bCst])
            k.op("dve", lambda: nc.vector.memset(mcar[:, :], 0.0), writes=[bmcar])
            order = sbs if d == 0 else [sbs[0]] + sbs[:0:-1]
            for (s0, T) in order:
                nch = T // 64
                k.dma("sp", ipre[:, 0:T], mkap(ml_g, (2 * d) * S + s0, [[S, 1], [1, T]]), bipre, b_mlg_)
                k.dma("sp", fpre[:, 0:T], mkap(ml_g, (2 * d + 1) * S + s0, [[S, 1], [1, T]]), bfpre, b_mlg_)
                k.dma("sp", kT[:, :, 0:T], mkap(ml_k, s0, [[S, 128], [128 * S, 2], [1, T]]), bkT, b_mlk)
                k.dma("sp", qT[:, :, 0:T], mkap(ml_q, s0, [[S, 128], [128 * S, 2], [1, T]]), bqT, b_mlq)
                k.dma("sp", Vp[:, 0:nch, 0:512], mkap(ml_v, s0 * 512, [[512, 64], [64 * 512, nch], [1, 512]]), bVp, b_mlv)
                if d == 1:
                    k.dma("sp", Ofw[:, :, 0:T], mkap(ml_h, s0, [[S, 128], [128 * S, 4], [1, T]]), bOfw, b_mlh)
                    k.dma("sp", OG[:, :, 0:T], mkap(ml_o, s0, [[S, 128], [128 * S, 4], [1, T]]), bOG, b_mlo)
                k.op("act", lambda: nc.scalar.activation(out=fpre[:, 0:T], in_=fpre[:, 0:T], func=AF.Sigmoid, bias=bg[0:1, 2 * d + 1:2 * d + 2], scale=1.0),
                     reads=[bfpre, bbg], writes=[bfpre])
                k.op("act", lambda: nc.scalar.activation(out=fpre[:, 0:T], in_=fpre[:, 0:T], func=AF.Ln), reads=[bfpre], writes=[bfpre])
                k.op("dve", lambda: nc.vector.tensor_tensor_scan(out=brow[:, 0:T], data0=rst64[:, 0:T], data1=fpre[:, 0:T], initial=0.0, op0=ALU.mult, op1=ALU.add),
                     reads=[brst, bfpre], writes=[bbrow])
                b3 = brow[:, 0:T].rearrange("p (c s) -> p c s", s=64)
                if d == 1:
                    k.op("dve", lambda: nc.vector.tensor_tensor(out=urow[:, 0:T], in0=fpre[:, 0:T], in1=brow[:, 0:T], op=ALU.subtract), reads=[bfpre, bbrow], writes=[burow])
                    u3 = urow[:, 0:T].rearrange("p (c s) -> p c s", s=64)
                    k.op("dve", lambda: nc.vector.tensor_tensor(out=erow[:, 0:T].rearrange("p (c s) -> p c s", s=64), in0=u3, in1=b3[:, :, 63:64].broadcast_to([1, nch, 64]), op=ALU.add),
                         reads=[burow, bbrow], writes=[berow])
                    k.op("dve", lambda: nc.vector.tensor_copy(out=brow[:, 0:T], in_=erow[:, 0:T]), reads=[berow], writes=[bbrow])
                    Bview = b3[:, :, 0]
                else:
                    Bview = b3[:, :, 63]
                k.op("dve", lambda: nc.vector.scalar_tensor_tensor(out=urow[:, 0:T], in0=ipre[:, 0:T], scalar=bg[0:1, 2 * d:2 * d + 1], in1=brow[:, 0:T], op0=ALU.add, op1=ALU.subtract),
                     reads=[bipre, bbg, bbrow], writes=[burow])
                u3 = urow[:, 0:T].rearrange("p (c s) -> p c s", s=64)
                k.op("dve", lambda: nc.vector.tensor_reduce(out=umax[:, 0:nch], in_=u3, axis=AX.X, op=ALU.max), reads=[burow], writes=[bumax])
                if d == 0:
                    k.op("dve", lambda: nc.vector.tensor_copy(out=marr[:, 0:1], in_=mcar[:, :]), reads=[bmcar], writes=[bmarr])
                    crange = list(range(nch))
                else:
                    k.op("dve", lambda: nc.vector.tensor_copy(out=marr[:, nch:nch + 1], in_=mcar[:, :]), reads=[bmcar], writes=[bmarr])
                    crange = list(range(nch - 1, -1, -1))
                for c in crange:
                    mi, mo = (c, c + 1) if d == 0 else (c + 1, c)
                    k.op("dve", lambda: nc.vector.tensor_tensor(out=Mst[:, c:c + 1], in0=marr[:, mi:mi + 1], in1=umax[:, c:c + 1], op=ALU.max), reads=[bmarr, bumax], writes=[bMst])
                    k.op("dve", lambda: nc.vector.tensor_tensor(out=marr[:, mo:mo + 1], in0=Mst[:, c:c + 1], in1=Bview[:, c:c + 1], op=ALU.add), reads=[bMst, bbrow], writes=[bmarr])
                mlast = nch if d == 0 else 0
                k.op("dve", lambda: nc.vector.tensor_copy(out=mcar[:, :], in_=marr[:, mlast:mlast + 1]), reads=[bmarr], writes=[bmcar])
                mb0 = 0 if d == 0 else 1
                k.op("dve", lambda: nc.vector.tensor_tensor(out=wprow[:, 0:nch], in0=marr[:, mb0:mb0 + nch], in1=Mst[:, 0:nch], op=ALU.subtract), reads=[bmarr, bMst], writes=[bwprow])
                k.op("act", lambda: nc.scalar.activation(out=wprow[:, 0:nch], in_=wprow[:, 0:nch], func=AF.Exp), reads=[bwprow], writes=[bwprow])
                Mb = Mst[:, 0:nch].unsqueeze(2).broadcast_to([1, nch, 64])
                k.op("dve", lambda: nc.vector.tensor_tensor(out=wsrow[:, 0:T].rearrange("p (c s) -> p c s", s=64), in0=u3, in1=Mb, op=ALU.subtract), reads=[burow, bMst], writes=[bwsrow])
                k.op("act", lambda: nc.scalar.activation(out=wsrow[:, 0:T], in_=wsrow[:, 0:T], func=AF.Exp), reads=[bwsrow], writes=[bwsrow])
                k.op("dve", lambda: nc.vector.tensor_tensor(out=erow[:, 0:T].rearrange("p (c s) -> p c s", s=64), in0=b3, in1=Mb, op=ALU.add), reads=[bbrow, bMst], writes=[berow])
                k.op("act", lambda: nc.scalar.activation(out=erow[:, 0:T], in_=erow[:, 0:T], func=AF.Exp, scale=-1.0), reads=[berow], writes=[berow])
                BP, bBP = p.ps[5], p.bps[5]
                for (row, brw, dst, bdst) in ((wsrow, bwsrow, WSbc, bWSbc), (erow, berow, Ebc, bEbc)):
                    for c0 in range(0, T, 512):
                        cn = min(512, T - c0)
                        k.op("pe", lambda: nc.tensor.matmul(BP[:, 0:cn], lhsT=onesf[0:1, :], rhs=row[0:1, c0:c0 + cn], start=True, stop=True), reads=[bonesf, brw], writes=[bBP])
                        p.evac_copy(dst[:, c0:c0 + cn], BP[:, 0:cn], [bBP], [bdst])
                k.op("pe", lambda: nc.tensor.matmul(BP[:, 0:nch], lhsT=onesf[0:1, :], rhs=wprow[0:1, 0:nch], start=True, stop=True), reads=[bonesf, bwprow], writes=[bBP])
                p.evac_copy(Wp[:, 0:nch], BP[:, 0:nch], [bBP], [bWp])
                k.op("dve", lambda: nc.vector.tensor_tensor(out=KH[:, :, 0:T], in0=kT[:, :, 0:T], in1=WSbc[:, 0:T].unsqueeze(1).broadcast_to([128, 2, T]), op=ALU.mult),
                     reads=[bkT, bWSbc], writes=[bKH])
                moff = 64 if d == 0 else 128
                NP, bNP = p.ps[2], p.bps[2]
                UV0, bUV0 = p.ps[3], p.bps[3]
                UV1, bUV1 = p.ps[4], p.bps[4]
                UN, bUN = p.ps[5], p.bps[5]

                def S1(c, par):
                    cs = slice(c * 64, (c + 1) * 64)
                    KTp, bKTp = KTps[par]
                    Kt, bKt = Kts[par]
                    A, bA = As[par]
                    ST, bST = STs[par]
                    for dc in range(2):
                        k.op("pe", lambda dc=dc: nc.tensor.transpose(out=KTp[0:64, dc * 128:(dc + 1) * 128], in_=KH[:, dc, cs], identity=p.ident[:, :]),
                             reads=[bKH, p.b_ident], writes=[bKTp], inc=(dc == 1))
                    for dc in range(2):
                        k.op("pe", lambda dc=dc: nc.tensor.matmul(ST[0:64, 0:64], lhsT=KH[:, dc, cs], rhs=qT[:, dc, cs], start=(dc == 0), stop=(dc == 1)),
                             reads=[bKH, bqT], writes=[bST], inc=(dc == 1))
                    k.op("act", lambda: nc.scalar.copy(out=Kt[:, :], in_=KTp[0:64, 0:256]), reads=[bKTp], writes=[bKt])
                    k.op("dve", lambda: nc.vector.tensor_tensor(out=A[:, :], in0=ST[0:64, 0:64], in1=p.tri[0:64, moff:moff + 64], op=ALU.mult), reads=[bST, p.b_tri], writes=[bA])

                def S3(c, par, cb):
                    cs = slice(c * 64, (c + 1) * 64)
                    Kt, bKt = Kts[par]
                    A, bA = As[par]
                    Cbf, bCbf = Cbfs[cb]
                    DN, bDN = DNs[cb]
                    k.op("act", lambda: nc.scalar.activation(out=Cbf[:, :, :], in_=Cst[:, :, :], func=AF.Copy, scale=Wp[:, c:c + 1]), reads=[bCst, bWp], writes=[bCbf])
                    for dc, (UV, bUV) in enumerate(((UV0, bUV0), (UV1, bUV1))):
                        k.op("pe", lambda dc=dc, UV=UV: nc.tensor.matmul(UV[:, 0:512], lhsT=Kt[:, dc * 128:(dc + 1) * 128], rhs=Vp[:, c, 0:512], start=True, stop=True),
                             reads=[bKt, bVp], writes=[bUV])
                        k.op("pe", lambda dc=dc: nc.tensor.matmul(UN[:, dc * 128:(dc + 1) * 128], lhsT=Kt[:, dc * 128:(dc + 1) * 128], rhs=Vp[:, c, 512:640], start=True, stop=True),
                             reads=[bKt, bVp], writes=[bUN])
                    for dc, (UV, bUV) in enumerate(((UV0, bUV0), (UV1, bUV1))):
                        k.op("dve", lambda dc=dc, UV=UV: nc.vector.scalar_tensor_tensor(out=Cst[:, dc, 0:512], in0=Cst[:, dc, 0:512], scalar=Wp[:, c:c + 1], in1=UV[:, 0:512],
                                                                                       op0=ALU.mult, op1=ALU.add),
                             reads=[bCst, bWp, bUV], writes=[bCst])
                    k.op("dve", lambda: nc.vector.scalar_tensor_tensor(out=Cst[:, :, 512:640], in0=Cst[:, :, 512:640], scalar=Wp[:, c:c + 1],
                                                                       in1=UN[:, 0:256].rearrange("p (a b) -> p a b", b=128), op0=ALU.mult, op1=ALU.add),
                         reads=[bCst, bWp, bUN], writes=[bCst])
                    for vj in range(5):
                        k.op("pe", lambda vj=vj: nc.tensor.matmul(NP[:, vj * 64:(vj + 1) * 64], lhsT=Vp[:, c, vj * 128:(vj + 1) * 128], rhs=A[:, :], start=True, stop=False),
                             reads=[bVp, bA], writes=[bNP], inc=False)
                        k.op("pe", lambda vj=vj: nc.tensor.matmul(NP[:, vj * 64:(vj + 1) * 64], lhsT=Cbf[:, 0, vj * 128:(vj + 1) * 128], rhs=qT[:, 0, cs], start=False, stop=False),
                             reads=[bCbf, bqT], writes=[bNP], inc=False)
                        k.op("pe", lambda vj=vj: nc.tensor.matmul(NP[:, vj * 64:(vj + 1) * 64], lhsT=Cbf[:, 1, vj * 128:(vj + 1) * 128], rhs=qT[:, 1, cs], start=False, stop=True),
                             reads=[bCbf, bqT], writes=[bNP], inc=(vj == 4))
                    k.op("act", lambda: nc.scalar.activation(out=DN[:, :], in_=NP[:, 256:320], func=AF.Abs), reads=[bNP], writes=[bDN])
                    k.op("dve", lambda: nc.vector.tensor_tensor(out=DN[:, :], in0=DN[:, :], in1=Ebc[:, cs], op=ALU.max), reads=[bDN, bEbc], writes=[bDN])
                    k.op("dve", lambda: nc.vector.reciprocal(out=DN[:, :], in_=DN[:, :]), reads=[bDN], writes=[bDN])
                    np3 = NP[:, 0:256].rearrange("p (v t) -> p v t", t=64)
                    dnb = DN[:, :].unsqueeze(1).broadcast_to([128, 4, 64])
                    if d == 0:
                        k.op("dve", lambda: nc.vector.tensor_tensor(out=O[:, :, cs], in0=np3, in1=dnb, op=ALU.mult), reads=[bNP, bDN], writes=[bO])
                    else:
                        k.op("dve", lambda: nc.vector.tensor_tensor(out=TH[:, :, :], in0=np3, in1=dnb, op=ALU.mult), reads=[bNP, bDN], writes=[bTH])
                        k.op("pool", lambda: nc.gpsimd.tensor_tensor(out=O[:, :, cs], in0=TH[:, :, :], in1=Ofw[:, :, cs], op=ALU.add), reads=[bTH, bOfw], writes=[bO])

                clist = list(crange)
                S1(clist[0], cpar % 2)
                for idx, c in enumerate(clist):
                    par = (cpar + idx) % 2
                    if idx + 1 < len(clist):
                        S1(clist[idx + 1], 1 - par)
                    S3(c, par, idx % 2)
                cpar += len(clist)
                if d == 0:
                    k.dma("pool", mkap(ml_h, s0, [[S, 128], [128 * S, 4], [1, T]]), O[:, :, 0:T], b_mlh, bO)
                else:
                    p.head_post(O, bO, 4, T, SQ, bSQ, R, bR, OG, bOG, lambda vj: mlg[:, vj:vj + 1], bmlg, OUTb, bOUTb, o_m, b_om, 0, s0)
        k.barrier()
    return p.finish()


def rope_tables(ROWS):
    n_freq = 64
    inv = (10000.0 ** (-np.arange(n_freq, dtype=np.float32) / n_freq)).astype(np.float32)
    sign = np.concatenate([-np.ones(64, np.float32), np.ones(64, np.float32)])
    f2 = np.concatenate([inv, inv])
    rows = np.arange(ROWS, dtype=np.float32)
    cols = np.arange(64, dtype=np.float32)
    angR = (rows[None, :] * f2[:, None]).astype(np.float32)
    angC = (cols[None, :] * f2[:, None]).astype(np.float32)
    ropeR = np.concatenate([np.cos(angR), sign[:, None] * np.sin(angR)], axis=1).astype(np.float32)
    ropeC = np.concatenate([np.cos(angC), sign[:, None] * np.sin(angC)], axis=1).astype(np.float32)
    return ropeR, ropeC


def prep_odd_win(w, h):
    perm = np.concatenate([np.arange(64, 128), np.arange(0, 64), np.arange(192, 256), np.arange(128, 192)])
    q0, k0, v0, o0, g0 = h * 256, 2048 + h * 256, 4096 + h * 512, 8192 + h * 512, 12288
    cols = np.concatenate([q0 + np.arange(256), q0 + perm, k0 + np.arange(256), k0 + perm, o0 + np.arange(512), v0 + np.arange(512)])
    wg = np.ascontiguousarray(w[:, [g0 + h, g0 + 8 + h, g0 + 16 + h, g0 + 24 + h]])
    return np.ascontiguousarray(w[:, cols]), wg


ROWS_FULL = 256
SEQ_FULL = ROWS_FULL * 64
CTX_LEN = 256
S_FULL = SEQ_FULL + CTX_LEN
NCT = 8
FF_FULL = 11008
DEPTH = 4
_PROGS = {}


def _prog(name, fn):
    if name not in _PROGS:
        _PROGS[name] = fn()
    return _PROGS[name]


def _run(nc, in_maps):
    res = run_bass_kernel_spmd(nc, in_maps, core_ids=list(range(len(in_maps))))
    return res.results


def kernel(x, c, ctx, c_ctx, norm_mix_g, norm_ffn_g, mod_w_a, mod_w_b, mod_b,
           even_w_in, even_w_out, na_rpb, hg_lb_logits, hg_norm_g,
           odd_w_in, odd_b_gates, odd_w_out, ml_norm_g,
           ffn_w_up, ffn_w_down, final_norm_g):
    f = lambda a: np.asarray(a, dtype=np.float32)
    x, c, ctx, c_ctx = f(x), f(c), f(ctx), f(c_ctx)
    norm_mix_g, norm_ffn_g, mod_w_a, mod_w_b, mod_b = f(norm_mix_g), f(norm_ffn_g), f(mod_w_a), f(mod_w_b), f(mod_b)
    even_w_in, even_w_out, na_rpb, hg_lb_logits, hg_norm_g = f(even_w_in), f(even_w_out), f(na_rpb), f(hg_lb_logits), f(hg_norm_g)
    odd_w_in, odd_b_gates, odd_w_out, ml_norm_g = f(odd_w_in), f(odd_b_gates), f(odd_w_out), f(ml_norm_g)
    ffn_w_up, ffn_w_down, final_norm_g = f(ffn_w_up), f(ffn_w_down), f(final_norm_g)
    SEQ, S, ROWS = SEQ_FULL, S_FULL, ROWS_FULL
    TL, TC = SEQ // NCT, CTX_LEN // NCT
    tri, rst, rst64, ident = const_tables()
    ropeR, ropeC = rope_tables(ROWS)

    cvec = np.concatenate([col_layout(c[0]), col_layout(c_ctx)], axis=1)
    ncM = _prog("M", build_M)
    rM = _run(ncM, [{"wa": mod_w_a[l], "wb": mod_w_b[l], "modb": col_layout(mod_b[l]), "cvec": cvec} for l in range(DEPTH)])
    modT = [rM[l]["modT"] for l in range(DEPTH)]

    xT = np.ascontiguousarray(x[0].T)
    cT = np.ascontiguousarray(ctx[0].T)
    xloc = [np.ascontiguousarray(np.concatenate([xT[:, t * TL:(t + 1) * TL], cT[:, t * TC:(t + 1) * TC]], axis=1)) for t in range(NCT)]
    del xT, cT

    def gather_hx(hs):
        out = np.empty((D, S), NPBF)
        for t in range(NCT):
            out[:, t * TC:(t + 1) * TC] = hs[t][:, TL:TL + TC]
            out[:, CTX_LEN + t * TL:CTX_LEN + (t + 1) * TL] = hs[t][:, 0:TL]
        return out

    ncA = _prog("A", lambda: build_A(TL, TC, False))
    rA = _run(ncA, [{"xT": xloc[t], "modT": modT[0], "ng": col_layout(norm_mix_g[0])} for t in range(NCT)])
    hx_all = gather_hx([rA[t]["hx"] for t in range(NCT)])
    del rA

    for l in range(DEPTH):
        j = l // 2
        if l % 2 == 0:
            ncP = _prog("PE", lambda: build_PE(ROWS))
            ims = []
            for cc in range(8):
                lbl, jsel, hg = even_small_params(hg_lb_logits, j, hg_norm_g, cc)
                ims.append({"hx": hx_all, "win": prep_even_win(even_w_in[j], cc), "lbl": lbl, "jsel": jsel, "hgg": hg,
                            "nab": na_bias_tables(na_rpb[j], cc, ROWS), "rst": rst, "ident": ident, "tri": tri})
            rP = _run(ncP, ims)
            del ims
            M_all = np.empty((D, S), NPBF)
            for cc in range(8):
                M_all[256 * cc:256 * cc + 256] = rP[cc]["mloc"][0:256]
                M_all[2048 + 256 * cc:2048 + 256 * cc + 256] = rP[cc]["mloc"][256:512]
            wout = even_w_out[j]
        else:
            ncP = _prog("PO", lambda: build_PO(ROWS))
            ims = []
            for h in range(8):
                win, wg = prep_odd_win(odd_w_in[j], h)
                ims.append({"hx": hx_all, "win": win, "wg": wg, "bg": np.ascontiguousarray(odd_b_gates[j][[h, 8 + h, 16 + h, 24 + h]][None]),
                            "mlg": col_layout(ml_norm_g[j][h * 512:(h + 1) * 512]), "ropeR": ropeR, "ropeC": ropeC, "rst64": rst64, "ident": ident, "tri": tri})
            rP = _run(ncP, ims)
            del ims
            M_all = np.empty((D, S), NPBF)
            for h in range(8):
                M_all[512 * h:512 * h + 512] = rP[h]["mloc"]
            wout = odd_w_out[j]
        del rP
        last = (l == DEPTH - 1)
        modn = np.zeros_like(modT[0]) if last else modT[l + 1]
        ngn = col_layout(final_norm_g) if last else col_layout(norm_mix_g[l + 1])
        ncT = _prog("T", lambda: build_T(TL, TC, FF_FULL))
        ims = []
        for t in range(NCT):
            mloc = np.ascontiguousarray(np.concatenate([M_all[:, CTX_LEN + t * TL:CTX_LEN + (t + 1) * TL], M_all[:, t * TC:(t + 1) * TC]], axis=1))
            ims.append({"xT": xloc[t], "mT": mloc, "wout": wout, "wup": ffn_w_up[l], "wdn": ffn_w_down[l], "modT": modT[l],
                        "nfg": col_layout(norm_ffn_g[l]), "modTn": modn, "ngn": ngn})
        del M_all
        rT = _run(ncT, ims)
        del ims
        xloc = [rT[t]["xo"] for t in range(NCT)]
        if not last:
            hx_all = gather_hx([rT[t]["hxo"] for t in range(NCT)])
        del rT

    ncF = _prog("F", lambda: build_A(TL, TC, True))
    zero_mod = np.zeros_like(modT[0])
    rF = _run(ncF, [{"xT": xloc[t], "modT": zero_mod, "ng": col_layout(final_norm_g)} for t in range(NCT)])
    out = np.empty((1, SEQ, D), np.float32)
    for t in range(NCT):
        out[0, t * TL:(t + 1) * TL, :] = rF[t]["hx"][:, 0:TL].T
    return out
```

```python
import numpy as np
import ml_dtypes
from contextlib import ExitStack
import concourse.bass as bass
import concourse.mybir as mybir
from concourse.bass_utils import run_bass_kernel_spmd

F32 = mybir.dt.float32
BF16 = mybir.dt.bfloat16
AF = mybir.ActivationFunctionType
ALU = mybir.AluOpType
AX = mybir.AxisListType
NPBF = ml_dtypes.bfloat16

D = 4096
KC = 32
EPS = 1e-6
import os as _os
SKIP_SELF_WAIT = _os.environ.get('KB_NOSELF', '0') == '1'


class Buf:
    __slots__ = ("name", "w", "r", "dkey")

    def __init__(self, name):
        self.name = name
        self.w = {}
        self.r = {}
        self.dkey = None


class KB:
    def __init__(self, nc):
        self.nc = nc
        self.eng = {"pe": nc.tensor, "act": nc.scalar, "dve": nc.vector, "pool": nc.gpsimd, "sp": nc.sync}
        self.semh = {}
        self.total = {}
        for e in ("pe", "act", "dve", "pool"):
            self.semh[e] = nc.alloc_semaphore(name="sem_" + e)
            self.total[e] = 0
        self.waited = {e: {} for e in self.eng}
        self.pending = {e: ([], []) for e in self.eng}
        self.ninstr = 0

    def buf(self, name):
        return Buf(name)

    def _wait(self, e, key, val):
        if key == e and (e == "pe" or SKIP_SELF_WAIT):
            return
        if self.waited[e].get(key, 0) >= val:
            return
        self.eng[e].wait_ge(self.semh[key], val)
        self.waited[e][key] = val
        self.ninstr += 1

    def _deps(self, e, reads, writes):
        for b in reads:
            for k, v in b.w.items():
                self._wait(e, k, v)
        for b in writes:
            for k, v in b.w.items():
                self._wait(e, k, v)
            for k, v in b.r.items():
                self._wait(e, k, v)

    def op(self, e, fn, reads=(), writes=(), inc=True):
        self._deps(e, reads, writes)
        ins = fn()
        self.ninstr += 1
        pr, pw = self.pending[e]
        if not inc:
            pr.extend(reads)
            pw.extend(writes)
            return ins
        self.total[e] += 1
        ins.then_inc(self.semh[e], 1)
        v = self.total[e]
        for b in list(reads) + pr:
            b.r[e] = v
        for b in list(writes) + pw:
            b.w[e] = v
            b.r = {}
        pr.clear()
        pw.clear()
        return ins

    def dma(self, q, out, in_, ob, ib, **kw):
        if ob.dkey is None:
            ob.dkey = "d_" + ob.name
            self.semh[ob.dkey] = self.nc.alloc_semaphore(name=ob.dkey)
            self.total[ob.dkey] = 0
        self._deps(q, [ib], [ob])
        self.eng[q].dma_start(out=out, in_=in_, **kw).then_inc(self.semh[ob.dkey], 16)
        self.ninstr += 1
        self.total[ob.dkey] += 16
        v = self.total[ob.dkey]
        ib.r[ob.dkey] = v
        ob.w[ob.dkey] = v
        ob.r = {}

    def barrier(self):
        for e in self.eng:
            for k, v in self.total.items():
                if v > 0:
                    self._wait(e, k, v)


def mkap(h, off, pat):
    return bass.AP(h, off, [list(p) for p in pat])


class Base:
    def __init__(self):
        nc = bass.Bass("TRN2", target_bir_lowering=False)
        self.nc = nc
        self.k = KB(nc)
        self.b_in = self.k.buf("ext_in")
        self.es = ExitStack()
        self.ps = []
        self.bps = []
        for i in range(8):
            t = self.es.enter_context(nc.psum_tensor(f"ps{i}", [128, 512], F32))
            self.ps.append(t)
            self.bps.append(self.k.buf(f"ps{i}"))
        self.nbuf = 0

    def din(self, name, shape, dt=F32):
        return self.nc.dram_tensor(name, list(shape), dt, kind="ExternalInput")

    def dout(self, name, shape, dt=F32):
        return self.nc.dram_tensor(name, list(shape), dt, kind="ExternalOutput")

    def dint(self, name, shape, dt):
        return self.nc.dram_tensor(name, list(shape), dt), self.k.buf(name)

    def sb(self, es, name, shape, dt):
        self.nbuf += 1
        name = f"s{self.nbuf}_{name}"
        t = es.enter_context(self.nc.sbuf_tensor(name, list(shape), dt))
        return t, self.k.buf(name)

    def consts(self):
        nc, k = self.nc, self.k
        self.ones, self.b_ones = self.sb(self.es, "ones", [128, 128], BF16)
        k.op("dve", lambda: nc.vector.memset(self.ones[:, :], 1.0), writes=[self.b_ones])

    def finish(self):
        self.k.barrier()
        self.es.close()
        return self.nc

    def cast_tiles(self, src_h, K, N, dst_h, b_dst, es):
        nc, k = self.nc, self.k
        nk = K // 128
        CW = 512 if N % 512 == 0 else 128
        G = 8
        ncc = CW // 128
        if not hasattr(self, "_cst"):
            self._cst = [self.sb(es, f"cst{i}", [128, G, 512], F32) for i in range(3)]
            self._csb = [self.sb(es, f"csb{i}", [128, 4, G, 128], BF16) for i in range(3)]
            self._cit = 0
        for kc0 in range(0, nk, G):
            gn = min(G, nk - kc0)
            for c0 in range(0, N, CW):
                it = self._cit
                self._cit += 1
                a, ba = self._cst[it % 3]
                b, bb = self._csb[it % 3]
                k.dma("sp", a[:, 0:gn, 0:CW], mkap(src_h, kc0 * 128 * N + c0, [[N, 128], [128 * N, gn], [1, CW]]), ba, self.b_in)
                src = a[:, 0:gn, 0:CW].rearrange("p g (a c) -> p g a c", c=128)
                dstv = b[:, 0:ncc, 0:gn, :].rearrange("p a g c -> p g a c")
                e = ("act", "dve", "pool")[it % 3]
                if e == "act":
                    k.op("act", lambda: nc.scalar.copy(out=dstv, in_=src), reads=[ba], writes=[bb])
                elif e == "dve":
                    k.op("dve", lambda: nc.vector.tensor_copy(out=dstv, in_=src), reads=[ba], writes=[bb])
                else:
                    k.op("pool", lambda: nc.gpsimd.tensor_copy(out=dstv, in_=src), reads=[ba], writes=[bb])
                cc0 = c0 // 128
                dst = mkap(dst_h, cc0 * 128 * nk * 128 + kc0 * 128, [[nk * 128, 128], [128 * nk * 128, ncc], [1, gn * 128]])
                k.dma("pool", dst, b[:, 0:ncc, 0:gn, :].rearrange("p a g c -> p a (g c)"), b_dst, bb)

    def wtile_view(self, dst_h, nk, cc, k0=0, kn=None):
        kn = nk if kn is None else kn
        return mkap(dst_h, cc * 128 * nk * 128 + k0 * 128, [[nk * 128, 128], [128, kn], [1, 128]])

    def normmod(self, X, bX, SQ, bSQ, OUT, bOUT, R, bR, tmpk, n, gs_ap, sh_ap, bmods):
        nc, k = self.nc, self.k
        P, bP = self.ps[7], self.bps[7]
        k.op("act", lambda: nc.scalar.activation(out=SQ[:, :, 0:n], in_=X[:, :, 0:n], func=AF.Square), reads=[bX], writes=[bSQ])
        for kc in range(KC):
            k.op("pe", lambda kc=kc: nc.tensor.matmul(P[:, 0:n], lhsT=self.ones[:, :], rhs=SQ[:, kc, 0:n], start=(kc == 0), stop=(kc == KC - 1)),
                 reads=[bSQ, self.b_ones], writes=[bP], inc=(kc == KC - 1))
        k.op("dve", lambda: nc.vector.tensor_scalar(out=R[:, 0:n], in0=P[:, 0:n], scalar1=1.0 / D, scalar2=EPS, op0=ALU.mult, op1=ALU.add),
             reads=[bP], writes=[bR])
        k.op("act", lambda: nc.scalar.activation(out=R[:, 0:n], in_=R[:, 0:n], func=AF.Sqrt), reads=[bR], writes=[bR])
        k.op("dve", lambda: nc.vector.reciprocal(out=R[:, 0:n], in_=R[:, 0:n]), reads=[bR], writes=[bR])
        for kc in range(KC):
            tk, btk = tmpk[kc % len(tmpk)]
            k.op("dve", lambda kc=kc, tk=tk: nc.vector.tensor_tensor(out=tk[:, 0:n], in0=X[:, kc, 0:n], in1=R[:, 0:n], op=ALU.mult),
                 reads=[bX, bR], writes=[btk])
            k.op("act", lambda kc=kc, tk=tk: nc.scalar.activation(out=OUT[:, kc, 0:n], in_=tk[:, 0:n], func=AF.Identity,
                                                                  scale=gs_ap(kc), bias=sh_ap(kc)),
                 reads=[btk] + bmods, writes=[bOUT])

    def make_gs(self, gs, bgs, modT, bmod, sc0, ng, bng, tmp, btmp):
        nc, k = self.nc, self.k
        k.op("dve", lambda: nc.vector.tensor_scalar(out=tmp[:, :, :], in0=modT[:, sc0:sc0 + 32, :], scalar1=1.0, scalar2=None, op0=ALU.add),
             reads=[bmod], writes=[btmp])
        k.op("dve", lambda: nc.vector.tensor_tensor(out=gs[:, :, :], in0=tmp[:, :, :], in1=ng[:, 0:32].unsqueeze(2).broadcast_to([128, 32, 2]), op=ALU.mult),
             reads=[btmp, bng], writes=[bgs])


def tblocks(TL, TC, NB=256):
    blocks = []
    for t0 in range(0, TL, NB):
        blocks.append((t0, min(NB, TL - t0), 0))
    blocks.append((TL, TC, 1))
    return blocks


def fm_view(h, ncols_total, t0, n):
    return mkap(h, t0, [[ncols_total, 128], [128 * ncols_total, KC], [1, n]])


def build_M():
    p = Base()
    nc, k = p.nc, p.k
    i_wa = p.din("wa", [D, 256])
    i_wb = p.din("wb", [256, 6 * D])
    i_modb = p.din("modb", [128, 192])
    i_cvec = p.din("cvec", [128, 64])
    o_mod = p.dout("modT", [128, 384])
    b_out = k.buf("o_mod")
    es = p.es
    wa, b_wa = p.sb(es, "wa", [128, 32, 256], F32)
    wb, b_wb = p.sb(es, "wb", [128, 2, 6144], F32)
    cv, b_cv = p.sb(es, "cv", [128, 64], F32)
    cT, b_cT = p.sb(es, "cT", [128, 32, 2], F32)
    hT, b_hT = p.sb(es, "hT", [128, 2, 2], F32)
    mb, b_mb = p.sb(es, "mb", [128, 192], F32)
    mo, b_mo = p.sb(es, "mo", [128, 192, 2], F32)
    k.dma("sp", cv[:, :], i_cvec[:, :], b_cv, p.b_in)
    k.dma("sp", mb[:, :], i_modb[:, :], b_mb, p.b_in)
    k.dma("sp", wa[:, :, :], mkap(i_wa, 0, [[256, 128], [128 * 256, 32], [1, 256]]), b_wa, p.b_in)
    for c in range(2):
        k.op("act", lambda c=c: nc.scalar.activation(out=cT[:, :, c], in_=cv[:, c * 32:(c + 1) * 32], func=AF.Silu), reads=[b_cv], writes=[b_cT])
    P0, bP0 = p.ps[0], p.bps[0]
    P1, bP1 = p.ps[1], p.bps[1]
    for r in range(2):
        for kc in range(32):
            k.op("pe", lambda r=r, kc=kc: nc.tensor.matmul(P0[:, r * 2:(r + 1) * 2], lhsT=wa[:, kc, r * 128:(r + 1) * 128], rhs=cT[:, kc, :],
                                                           start=(kc == 0), stop=(kc == 31)),
                 reads=[b_wa, b_cT], writes=[bP0], inc=(kc == 31))
    k.op("dve", lambda: nc.vector.tensor_copy(out=hT[:, :, :], in_=P0[:, 0:4].rearrange("p (r c) -> p r c", c=2)), reads=[bP0], writes=[b_hT])
    for piece in range(4):
        k.dma("sp", wb[:, :, :], mkap(i_wb, piece * 6144, [[6 * D, 128], [128 * 6 * D, 2], [1, 6144]]), b_wb, p.b_in)
        for jj in range(48):
            jg = piece * 48 + jj
            for r in range(2):
                k.op("pe", lambda r=r, jj=jj, jg=jg: nc.tensor.matmul(P1[:, jg * 2:(jg + 1) * 2], lhsT=wb[:, r, jj * 128:(jj + 1) * 128], rhs=hT[:, r, :],
                                                                      start=(r == 0), stop=(r == 1)),
                     reads=[b_wb, b_hT], writes=[bP1], inc=(r == 1 and jj == 47))
    k.op("dve", lambda: nc.vector.tensor_tensor(out=mo[:, :, :], in0=P1[:, 0:384].rearrange("p (j c) -> p j c", c=2),
                                                in1=mb[:, :].unsqueeze(2).broadcast_to([128, 192, 2]), op=ALU.add),
         reads=[bP1, b_mb], writes=[b_mo])
    k.dma("pool", o_mod[:, :], mo[:, :, :].rearrange("p j c -> p (j c)"), b_out, b_mo)
    return p.finish()


def build_A(TL, TC, out_f32):
    p = Base()
    nc, k = p.nc, p.k
    TLOC = TL + TC
    i_x = p.din("xT", [D, TLOC])
    i_mod = p.din("modT", [128, 384])
    i_ng = p.din("ng", [128, 32])
    odt = F32 if out_f32 else BF16
    o_hx = p.dout("hx", [D, TLOC], odt)
    b_out = k.buf("o_hx")
    es = p.es
    p.consts()
    modT, b_mod = p.sb(es, "modT", [128, 192, 2], F32)
    ng, b_ng = p.sb(es, "ng", [128, 32], F32)
    gs, b_gs = p.sb(es, "gs", [128, 32, 2], F32)
    tmp, b_tmp = p.sb(es, "tmp", [128, 32, 2], F32)
    k.dma("sp", modT[:, :, :].rearrange("p j c -> p (j c)"), i_mod[:, :], b_mod, p.b_in)
    k.dma("sp", ng[:, :], i_ng[:, :], b_ng, p.b_in)
    p.make_gs(gs, b_gs, modT, b_mod, 32, ng, b_ng, tmp, b_tmp)
    Xs = [p.sb(es, f"X{i}", [128, 32, 256], F32) for i in range(2)]
    SQ, bSQ = p.sb(es, "SQ", [128, 32, 256], BF16)
    OUTs = [p.sb(es, f"O{i}", [128, 32, 256], odt) for i in range(2)]
    R, bR = p.sb(es, "R", [128, 256], F32)
    tmpk = [p.sb(es, f"tk{i}", [128, 256], F32) for i in range(3)]
    for bi, (t0, n, cond) in enumerate(tblocks(TL, TC)):
        X, bX = Xs[bi % 2]
        O, bO = OUTs[bi % 2]
        k.dma("sp", X[:, :, 0:n], fm_view(i_x, TLOC, t0, n), bX, p.b_in)
        p.normmod(X, bX, SQ, bSQ, O, bO, R, bR, tmpk, n,
                  lambda kc, cond=cond: gs[:, kc, cond:cond + 1], lambda kc, cond=cond: modT[:, kc, cond:cond + 1], [b_gs, b_mod])
        k.dma("pool", fm_view(o_hx, TLOC, t0, n), O[:, :, 0:n], b_out, bO)
    return p.finish()


def build_T(TL, TC, FF):
    p = Base()
    nc, k = p.nc, p.k
    TLOC = TL + TC
    HC = FF // 128
    i_x = p.din("xT", [D, TLOC])
    i_m = p.din("mT", [D, TLOC], BF16)
    woutb = p.din("woutb", [32 * 128 * 32 * 128], BF16)
    wupb = p.din("wupb", [(2 * FF // 128) * 128 * 32 * 128], BF16)
    wdnb = p.din("wdnb", [32 * 128 * HC * 128], BF16)
    b_woutb = b_wupb = b_wdnb = p.b_in
    i_mod = p.din("modT", [128, 384])
    i_nfg = p.din("nfg", [128, 32])
    i_modn = p.din("modTn", [128, 384])
    i_ngn = p.din("ngn", [128, 32])
    o_x = p.dout("xo", [D, TLOC], F32)
    o_hx = p.dout("hxo", [D, TLOC], BF16)
    b_ox = k.buf("o_x")
    b_ohx = k.buf("o_hx")
    es = p.es
    p.consts()
    modT, b_mod = p.sb(es, "modT", [128, 192, 2], F32)
    modn, b_modn = p.sb(es, "modn", [128, 192, 2], F32)
    nfg, b_nfg = p.sb(es, "nfg", [128, 32], F32)
    ngn, b_ngn = p.sb(es, "ngn", [128, 32], F32)
    gs2, b_gs2 = p.sb(es, "gs2", [128, 32, 2], F32)
    gsn, b_gsn = p.sb(es, "gsn", [128, 32, 2], F32)
    tmp, b_tmp = p.sb(es, "tmp", [128, 32, 2], F32)
    k.dma("sp", modT[:, :, :].rearrange("p j c -> p (j c)"), i_mod[:, :], b_mod, p.b_in)
    k.dma("sp", modn[:, :, :].rearrange("p j c -> p (j c)"), i_modn[:, :], b_modn, p.b_in)
    k.dma("sp", nfg[:, :], i_nfg[:, :], b_nfg, p.b_in)
    k.dma("sp", ngn[:, :], i_ngn[:, :], b_ngn, p.b_in)
    p.make_gs(gs2, b_gs2, modT, b_mod, 128, nfg, b_nfg, tmp, b_tmp)
    p.make_gs(gsn, b_gsn, modn, b_modn, 32, ngn, b_ngn, tmp, b_tmp)
    NB = 416
    hh = (HC + 1) // 2
    X, bX = p.sb(es, "X", [128, 32, NB], F32)
    HX, bHX = p.sb(es, "HX", [128, 32, NB], BF16)
    Fh, bF = p.sb(es, "F", [128, max(hh, 32), NB], BF16)
    MT, bMT = Fh, bF
    R, bR = p.sb(es, "R", [128, NB], F32)
    tmpk = [p.sb(es, f"tk{i}", [128, NB], F32) for i in range(3)]
    W8 = [p.sb(es, f"W8_{i}", [128, 32, 128], BF16) for i in range(4)]
    WD = [p.sb(es, f"WD_{i}", [128, hh, 128], BF16) for i in range(2)]
    SG, bSG = p.sb(es, "SG", [128, NB], F32)
    cnt = {"w8": 0, "wd": 0}

    def load_w8(h, bh, cc):
        W, bW = W8[cnt["w8"] % len(W8)]
        cnt["w8"] += 1
        k.dma("sp", W[:, :, :], p.wtile_view(h, 32, cc), bW, bh)
        return W, bW

    blocks = []
    t0 = 0
    while t0 < TLOC:
        n = min(NB, TLOC - t0)
        segs = []
        if t0 < TL:
            segs.append((0, min(n, TL - t0), 0))
        if t0 + n > TL:
            c0 = max(TL - t0, 0)
            segs.append((c0, n - c0, 1))
        blocks.append((t0, n, segs))
        t0 += n

    def resid(P, bP, fo, n, segs, gbase):
        for (c0, cn, cond) in segs:
            k.op("dve", lambda: nc.vector.scalar_tensor_tensor(out=X[:, fo, c0:c0 + cn], in0=P[:, c0:c0 + cn], scalar=modT[:, gbase + fo, cond:cond + 1],
                                                               in1=X[:, fo, c0:c0 + cn], op0=ALU.mult, op1=ALU.add),
                 reads=[bP, b_mod, bX], writes=[bX])

    def normmod_seg(n, segs, gs, bgs, md, bmd, sh0):
        P, bP = p.ps[7], p.bps[7]
        k.op("act", lambda: nc.scalar.activation(out=HX[:, :, 0:n], in_=X[:, :, 0:n], func=AF.Square), reads=[bX], writes=[bHX])
        for kc in range(KC):
            k.op("pe", lambda: nc.tensor.matmul(P[:, 0:n], lhsT=p.ones[:, :], rhs=HX[:, kc, 0:n], start=(kc == 0), stop=(kc == KC - 1)),
                 reads=[bHX, p.b_ones], writes=[bP], inc=(kc == KC - 1))
        k.op("dve", lambda: nc.vector.tensor_scalar(out=R[:, 0:n], in0=P[:, 0:n], scalar1=1.0 / D, scalar2=EPS, op0=ALU.mult, op1=ALU.add), reads=[bP], writes=[bR])
        k.op("act", lambda: nc.scalar.activation(out=R[:, 0:n], in_=R[:, 0:n], func=AF.Sqrt), reads=[bR], writes=[bR])
        k.op("dve", lambda: nc.vector.reciprocal(out=R[:, 0:n], in_=R[:, 0:n]), reads=[bR], writes=[bR])
        for kc in range(KC):
            tk, btk = tmpk[kc % len(tmpk)]
            k.op("dve", lambda: nc.vector.tensor_tensor(out=tk[:, 0:n], in0=X[:, kc, 0:n], in1=R[:, 0:n], op=ALU.mult), reads=[bX, bR], writes=[btk])
            for (c0, cn, cond) in segs:
                k.op("act", lambda: nc.scalar.activation(out=HX[:, kc, c0:c0 + cn], in_=tk[:, c0:c0 + cn], func=AF.Identity,
                                                         scale=gs[:, kc, cond:cond + 1], bias=md[:, sh0 + kc, cond:cond + 1]),
                     reads=[btk, bgs, bmd], writes=[bHX])

    for bi, (t0, n, segs) in enumerate(blocks):
        k.dma("sp", X[:, :, 0:n], fm_view(i_x, TLOC, t0, n), bX, p.b_in)
        k.dma("sp", MT[:, 0:32, 0:n], fm_view(i_m, TLOC, t0, n), bMT, p.b_in)
        for fo in range(32):
            W, bW = load_w8(woutb, b_woutb, fo)
            P, bP = p.ps[fo % 2], p.bps[fo % 2]
            for kc in range(32):
                k.op("pe", lambda: nc.tensor.matmul(P[:, 0:n], lhsT=W[:, kc, :], rhs=MT[:, kc, 0:n], start=(kc == 0), stop=(kc == 31)),
                     reads=[bW, bMT], writes=[bP], inc=(kc == 31))
            resid(P, bP, fo, n, segs, 64)
        normmod_seg(n, segs, gs2, b_gs2, modT, b_mod, 96)
        for half in range(2):
            h0 = half * hh
            hn = min(hh, HC - h0)
            if hn <= 0:
                continue
            for i in range(hn):
                hc = h0 + i
                Wa, bWa = load_w8(wupb, b_wupb, hc)
                Wg, bWg = load_w8(wupb, b_wupb, HC + hc)
                Pa, bPa = p.ps[2 + (hc % 2) * 2], p.bps[2 + (hc % 2) * 2]
                Pg, bPg = p.ps[3 + (hc % 2) * 2], p.bps[3 + (hc % 2) * 2]
                for kc in range(32):
                    k.op("pe", lambda: nc.tensor.matmul(Pa[:, 0:n], lhsT=Wa[:, kc, :], rhs=HX[:, kc, 0:n], start=(kc == 0), stop=(kc == 31)),
                         reads=[bWa, bHX], writes=[bPa], inc=(kc == 31))
                for kc in range(32):
                    k.op("pe", lambda: nc.tensor.matmul(Pg[:, 0:n], lhsT=Wg[:, kc, :], rhs=HX[:, kc, 0:n], start=(kc == 0), stop=(kc == 31)),
                         reads=[bWg, bHX], writes=[bPg], inc=(kc == 31))
                k.op("act", lambda: nc.scalar.activation(out=SG[:, 0:n], in_=Pg[:, 0:n], func=AF.Silu), reads=[bPg], writes=[bSG])
                k.op("dve", lambda: nc.vector.tensor_tensor(out=Fh[:, i, 0:n], in0=Pa[:, 0:n], in1=SG[:, 0:n], op=ALU.mult), reads=[bPa, bSG], writes=[bF])
            for fo in range(32):
                P, bP = p.ps[fo % 2], p.bps[fo % 2]
                W, bW = WD[cnt["wd"] % len(WD)]
                cnt["wd"] += 1
                k.dma("sp", W[:, 0:hn, :], p.wtile_view(wdnb, HC, fo, h0, hn), bW, b_wdnb)
                for i in range(hn):
                    k.op("pe", lambda: nc.tensor.matmul(P[:, 0:n], lhsT=W[:, i, :], rhs=Fh[:, i, 0:n], start=(i == 0), stop=(i == hn - 1)),
                         reads=[bW, bF], writes=[bP], inc=(i == hn - 1))
                resid(P, bP, fo, n, segs, 160)
        k.dma("pool", fm_view(o_x, TLOC, t0, n), X[:, :, 0:n], b_ox, bX)
        normmod_seg(n, segs, gsn, b_gsn, modn, b_modn, 0)
        k.dma("pool", fm_view(o_hx, TLOC, t0, n), HX[:, :, 0:n], b_ohx, bHX)
    return p.finish()


def pblocks(S):
    blocks = [(0, 256)]
    s = 256
    while s < S:
        n = min(512, S - s)
        blocks.append((s, n))
        s += n
    return blocks


class PBase(Base):
    def setup(self, S):
        nc, k = self.nc, self.k
        self.S = S
        self.consts()
        self.i_ident = self.din("ident", [128, 128], BF16)
        self.i_tri = self.din("tri", [64, 256])
        self.ident, self.b_ident = self.sb(self.es, "ident", [128, 128], BF16)
        self.tri, self.b_tri = self.sb(self.es, "tri", [64, 256], F32)
        k.dma("sp", self.ident[:, :], self.i_ident[:, :], self.b_ident, self.b_in)
        k.dma("sp", self.tri[:, :], self.i_tri[:, :], self.b_tri, self.b_in)
        self.rr = 0

    def proj_fm(self, W, bW, ci, HXB, bHXB, n, P, bP, M=128):
        nc, k = self.nc, self.k
        for kc in range(KC):
            k.op("pe", lambda kc=kc: nc.tensor.matmul(P[0:M, 0:n], lhsT=W[:, kc, ci * 128:ci * 128 + M], rhs=HXB[:, kc, 0:n], start=(kc == 0), stop=(kc == KC - 1)),
                 reads=[bW, bHXB], writes=[bP], inc=(kc == KC - 1))

    def proj_tm(self, W, bW, c0, ncols, HXB, bHXB, ts, P, bP):
        nc, k = self.nc, self.k
        for kc in range(KC):
            k.op("pe", lambda kc=kc: nc.tensor.matmul(P[:, 0:ncols], lhsT=HXB[:, kc, ts * 128:(ts + 1) * 128], rhs=W[:, kc, c0:c0 + ncols], start=(kc == 0), stop=(kc == KC - 1)),
                 reads=[bW, bHXB], writes=[bP], inc=(kc == KC - 1))

    def evac_copy(self, out, in_, reads, writes, scale=None, func=None):
        nc, k = self.nc, self.k
        self.rr += 1
        if func is not None or scale is not None or self.rr % 2 == 0:
            f = func if func is not None else AF.Copy
            if scale is None:
                k.op("act", lambda: nc.scalar.activation(out=out, in_=in_, func=f), reads=reads, writes=writes)
            else:
                k.op("act", lambda: nc.scalar.activation(out=out, in_=in_, func=f, scale=scale), reads=reads, writes=writes)
        else:
            k.op("dve", lambda: nc.vector.tensor_copy(out=out, in_=in_), reads=reads, writes=writes)

    def head_post(self, H, bH, nv, T, SQ, bSQ, R, bR, G, bG, gain_ap, b_gain, OUTb, bOUTb, o_m, b_om, row0, s0):
        nc, k = self.nc, self.k
        S = self.S
        P, bP = self.ps[7], self.bps[7]
        k.op("act", lambda: nc.scalar.activation(out=SQ[:, :, 0:T], in_=H[:, :, 0:T], func=AF.Square), reads=[bH], writes=[bSQ])
        for c0 in range(0, T, 512):
            cn = min(512, T - c0)
            for vj in range(nv):
                k.op("pe", lambda vj=vj: nc.tensor.matmul(P[:, 0:cn], lhsT=self.ones[:, :], rhs=SQ[:, vj, c0:c0 + cn], start=(vj == 0), stop=(vj == nv - 1)),
                     reads=[bSQ, self.b_ones], writes=[bP], inc=(vj == nv - 1))
            k.op("dve", lambda: nc.vector.tensor_scalar(out=R[:, 0:cn], in0=P[:, 0:cn], scalar1=1.0 / (128 * nv), scalar2=EPS, op0=ALU.mult, op1=ALU.add),
                 reads=[bP], writes=[bR])
            k.op("act", lambda: nc.scalar.activation(out=R[:, 0:cn], in_=R[:, 0:cn], func=AF.Sqrt), reads=[bR], writes=[bR])
            k.op("dve", lambda: nc.vector.reciprocal(out=R[:, 0:cn], in_=R[:, 0:cn]), reads=[bR], writes=[bR])
            for vj in range(nv):
                k.op("dve", lambda vj=vj: nc.vector.tensor_tensor(out=H[:, vj, c0:c0 + cn], in0=H[:, vj, c0:c0 + cn], in1=R[:, 0:cn], op=ALU.mult),
                     reads=[bH, bR], writes=[bH])
                k.op("dve", lambda vj=vj: nc.vector.scalar_tensor_tensor(out=OUTb[:, vj, c0:c0 + cn], in0=H[:, vj, c0:c0 + cn], scalar=gain_ap(vj),
                                                                        in1=G[:, vj, c0:c0 + cn], op0=ALU.mult, op1=ALU.mult),
                     reads=[bH, bG, b_gain], writes=[bOUTb])
        for vj in range(nv):
            k.dma("pool", mkap(o_m, (row0 + vj * 128) * S + s0, [[S, 128], [1, T]]), OUTb[:, vj, 0:T], b_om, bOUTb)


def build_PE(ROWS):
    p = PBase()
    nc, k = p.nc, p.k
    SEQ = ROWS * 64
    S = SEQ + 256
    NT = ROWS // 2
    i_hx = p.din("hx", [D, S], BF16)
    i_win = p.din("win", [D, 2048])
    i_lbl = p.din("lbl", [128, 8])
    i_jsel = p.din("jsel", [128, 1])
    i_hgg = p.din("hgg", [128, 2])
    i_nab = p.din("nab", [128, 2 * 5 * 640])
    i_rst = p.din("rst", [128, 2048])
    o_m = p.dout("mloc", [512, S], BF16)
    b_om = k.buf("o_m")
    p.setup(S)
    es = p.es
    wbf, b_wbf = p.dint("wbf", [16 * 128 * 32 * 128], BF16)
    na_q, b_naq = p.dint("na_q", [2 * 128 * S], BF16)
    na_k, b_nak = p.dint("na_k", [2 * 128 * S], BF16)
    na_v, b_nav = p.dint("na_v", [S * 256], BF16)
    hg_q, b_hgq = p.dint("hg_q", [2 * 128 * S], F32)
    hg_ff, b_hgff = p.dint("hg_ff", [2 * 128 * S], F32)
    hg_fb, b_hgfb = p.dint("hg_fb", [2 * 128 * S], F32)
    hg_g, b_hgg_ = p.dint("hg_g", [2 * 128 * S], BF16)
    hg_i, b_hgi = p.dint("hg_i", [S * 256], BF16)
    hg_o, b_hgo = p.dint("hg_o", [2 * 128 * S], F32)
    with ExitStack() as e0:
        p.cast_tiles(i_win, D, 2048, wbf, b_wbf, e0)
        k.barrier()
    del p._cst

    def rows_view(h, hidx, s0, n):
        return mkap(h, hidx * 128 * S + s0, [[S, 128], [1, n]])

    with ExitStack() as e1:
        Wfm, bWfm = p.sb(e1, "Wfm", [128, 32, 512], BF16)
        Wtm, bWtm = p.sb(e1, "Wtm", [128, 32, 256], BF16)
        for i in range(4):
            k.dma("sp", Wfm[:, :, i * 128:(i + 1) * 128], p.wtile_view(wbf, 32, i), bWfm, b_wbf)
        for i in range(2):
            k.dma("sp", Wtm[:, :, i * 128:(i + 1) * 128], p.wtile_view(wbf, 32, 4 + i), bWtm, b_wbf)
        HXBs = [p.sb(e1, f"HXB{i}", [128, 32, 512], BF16) for i in range(2)]
        stg = [p.sb(e1, f"stg{i}", [128, 512], BF16) for i in range(4)]
        si = 0
        for bi, (s0, n) in enumerate(pblocks(S)):
            HXB, bHXB = HXBs[bi % 2]
            k.dma("sp", HXB[:, :, 0:n], fm_view(i_hx, S, s0, n), bHXB, p.b_in)
            for ci in range(4):
                P, bP = p.ps[ci % 4], p.bps[ci % 4]
                p.proj_fm(Wfm, bWfm, ci, HXB, bHXB, n, P, bP)
                st, bst = stg[si % 4]
                si += 1
                p.evac_copy(st[:, 0:n], P[:, 0:n], [bP], [bst], scale=(128 ** -0.5 if ci < 2 else None))
                dh, bd = (na_q, b_naq) if ci < 2 else (na_k, b_nak)
                k.dma("pool", rows_view(dh, ci % 2, s0, n), st[:, 0:n], bd, bst)
            for ts in range(n // 128):
                P, bP = p.ps[4 + ts % 2], p.bps[4 + ts % 2]
                p.proj_tm(Wtm, bWtm, 0, 256, HXB, bHXB, ts, P, bP)
                st, bst = stg[si % 4]
                si += 1
                p.evac_copy(st[:, 0:256], P[:, 0:256], [bP], [bst])
                k.dma("pool", mkap(na_v, (s0 + ts * 128) * 256, [[256, 128], [1, 256]]), st[:, 0:256], b_nav, bst)
        k.barrier()

    with ExitStack() as e2:
        kT, bkT = p.sb(e2, "kT", [128, S], BF16)
        qT, bqT = p.sb(e2, "qT", [128, S], BF16)
        V, bV = p.sb(e2, "V", [128, S // 128, 128], BF16)
        bias, bbias = p.sb(e2, "bias", [128, 5, 640], F32)
        sets = []
        for par in range(2):
            Sc_, bSc_ = p.sb(e2, f"Sc{par}", [128, 896], F32)
            Pn_, bPn_ = p.sb(e2, f"Pn{par}", [128, 896], BF16)
            PT_, bPT_ = p.sb(e2, f"PT{par}", [128, 7, 128], BF16)
            mx_, bmx_ = p.sb(e2, f"mx{par}", [128, 1], F32)
            sm_, bsm_ = p.sb(e2, f"sm{par}", [128, 1], F32)
            sets.append(dict(Sc=Sc_, bSc=bSc_, Pn=Pn_, bPn=bPn_, PT=PT_, bPT=bPT_, mx=mx_, bmx=bmx_, sm=sm_, bsm=bsm_,
                             SA=p.ps[0 + 4 * par], bSA=p.bps[0 + 4 * par], SB=p.ps[1 + 4 * par], bSB=p.bps[1 + 4 * par],
                             PTp=p.ps[2 + 4 * par][:, :].bitcast(BF16), bPTp=p.bps[2 + 4 * par], OA=p.ps[3 + 4 * par], bOA=p.bps[3 + 4 * par]))
        ostg = [p.sb(e2, f"ostg{i}", [128, 128], BF16) for i in range(3)]
        oi = 0
        blk = 0

        def softmax_pv(z, nk, vtiles, h, q0):
            nonlocal oi
            Sc, bSc, Pn, bPn, PT, bPT, mx, bmx, sm, bsm = z["Sc"], z["bSc"], z["Pn"], z["bPn"], z["PT"], z["bPT"], z["mx"], z["bmx"], z["sm"], z["bsm"]
            PTp, bPTp, OA, bOA = z["PTp"], z["bPTp"], z["OA"], z["bOA"]
            k.op("dve", lambda: nc.vector.tensor_reduce(out=mx[:, :], in_=Sc[:, 0:nk], axis=AX.X, op=ALU.max, negate=True), reads=[bSc], writes=[bmx])
            k.op("act", lambda: nc.scalar.activation(out=Sc[:, 0:nk], in_=Sc[:, 0:nk], func=AF.Exp, bias=mx[:, 0:1], scale=1.0), reads=[bSc, bmx], writes=[bSc])
            k.op("dve", lambda: nc.vector.tensor_reduce(out=sm[:, :], in_=Sc[:, 0:nk], axis=AX.X, op=ALU.add), reads=[bSc], writes=[bsm])
            k.op("dve", lambda: nc.vector.reciprocal(out=sm[:, :], in_=sm[:, :]), reads=[bsm], writes=[bsm])
            k.op("dve", lambda: nc.vector.tensor_scalar(out=Pn[:, 0:nk], in0=Sc[:, 0:nk], scalar1=sm[:, 0:1], scalar2=None, op0=ALU.mult), reads=[bSc, bsm], writes=[bPn])
            nt = nk // 128
            for i in range(nt):
                k.op("pe", lambda i=i: nc.tensor.transpose(out=PTp[:, i * 128:(i + 1) * 128], in_=Pn[:, i * 128:(i + 1) * 128], identity=p.ident[:, :]),
                     reads=[bPn, p.b_ident], writes=[bPTp], inc=(i == nt - 1))
            p.evac_copy(PT[:, 0:nt, :], PTp[:, 0:nt * 128].rearrange("p (a c) -> p a c", c=128), [bPTp], [bPT])
            for i, vt in enumerate(vtiles):
                k.op("pe", lambda i=i, vt=vt: nc.tensor.matmul(OA[:, 0:128], lhsT=V[:, vt, :], rhs=PT[:, i, :], start=(i == 0), stop=(i == nt - 1)),
                     reads=[bV, bPT], writes=[bOA], inc=(i == nt - 1))
            st, bst = ostg[oi % 3]
            oi += 1
            p.evac_copy(st[:, :], OA[:, 0:128], [bOA], [bst])
            k.dma("pool", mkap(o_m, h * 128 * S + q0, [[S, 128], [1, 128]]), st[:, :], b_om, bst)

        for h in range(2):
            k.dma("sp", kT[:, :], rows_view(na_k, h, 0, S), bkT, b_nak)
            k.dma("sp", qT[:, :], rows_view(na_q, h, 0, S), bqT, b_naq)
            k.dma("sp", V[:, :, :], mkap(na_v, h * 128, [[256, 128], [128 * 256, S // 128], [1, 128]]), bV, b_nav)
            k.dma("sp", bias[:, :, :].rearrange("p a b -> p (a b)"), i_nab[:, h * 3200:(h + 1) * 3200], bbias, p.b_in)
            for qb in range(2):
                z = sets[blk % 2]
                blk += 1
                Sc, bSc, SB, bSB = z["Sc"], z["bSc"], z["SB"], z["bSB"]
                q0 = qb * 128
                k.op("pe", lambda: nc.tensor.matmul(SB[:, 0:256], lhsT=qT[:, q0:q0 + 128], rhs=kT[:, 0:256], start=True, stop=True),
                     reads=[bqT, bkT], writes=[bSB])
                p.evac_copy(Sc[:, 0:256], SB[:, 0:256], [bSB], [bSc])
                softmax_pv(z, 256, [0, 1], h, q0)
            for j in range(NT):
                z = sets[blk % 2]
                blk += 1
                Sc, bSc, SA, bSA, SB, bSB = z["Sc"], z["bSc"], z["SA"], z["bSA"], z["SB"], z["bSB"]
                ts = min(max(j - 2, 0), NT - 5)
                typ = 0 if j == 0 else (1 if j == 1 else (3 if j == NT - 2 else (4 if j == NT - 1 else 2)))
                q0 = 256 + j * 128
                kl0 = 256 + ts * 128
                k.op("pe", lambda: nc.tensor.matmul(SA[:, 0:512], lhsT=qT[:, q0:q0 + 128], rhs=kT[:, kl0:kl0 + 512], start=True, stop=True),
                     reads=[bqT, bkT], writes=[bSA])
                k.op("pe", lambda: nc.tensor.matmul(SB[:, 0:128], lhsT=qT[:, q0:q0 + 128], rhs=kT[:, kl0 + 512:kl0 + 640], start=True, stop=True),
                     reads=[bqT, bkT], writes=[bSB], inc=False)
                k.op("pe", lambda: nc.tensor.matmul(SB[:, 128:384], lhsT=qT[:, q0:q0 + 128], rhs=kT[:, 0:256], start=True, stop=True),
                     reads=[bqT, bkT], writes=[bSB])
                k.op("dve", lambda: nc.vector.tensor_tensor(out=Sc[:, 0:512], in0=SA[:, 0:512], in1=bias[:, typ, 0:512], op=ALU.add), reads=[bSA, bbias], writes=[bSc])
                k.op("dve", lambda: nc.vector.tensor_tensor(out=Sc[:, 512:640], in0=SB[:, 0:128], in1=bias[:, typ, 512:640], op=ALU.add), reads=[bSB, bbias], writes=[bSc])
                k.op("act", lambda: nc.scalar.copy(out=Sc[:, 640:896], in_=SB[:, 128:384]), reads=[bSB], writes=[bSc])
                softmax_pv(z, 896, [2 + ts + i for i in range(5)] + [0, 1], h, q0)
        k.barrier()

    with ExitStack() as e3:
        Wfm, bWfm = p.sb(e3, "Wfm3", [128, 32, 1024], BF16)
        Wtm, bWtm = p.sb(e3, "Wtm3", [128, 32, 256], BF16)
        for i, cc in enumerate([6, 7, 8, 9, 10, 11, 14, 15]):
            k.dma("sp", Wfm[:, :, i * 128:(i + 1) * 128], p.wtile_view(wbf, 32, cc), bWfm, b_wbf)
        for i in range(2):
            k.dma("sp", Wtm[:, :, i * 128:(i + 1) * 128], p.wtile_view(wbf, 32, 12 + i), bWtm, b_wbf)
        HXBs = [p.sb(e3, f"HXB3{i}", [128, 32, 512], BF16) for i in range(2)]
        stgf = [p.sb(e3, f"stgf{i}", [128, 512], F32) for i in range(4)]
        stgb = [p.sb(e3, f"stgb{i}", [128, 512], BF16) for i in range(3)]
        si = 0
        sj = 0
        dests = [(hg_q, b_hgq), (hg_ff, b_hgff), (hg_fb, b_hgfb), (hg_g, b_hgg_)]
        for bi, (s0, n) in enumerate(pblocks(S)):
            HXB, bHXB = HXBs[bi % 2]
            k.dma("sp", HXB[:, :, 0:n], fm_view(i_hx, S, s0, n), bHXB, p.b_in)
            for ci in range(8):
                P, bP = p.ps[ci % 4], p.bps[ci % 4]
                p.proj_fm(Wfm, bWfm, ci, HXB, bHXB, n, P, bP)
                dh, bd = dests[ci // 2]
                if ci < 6:
                    st, bst = stgf[si % 4]
                    si += 1
                    p.evac_copy(st[:, 0:n], P[:, 0:n], [bP], [bst])
                else:
                    st, bst = stgb[sj % 3]
                    sj += 1
                    p.evac_copy(st[:, 0:n], P[:, 0:n], [bP], [bst], func=AF.Silu)
                k.dma("pool", rows_view(dh, ci % 2, s0, n), st[:, 0:n], bd, bst)
            for ts in range(n // 128):
                P, bP = p.ps[4 + ts % 2], p.bps[4 + ts % 2]
                p.proj_tm(Wtm, bWtm, 0, 256, HXB, bHXB, ts, P, bP)
                st, bst = stgb[sj % 3]
                sj += 1
                p.evac_copy(st[:, 0:256], P[:, 0:256], [bP], [bst])
                k.dma("pool", mkap(hg_i, (s0 + ts * 128) * 256, [[256, 128], [1, 256]]), st[:, 0:256], b_hgi, bst)
        k.barrier()

    with ExitStack() as e4:
        TM = 1024
        lbl, blbl = p.sb(e4, "lbl", [128, 8], F32)
        jsel, bjsel = p.sb(e4, "jsel", [128, 1], F32)
        hgg, bhgg = p.sb(e4, "hgg", [128, 2], F32)
        rst, brst = p.sb(e4, "rst", [128, TM], F32)
        lb, blb = p.sb(e4, "lb", [128, 4], F32)
        oml, boml = p.sb(e4, "oml", [128, 4], F32)
        k.dma("sp", lbl[:, :], i_lbl[:, :], blbl, p.b_in)
        k.dma("sp", jsel[:, :], i_jsel[:, :], bjsel, p.b_in)
        k.dma("sp", hgg[:, :], i_hgg[:, :], bhgg, p.b_in)
        k.dma("sp", rst[:, :], i_rst[:, 0:TM], brst, p.b_in)
        k.op("dve", lambda: nc.vector.tensor_tensor(out=lb[:, :], in0=lbl[:, 4:8], in1=lbl[:, 0:4], op=ALU.subtract), reads=[blbl], writes=[blb])
        k.op("act", lambda: nc.scalar.activation(out=lb[:, :], in_=lb[:, :], func=AF.Sigmoid), reads=[blb], writes=[blb])
        k.op("dve", lambda: nc.vector.tensor_scalar(out=lb[:, :], in0=lb[:, :], scalar1=jsel[:, 0:1], scalar2=None, op0=ALU.mult), reads=[blb, bjsel], writes=[blb])
        k.op("dve", lambda: nc.vector.tensor_scalar(out=oml[:, :], in0=lb[:, :], scalar1=-1.0, scalar2=1.0, op0=ALU.mult, op1=ALU.add), reads=[blb], writes=[boml])
        HT = []
        for h in range(2):
            t = {}
            for nm in ("q", "fp", "F1", "K1", "Bc", "CUM", "EX", "Ofw"):
                t[nm], t["b" + nm] = p.sb(e4, f"{nm}{h}", [128, TM], F32)
            for nm in ("QT", "KTb", "KH"):
                t[nm], t["b" + nm] = p.sb(e4, f"{nm}{h}", [128, TM], BF16)
            for nm in ("G", "SQ", "OUTb"):
                t[nm], t["b" + nm] = p.sb(e4, f"{nm}{h}", [128, 1, TM], BF16)
            t["O"], t["bO"] = p.sb(e4, f"O{h}", [128, 1, TM], F32)
            t["EL"], t["bEL"] = p.sb(e4, f"EL{h}", [128, TM // 32], F32)
            t["Vt"], t["bVt"] = p.sb(e4, f"Vt{h}", [32, TM // 32, 128], BF16)
            t["St"], t["bSt"] = p.sb(e4, f"St{h}", [128, 128], F32)
            t["Sbf"], t["bSbf"] = p.sb(e4, f"Sbf{h}", [128, 128], BF16)
            t["A"], t["bA"] = p.sb(e4, f"A{h}", [32, 32], BF16)
            t["KHt"], t["bKHt"] = p.sb(e4, f"KHt{h}", [32, 128], BF16)
            t["KHp"] = p.ps[4 * h + 1][:, :].bitcast(BF16)
            HT.append(t)
        R, bR = p.sb(e4, "R4", [128, 512], F32)
        sbs = [(0, 256)] + [(256 + i * TM, min(TM, SEQ - i * TM)) for i in range((SEQ + TM - 1) // TM)]
        for d in range(2):
            for h in range(2):
                t = HT[h]
                k.op("dve", lambda: nc.vector.memset(t["St"][:, :], 0.0), writes=[t["bSt"]])
                k.op("dve", lambda: nc.vector.memset(t["Sbf"][:, :], 0.0), writes=[t["bSbf"]])
            order = sbs if d == 0 else [sbs[0]] + sbs[:0:-1]
            for (s0, T) in order:
                nch = T // 32
                for h in range(2):
                    t = HT[h]
                    col = d * 2 + h
                    q, bq, fp, bfp, F1, bF1, K1, bK1, Bc, bBc, CUM, bCUM, EX, bEX = (t["q"], t["bq"], t["fp"], t["bfp"], t["F1"], t["bF1"], t["K1"], t["bK1"],
                                                                                      t["Bc"], t["bBc"], t["CUM"], t["bCUM"], t["EX"], t["bEX"])
                    QT, bQT, KTb, bKTb, KH, bKH, EL, bEL, Vt, bVt = t["QT"], t["bQT"], t["KTb"], t["bKTb"], t["KH"], t["bKH"], t["EL"], t["bEL"], t["Vt"], t["bVt"]
                    fph, bfph = (hg_ff, b_hgff) if d == 0 else (hg_fb, b_hgfb)
                    k.dma("sp", q[:, 0:T], rows_view(hg_q, h, s0, T), bq, b_hgq)
                    k.dma("sp", fp[:, 0:T], rows_view(fph, h, s0, T), bfp, bfph)
                    k.dma("sp", Vt[:, 0:nch, :], mkap(hg_i, s0 * 256 + h * 128, [[256, 32], [32 * 256, nch], [1, 128]]), bVt, b_hgi)
                    if d == 1:
                        k.dma("sp", t["Ofw"][:, 0:T], rows_view(hg_o, h, s0, T), t["bOfw"], b_hgo)
                        k.dma("sp", t["G"][:, 0, 0:T], rows_view(hg_g, h, s0, T), t["bG"], b_hgg_)
                    k.op("act", lambda: nc.scalar.activation(out=F1[:, 0:T], in_=fp[:, 0:T], func=AF.Sigmoid), reads=[bfp], writes=[bF1])
                    k.op("dve", lambda: nc.vector.tensor_scalar(out=F1[:, 0:T], in0=F1[:, 0:T], scalar1=oml[:, col:col + 1], scalar2=lb[:, col:col + 1], op0=ALU.mult, op1=ALU.add),
                         reads=[bF1, boml, blb], writes=[bF1])
                    k.op("act", lambda: nc.scalar.activation(out=K1[:, 0:T], in_=fp[:, 0:T], func=AF.Sigmoid, scale=-1.0), reads=[bfp], writes=[bK1])
                    k.op("pool", lambda: nc.gpsimd.tensor_scalar(out=K1[:, 0:T], in0=K1[:, 0:T], scalar1=oml[:, col:col + 1], scalar2=None, op0=ALU.mult),
                         reads=[bK1, boml], writes=[bK1])
                    k.op("act", lambda: nc.scalar.activation(out=F1[:, 0:T], in_=F1[:, 0:T], func=AF.Ln), reads=[bF1], writes=[bF1])
                    k.op("dve", lambda: nc.vector.tensor_tensor_scan(out=Bc[:, 0:T], data0=rst[:, 0:T], data1=F1[:, 0:T], initial=0.0, op0=ALU.mult, op1=ALU.add),
                         reads=[brst, bF1], writes=[bBc])
                    B3 = Bc[:, 0:T].rearrange("p (c s) -> p c s", s=32)
                    if d == 0:
                        cum, bcum = Bc, bBc
                        last_ap = B3[:, :, 31]
                    else:
                        k.op("dve", lambda: nc.vector.tensor_tensor(out=CUM[:, 0:T], in0=F1[:, 0:T], in1=Bc[:, 0:T], op=ALU.subtract), reads=[bF1, bBc], writes=[bCUM])
                        C3 = CUM[:, 0:T].rearrange("p (c s) -> p c s", s=32)
                        k.op("dve", lambda: nc.vector.tensor_tensor(out=C3, in0=C3, in1=B3[:, :, 31:32].broadcast_to([128, nch, 32]), op=ALU.add), reads=[bCUM, bBc], writes=[bCUM])
                        cum, bcum = CUM, bCUM
                        last_ap = C3[:, :, 0]
                    k.op("act", lambda: nc.scalar.activation(out=EX[:, 0:T], in_=cum[:, 0:T], func=AF.Exp), reads=[bcum], writes=[bEX])
                    k.op("dve", lambda: nc.vector.tensor_tensor(out=QT[:, 0:T], in0=q[:, 0:T], in1=EX[:, 0:T], op=ALU.mult), reads=[bq, bEX], writes=[bQT])
                    k.op("act", lambda: nc.scalar.activation(out=EX[:, 0:T], in_=cum[:, 0:T], func=AF.Exp, scale=-1.0), reads=[bcum], writes=[bEX])
                    k.op("dve", lambda: nc.vector.tensor_tensor(out=K1[:, 0:T], in0=K1[:, 0:T], in1=EX[:, 0:T], op=ALU.mult), reads=[bK1, bEX], writes=[bK1])
                    k.op("act", lambda: nc.scalar.copy(out=KTb[:, 0:T], in_=K1[:, 0:T]), reads=[bK1], writes=[bKTb])
                    k.op("act", lambda: nc.scalar.activation(out=EL[:, 0:nch], in_=last_ap, func=AF.Exp), reads=[bcum], writes=[bEL])
                    k.op("dve", lambda: nc.vector.tensor_tensor(out=KH[:, 0:T].rearrange("p (c s) -> p c s", s=32), in0=K1[:, 0:T].rearrange("p (c s) -> p c s", s=32),
                                                                in1=EL[:, 0:nch].unsqueeze(2).broadcast_to([128, nch, 32]), op=ALU.mult),
                         reads=[bK1, bEL], writes=[bKH])
                crange = range(nch) if d == 0 else range(nch - 1, -1, -1)
                moff = 0 if d == 0 else 32
                for c in crange:
                    cs = slice(c * 32, (c + 1) * 32)
                    for h in range(2):
                        t = HT[h]
                        QT, bQT, KTb, bKTb, KH, bKH, EL, bEL, Vt, bVt = t["QT"], t["bQT"], t["KTb"], t["bKTb"], t["KH"], t["bKH"], t["EL"], t["bEL"], t["Vt"], t["bVt"]
                        St, bSt, Sbf, bSbf, A, bA, KHt, bKHt, O, bO = t["St"], t["bSt"], t["Sbf"], t["bSbf"], t["A"], t["bA"], t["KHt"], t["bKHt"], t["O"], t["bO"]
                        KHp = t["KHp"]
                        AT, bAT = p.ps[4 * h], p.bps[4 * h]
                        bKHp = p.bps[4 * h + 1]
                        OP, bOP = p.ps[4 * h + 2], p.bps[4 * h + 2]
                        UP, bUP = p.ps[4 * h + 3], p.bps[4 * h + 3]
                        k.op("pe", lambda: nc.tensor.matmul(AT[0:32, 0:32], lhsT=KTb[:, cs], rhs=QT[:, cs], start=True, stop=True), reads=[bKTb, bQT], writes=[bAT])
                        k.op("dve", lambda: nc.vector.tensor_tensor(out=A[:, :], in0=AT[0:32, 0:32], in1=p.tri[0:32, moff:moff + 32], op=ALU.mult), reads=[bAT, p.b_tri], writes=[bA])
                        k.op("pe", lambda: nc.tensor.transpose(out=KHp[0:32, 0:128], in_=KH[:, cs], identity=p.ident[:, :]), reads=[bKH, p.b_ident], writes=[bKHp])
                        k.op("act", lambda: nc.scalar.copy(out=KHt[:, :], in_=KHp[0:32, 0:128]), reads=[bKHp], writes=[bKHt])
                        k.op("pe", lambda: nc.tensor.matmul(OP[:, 0:32], lhsT=Vt[:, c, :], rhs=A[:, :], start=True, stop=False), reads=[bVt, bA], writes=[bOP], inc=False)
                        k.op("pe", lambda: nc.tensor.matmul(OP[:, 0:32], lhsT=Sbf[:, :], rhs=QT[:, cs], start=False, stop=True), reads=[bSbf, bQT], writes=[bOP])
                        k.op("pe", lambda: nc.tensor.matmul(UP[:, 0:128], lhsT=KHt[:, :], rhs=Vt[:, c, :], start=True, stop=True), reads=[bKHt, bVt], writes=[bUP])
                        if d == 0:
                            k.op("act", lambda: nc.scalar.copy(out=O[:, 0, cs], in_=OP[:, 0:32]), reads=[bOP], writes=[bO])
                        else:
                            k.op("pool", lambda: nc.gpsimd.tensor_copy(out=O[:, 0, cs], in_=t["Ofw"][:, cs]), reads=[t["bOfw"]], writes=[bO]) if False else None
                            k.op("dve", lambda: nc.vector.tensor_tensor(out=O[:, 0, cs], in0=OP[:, 0:32], in1=t["Ofw"][:, cs], op=ALU.add), reads=[bOP, t["bOfw"]], writes=[bO])
                        k.op("dve", lambda: nc.vector.scalar_tensor_tensor(out=St[:, :], in0=St[:, :], scalar=EL[:, c:c + 1], in1=UP[:, 0:128], op0=ALU.mult, op1=ALU.add),
                             reads=[bSt, bEL, bUP], writes=[bSt])
                        k.op("act", lambda: nc.scalar.copy(out=Sbf[:, :], in_=St[:, :]), reads=[bSt], writes=[bSbf])
                for h in range(2):
                    t = HT[h]
                    if d == 0:
                        k.dma("pool", rows_view(hg_o, h, s0, T), t["O"][:, 0, 0:T], b_hgo, t["bO"])
                    else:
                        p.head_post(t["O"], t["bO"], 1, T, t["SQ"], t["bSQ"], R, bR, t["G"], t["bG"], lambda vj, h=h: hgg[:, h:h + 1], bhgg,
                                    t["OUTb"], t["bOUTb"], o_m, b_om, 256 + h * 128, s0)
        k.barrier()
    return p.finish()


def col_layout(v):
    v = np.asarray(v)
    return np.ascontiguousarray(v.reshape(-1, 128).T)


def const_tables():
    s = np.arange(64)
    tri = np.zeros((64, 256), np.float32)
    f32_ = (s[:32, None] <= s[None, :32]).astype(np.float32)
    tri[:32, 0:32] = f32_
    tri[:32, 32:64] = f32_.T
    f64_ = (s[:, None] <= s[None, :]).astype(np.float32)
    tri[:, 64:128] = f64_
    tri[:, 128:192] = f64_.T
    rst = np.ones((128, 2048), np.float32)
    rst[:, ::32] = 0.0
    rst64 = np.ones((1, 2048), np.float32)
    rst64[:, ::64] = 0.0
    ident = np.eye(128, dtype=np.float32).astype(NPBF)
    return tri, rst, rst64, ident


def prep_even_win(w, c):
    return np.ascontiguousarray(np.concatenate([w[:, sec * 2048 + 256 * c: sec * 2048 + 256 * c + 256] for sec in range(8)], axis=1))


def na_bias_tables(rpb, c, ROWS):
    NT = ROWS // 2
    out = np.empty((2, 5, 128, 640), np.float32)
    qi = np.arange(128)
    ki = np.arange(640)
    for ti, j in enumerate([0, 1, 2, NT - 2, NT - 1]):
        ts = min(max(j - 2, 0), NT - 5)
        r = 2 * j + qi // 64
        cq = qi % 64
        kr = 2 * ts + ki // 64
        kc = ki % 64
        r0 = np.clip(r - 4, 0, ROWS - 8)
        c0 = np.clip(cq - 8, 0, 64 - 16)
        inwin = ((kr[None, :] >= r0[:, None]) & (kr[None, :] < r0[:, None] + 8) &
                 (kc[None, :] >= c0[:, None]) & (kc[None, :] < c0[:, None] + 16))
        di = np.clip(kr[None, :] - r[:, None] + 7, 0, 14)
        dj = np.clip(kc[None, :] - cq[:, None] + 15, 0, 30)
        for h in range(2):
            g = rpb[2 * c + h][di, dj]
            out[h, ti] = np.where(inwin, g, np.float32(-30000.0))
    return np.ascontiguousarray(out.transpose(2, 0, 1, 3).reshape(128, 2 * 5 * 640))


def even_small_params(lbl_all, j, hgg_all, c):
    lbl = np.empty((128, 8), np.float32)
    hg = np.empty((128, 2), np.float32)
    for d in range(2):
        for h in range(2):
            sl = slice((2 * c + h) * 128, (2 * c + h + 1) * 128)
            lbl[:, d * 2 + h] = lbl_all[d, 0, sl]
            lbl[:, 4 + d * 2 + h] = lbl_all[d, 1, sl]
    for h in range(2):
        hg[:, h] = hgg_all[j, (2 * c + h) * 128:(2 * c + h + 1) * 128]
    jsel = np.full((128, 1), float(j), np.float32)
    return lbl, jsel, hg


def build_PO(ROWS):
    p = PBase()
    nc, k = p.nc, p.k
    SEQ = ROWS * 64
    S = SEQ + 256
    i_hx = p.din("hx", [D, S], BF16)
    i_win = p.din("win", [D, 2048])
    i_wg = p.din("wg", [D, 4])
    i_bg = p.din("bg", [1, 4])
    i_mlg = p.din("mlg", [128, 4])
    i_ropeR = p.din("ropeR", [128, 2 * ROWS])
    i_ropeC = p.din("ropeC", [128, 128])
    i_rst64 = p.din("rst64", [1, 2048])
    o_m = p.dout("mloc", [512, S], BF16)
    b_om = k.buf("o_m")
    p.setup(S)
    es = p.es
    wbf, b_wbf = p.dint("wbf", [16 * 128 * 32 * 128], BF16)
    ml_q, b_mlq = p.dint("ml_q", [2 * 128 * S], BF16)
    ml_k, b_mlk = p.dint("ml_k", [2 * 128 * S], BF16)
    ml_o, b_mlo = p.dint("ml_o", [4 * 128 * S], BF16)
    ml_v, b_mlv = p.dint("ml_v", [S * 512], BF16)
    ml_g, b_mlg_ = p.dint("ml_g", [4 * S], F32)
    ml_h, b_mlh = p.dint("ml_h", [4 * 128 * S], F32)
    with ExitStack() as e0:
        p.cast_tiles(i_win, D, 2048, wbf, b_wbf, e0)
        k.barrier()
    del p._cst

    def rows_view(h, hidx, s0, n):
        return mkap(h, hidx * 128 * S + s0, [[S, 128], [1, n]])

    with ExitStack() as e1:
        Wfm, bWfm = p.sb(e1, "Wfm", [128, 32, 1024], BF16)
        for i in range(8):
            k.dma("sp", Wfm[:, :, i * 128:(i + 1) * 128], p.wtile_view(wbf, 32, i), bWfm, b_wbf)
        wg32, bwg32 = p.sb(e1, "wg32", [128, 32, 4], F32)
        Wg, bWg = p.sb(e1, "Wg", [128, 32, 4], BF16)
        k.dma("sp", wg32[:, :, :], mkap(i_wg, 0, [[4, 128], [128 * 4, 32], [1, 4]]), bwg32, p.b_in)
        k.op("dve", lambda: nc.vector.tensor_copy(out=Wg[:, :, :], in_=wg32[:, :, :]), reads=[bwg32], writes=[bWg])
        ropeR, bropeR = p.sb(e1, "ropeR", [128, 2 * ROWS], F32)
        ropeC, bropeC = p.sb(e1, "ropeC", [128, 128], F32)
        k.dma("sp", ropeR[:, :], i_ropeR[:, :], bropeR, p.b_in)
        k.dma("sp", ropeC[:, :], i_ropeC[:, :], bropeC, p.b_in)
        HXBs = [p.sb(e1, f"HXB{i}", [128, 32, 512], BF16) for i in range(2)]
        stg = [p.sb(e1, f"stg{i}", [128, 512], BF16) for i in range(4)]
        T1, bT1 = p.sb(e1, "T1", [128, 512], F32)
        T2, bT2 = p.sb(e1, "T2", [128, 512], F32)
        gst, bgst = p.sb(e1, "gst", [4, 512], F32)
        si = 0
        for bi, (s0, n) in enumerate(pblocks(S)):
            HXB, bHXB = HXBs[bi % 2]
            k.dma("sp", HXB[:, :, 0:n], fm_view(i_hx, S, s0, n), bHXB, p.b_in)
            for which in range(2):
                scale = 1.0 if which == 0 else 256 ** -0.5
                dh, bd = (ml_q, b_mlq) if which == 0 else (ml_k, b_mlk)
                for c in range(2):
                    cc = which * 4 + c
                    P1, bP1 = p.ps[(which * 2 + c) % 2 * 2], p.bps[(which * 2 + c) % 2 * 2]
                    p.proj_fm(Wfm, bWfm, cc, HXB, bHXB, n, P1, bP1)
                    st, bst = stg[si % 4]
                    si += 1
                    if s0 < 256:
                        p.evac_copy(st[:, 0:n], P1[:, 0:n], [bP1], [bst], scale=scale)
                    else:
                        P2, bP2 = p.ps[(which * 2 + c) % 2 * 2 + 1], p.bps[(which * 2 + c) % 2 * 2 + 1]
                        p.proj_fm(Wfm, bWfm, cc + 2, HXB, bHXB, n, P2, bP2)
                        t0 = s0 - 256
                        r0, nr = t0 // 64, n // 64
                        if c == 0:
                            cosb = ropeR[:, r0:r0 + nr].unsqueeze(2).broadcast_to([128, nr, 64])
                            sinb = ropeR[:, ROWS + r0:ROWS + r0 + nr].unsqueeze(2).broadcast_to([128, nr, 64])
                        else:
                            cosb = ropeC[:, 0:64].unsqueeze(1).broadcast_to([128, nr, 64])
                            sinb = ropeC[:, 64:128].unsqueeze(1).broadcast_to([128, nr, 64])
                        v3 = lambda t: t[:, 0:n].rearrange("p (r c) -> p r c", c=64)
                        k.op("dve", lambda: nc.vector.tensor_tensor(out=v3(T1), in0=v3(P1), in1=cosb, op=ALU.mult), reads=[bP1, bropeR, bropeC], writes=[bT1])
                        k.op("dve", lambda: nc.vector.tensor_tensor(out=v3(T2), in0=v3(P2), in1=sinb, op=ALU.mult), reads=[bP2, bropeR, bropeC], writes=[bT2])
                        k.op("pool", lambda: nc.gpsimd.tensor_tensor(out=T1[:, 0:n], in0=T1[:, 0:n], in1=T2[:, 0:n], op=ALU.add), reads=[bT1, bT2], writes=[bT1])
                        k.op("act", lambda: nc.scalar.activation(out=st[:, 0:n], in_=T1[:, 0:n], func=AF.Copy, scale=scale), reads=[bT1], writes=[bst])
                    k.dma("pool", rows_view(dh, c, s0, n), st[:, 0:n], bd, bst)
            PG, bPG = p.ps[6], p.bps[6]
            p.proj_fm(Wg, bWg, 0, HXB, bHXB, n, PG, bPG, M=4)
            k.op("dve", lambda: nc.vector.tensor_copy(out=gst[:, 0:n], in_=PG[0:4, 0:n]), reads=[bPG], writes=[bgst])
            k.dma("pool", mkap(ml_g, s0, [[S, 4], [1, n]]), gst[:, 0:n], b_mlg_, bgst)
        k.barrier()

    with ExitStack() as e1:
        Wfm, bWfm = p.sb(e1, "Wfmb", [128, 32, 512], BF16)
        Wtm, bWtm = p.sb(e1, "Wtmb", [128, 32, 512], BF16)
        for i in range(4):
            k.dma("sp", Wfm[:, :, i * 128:(i + 1) * 128], p.wtile_view(wbf, 32, 8 + i), bWfm, b_wbf)
            k.dma("sp", Wtm[:, :, i * 128:(i + 1) * 128], p.wtile_view(wbf, 32, 12 + i), bWtm, b_wbf)
        HXBs = [p.sb(e1, f"HXBb{i}", [128, 32, 512], BF16) for i in range(2)]
        stg = [p.sb(e1, f"stgb{i}", [128, 512], BF16) for i in range(4)]
        si = 0
        for bi, (s0, n) in enumerate(pblocks(S)):
            HXB, bHXB = HXBs[bi % 2]
            k.dma("sp", HXB[:, :, 0:n], fm_view(i_hx, S, s0, n), bHXB, p.b_in)
            for ci in range(4):
                P, bP = p.ps[ci % 4], p.bps[ci % 4]
                p.proj_fm(Wfm, bWfm, ci, HXB, bHXB, n, P, bP)
                st, bst = stg[si % 4]
                si += 1
                p.evac_copy(st[:, 0:n], P[:, 0:n], [bP], [bst], func=AF.Sigmoid)
                k.dma("pool", rows_view(ml_o, ci, s0, n), st[:, 0:n], b_mlo, bst)
            for ts in range(n // 128):
                P, bP = p.ps[4 + ts % 2], p.bps[4 + ts % 2]
                p.proj_tm(Wtm, bWtm, 0, 512, HXB, bHXB, ts, P, bP)
                st, bst = stg[si % 4]
                si += 1
                p.evac_copy(st[:, 0:512], P[:, 0:512], [bP], [bst])
                k.dma("pool", mkap(ml_v, (s0 + ts * 128) * 512, [[512, 128], [1, 512]]), st[:, 0:512], b_mlv, bst)
        k.barrier()

    with ExitStack() as e2:
        TM = 1024
        NCH = TM // 64
        bg, bbg = p.sb(e2, "bg", [1, 4], F32)
        mlg, bmlg = p.sb(e2, "mlg", [128, 4], F32)
        rst64, brst = p.sb(e2, "rst64", [1, TM], F32)
        onesf, bonesf = p.sb(e2, "onesf", [1, 128], F32)
        k.dma("sp", bg[:, :], i_bg[:, :], bbg, p.b_in)
        k.dma("sp", mlg[:, :], i_mlg[:, :], bmlg, p.b_in)
        k.dma("sp", rst64[:, :], i_rst64[:, 0:TM], brst, p.b_in)
        k.op("dve", lambda: nc.vector.memset(onesf[:, :], 1.0), writes=[bonesf])
        ipre, bipre = p.sb(e2, "ipre", [1, TM], F32)
        fpre, bfpre = p.sb(e2, "fpre", [1, TM], F32)
        brow, bbrow = p.sb(e2, "brow", [1, TM], F32)
        urow, burow = p.sb(e2, "urow", [1, TM], F32)
        wsrow, bwsrow = p.sb(e2, "wsrow", [1, TM], F32)
        erow, berow = p.sb(e2, "erow", [1, TM], F32)
        umax, bumax = p.sb(e2, "umax", [1, NCH], F32)
        Mst, bMst = p.sb(e2, "Mst", [1, NCH], F32)
        marr, bmarr = p.sb(e2, "marr", [1, NCH + 1], F32)
        wprow, bwprow = p.sb(e2, "wprow", [1, NCH], F32)
        mcar, bmcar = p.sb(e2, "mcar", [1, 1], F32)
        kT, bkT = p.sb(e2, "kT", [128, 2, TM], BF16)
        qT, bqT = p.sb(e2, "qT", [128, 2, TM], BF16)
        KH, bKH = p.sb(e2, "KH", [128, 2, TM], BF16)
        Vp, bVp = p.sb(e2, "Vp", [64, NCH, 640], BF16)
        WSbc, bWSbc = p.sb(e2, "WSbc", [128, TM], F32)
        Ebc, bEbc = p.sb(e2, "Ebc", [128, TM], F32)
        Wp, bWp = p.sb(e2, "Wp", [128, NCH], F32)
        O, bO = p.sb(e2, "O", [128, 4, TM], F32)
        Ofw, bOfw = p.sb(e2, "Ofw", [128, 4, TM], F32)
        OG, bOG = p.sb(e2, "OG", [128, 4, TM], BF16)
        SQ, bSQ = p.sb(e2, "SQ", [128, 4, TM], BF16)
        OUTb, bOUTb = p.sb(e2, "OUTb", [128, 4, TM], BF16)
        R, bR = p.sb(e2, "R", [128, 512], F32)
        Cst, bCst = p.sb(e2, "Cst", [128, 2, 640], F32)
        Cbfs = [p.sb(e2, f"Cbf{i}", [128, 2, 640], BF16) for i in range(2)]
        Kts = [p.sb(e2, f"Kt{i}", [64, 256], BF16) for i in range(2)]
        As = [p.sb(e2, f"A{i}", [64, 64], BF16) for i in range(2)]
        cpar = 0
        DNs = [p.sb(e2, f"DN{i}", [128, 64], F32) for i in range(2)]
        TH, bTH = p.sb(e2, "TH", [128, 4, 64], F32)
        k.op("dve", lambda: nc.vector.memset(Vp[:, :, 512:640], 1.0), writes=[bVp])
        sbs = [(0, 256)] + [(256 + i * TM, min(TM, SEQ - i * TM)) for i in range((SEQ + TM - 1) // TM)]
        KTps = [(p.ps[0][:, :].bitcast(BF16), p.bps[0]), (p.ps[6][:, :].bitcast(BF16), p.bps[6])]
        STs = [(p.ps[1], p.bps[1]), (p.ps[7], p.bps[7])]
        for d in range(2):
            k.op("dve", lambda: nc.vector.memset(Cst[:, :, :], 0.0), writes=[bCst])
            k.op("dve", lambda: nc.vector.memset(mcar[:, :], 0.0), writes=[bmcar])
            order = sbs if d == 0 else [sbs[0]] + sbs[:0:-1]
            for (s0, T) in order:
                nch = T // 64
                k.dma("sp", ipre[:, 0:T], mkap(ml_g, (2 * d) * S + s0, [[S, 1], [1, T]]), bipre, b_mlg_)
                k.dma("sp", fpre[:, 0:T], mkap(ml_g, (2 * d + 1) * S + s0, [[S, 1], [1, T]]), bfpre, b_mlg_)
                k.dma("sp", kT[:, :, 0:T], mkap(ml_k, s0, [[S, 128], [128 * S, 2], [1, T]]), bkT, b_mlk)
                k.dma("sp", qT[:, :, 0:T], mkap(ml_q, s0, [[S, 128], [128 * S, 2], [1, T]]), bqT, b_mlq)
                k.dma("sp", Vp[:, 0:nch, 0:512], mkap(ml_v, s0 * 512, [[512, 64], [64 * 512, nch], [1, 512]]), bVp, b_mlv)
                if d == 1:
                    k.dma("sp", Ofw[:, :, 0:T], mkap(ml_h, s0, [[S, 128], [128 * S, 4], [1, T]]), bOfw, b_mlh)
                    k.dma("sp", OG[:, :, 0:T], mkap(ml_o, s0, [[S, 128], [128 * S, 4], [1, T]]), bOG, b_mlo)
                k.op("act", lambda: nc.scalar.activation(out=fpre[:, 0:T], in_=fpre[:, 0:T], func=AF.Sigmoid, bias=bg[0:1, 2 * d + 1:2 * d + 2], scale=1.0),
                     reads=[bfpre, bbg], writes=[bfpre])
                k.op("act", lambda: nc.scalar.activation(out=fpre[:, 0:T], in_=fpre[:, 0:T], func=AF.Ln), reads=[bfpre], writes=[bfpre])
                k.op("dve", lambda: nc.vector.tensor_tensor_scan(out=brow[:, 0:T], data0=rst64[:, 0:T], data1=fpre[:, 0:T], initial=0.0, op0=ALU.mult, op1=ALU.add),
                     reads=[brst, bfpre], writes=[bbrow])
                b3 = brow[:, 0:T].rearrange("p (c s) -> p c s", s=64)
                if d == 1:
                    k.op("dve", lambda: nc.vector.tensor_tensor(out=urow[:, 0:T], in0=fpre[:, 0:T], in1=brow[:, 0:T], op=ALU.subtract), reads=[bfpre, bbrow], writes=[burow])
                    u3 = urow[:, 0:T].rearrange("p (c s) -> p c s", s=64)
                    k.op("dve", lambda: nc.vector.tensor_tensor(out=erow[:, 0:T].rearrange("p (c s) -> p c s", s=64), in0=u3, in1=b3[:, :, 63:64].broadcast_to([1, nch, 64]), op=ALU.add),
                         reads=[burow, bbrow], writes=[berow])
                    k.op("dve", lambda: nc.vector.tensor_copy(out=brow[:, 0:T], in_=erow[:, 0:T]), reads=[berow], writes=[bbrow])
                    Bview = b3[:, :, 0]
                else:
                    Bview = b3[:, :, 63]
                k.op("dve", lambda: nc.vector.scalar_tensor_tensor(out=urow[:, 0:T], in0=ipre[:, 0:T], scalar=bg[0:1, 2 * d:2 * d + 1], in1=brow[:, 0:T], op0=ALU.add, op1=ALU.subtract),
                     reads=[bipre, bbg, bbrow], writes=[burow])
                u3 = urow[:, 0:T].rearrange("p (c s) -> p c s", s=64)
                k.op("dve", lambda: nc.vector.tensor_reduce(out=umax[:, 0:nch], in_=u3, axis=AX.X, op=ALU.max), reads=[burow], writes=[bumax])
                if d == 0:
                    k.op("dve", lambda: nc.vector.tensor_copy(out=marr[:, 0:1], in_=mcar[:, :]), reads=[bmcar], writes=[bmarr])
                    crange = list(range(nch))
                else:
                    k.op("dve", lambda: nc.vector.tensor_copy(out=marr[:, nch:nch + 1], in_=mcar[:, :]), reads=[bmcar], writes=[bmarr])
                    crange = list(range(nch - 1, -1, -1))
                for c in crange:
                    mi, mo = (c, c + 1) if d == 0 else (c + 1, c)
                    k.op("dve", lambda: nc.vector.tensor_tensor(out=Mst[:, c:c + 1], in0=marr[:, mi:mi + 1], in1=umax[:, c:c + 1], op=ALU.max), reads=[bmarr, bumax], writes=[bMst])
                    k.op("dve", lambda: nc.vector.tensor_tensor(out=marr[:, mo:mo + 1], in0=Mst[:, c:c + 1], in1=Bview[:, c:c + 1], op=ALU.add), reads=[bMst, bbrow], writes=[bmarr])
                mlast = nch if d == 0 else 0
                k.op("dve", lambda: nc.vector.tensor_copy(out=mcar[:, :], in_=marr[:, mlast:mlast + 1]), reads=[bmarr], writes=[bmcar])
                mb0 = 0 if d == 0 else 1
                k.op("dve", lambda: nc.vector.tensor_tensor(out=wprow[:, 0:nch], in0=marr[:, mb0:mb0 + nch], in1=Mst[:, 0:nch], op=ALU.subtract), reads=[bmarr, bMst], writes=[bwprow])
                k.op("act", lambda: nc.scalar.activation(out=wprow[:, 0:nch], in_=wprow[:, 0:nch], func=AF.Exp), reads=[bwprow], writes=[bwprow])
                Mb = Mst[:, 0:nch].unsqueeze(2).broadcast_to([1, nch, 64])
                k.op("dve", lambda: nc.vector.tensor_tensor(out=wsrow[:, 0:T].rearrange("p (c s) -> p c s", s=64), in0=u3, in1=Mb, op=ALU.subtract), reads=[burow, bMst], writes=[bwsrow])
                k.op("act", lambda: nc.scalar.activation(out=wsrow[:, 0:T], in_=wsrow[:, 0:T], func=AF.Exp), reads=[bwsrow], writes=[bwsrow])
                k.op("dve", lambda: nc.vector.tensor_tensor(out=erow[:, 0:T].rearrange("p (c s) -> p c s", s=64), in0=b3, in1=Mb, op=ALU.add), reads=[bbrow, bMst], writes=[berow])
                k.op("act", lambda: nc.scalar.activation(out=erow[:, 0:T], in_=erow[:, 0:T], func=AF.Exp, scale=-1.0), reads=[berow], writes=[berow])
                BP, bBP = p.ps[5], p.bps[5]
                for (row, brw, dst, bdst) in ((wsrow, bwsrow, WSbc, bWSbc), (erow, berow, Ebc, bEbc)):
                    for c0 in range(0, T, 512):
                        cn = min(512, T - c0)
                        k.op("pe", lambda: nc.tensor.matmul(BP[:, 0:cn], lhsT=onesf[0:1, :], rhs=row[0:1, c0:c0 + cn], start=True, stop=True), reads=[bonesf, brw], writes=[bBP])
                        p.evac_copy(dst[:, c0:c0 + cn], BP[:, 0:cn], [bBP], [bdst])
                k.op("pe", lambda: nc.tensor.matmul(BP[:, 0:nch], lhsT=onesf[0:1, :], rhs=wprow[0:1, 0:nch], start=True, stop=True), reads=[bonesf, bwprow], writes=[bBP])
                p.evac_copy(Wp[:, 0:nch], BP[:, 0:nch], [bBP], [bWp])
                k.op("dve", lambda: nc.vector.tensor_tensor(out=KH[:, :, 0:T], in0=kT[:, :, 0:T], in1=WSbc[:, 0:T].unsqueeze(1).broadcast_to([128, 2, T]), op=ALU.mult),
                     reads=[bkT, bWSbc], writes=[bKH])
                moff = 64 if d == 0 else 128
                NP, bNP = p.ps[2], p.bps[2]
                UV0, bUV0 = p.ps[3], p.bps[3]
                UV1, bUV1 = p.ps[4], p.bps[4]
                UN, bUN = p.ps[5], p.bps[5]

                def S1(c, par):
                    cs = slice(c * 64, (c + 1) * 64)
                    KTp, bKTp = KTps[par]
                    Kt, bKt = Kts[par]
                    A, bA = As[par]
                    ST, bST = STs[par]
                    for dc in range(2):
                        k.op("pe", lambda dc=dc: nc.tensor.transpose(out=KTp[0:64, dc * 128:(dc + 1) * 128], in_=KH[:, dc, cs], identity=p.ident[:, :]),
                             reads=[bKH, p.b_ident], writes=[bKTp], inc=(dc == 1))
                    for dc in range(2):
                        k.op("pe", lambda dc=dc: nc.tensor.matmul(ST[0:64, 0:64], lhsT=KH[:, dc, cs], rhs=qT[:, dc, cs], start=(dc == 0), stop=(dc == 1)),
                             reads=[bKH, bqT], writes=[bST], inc=(dc == 1))
                    k.op("act", lambda: nc.scalar.copy(out=Kt[:, :], in_=KTp[0:64, 0:256]), reads=[bKTp], writes=[bKt])
                    k.op("dve", lambda: nc.vector.tensor_tensor(out=A[:, :], in0=ST[0:64, 0:64], in1=p.tri[0:64, moff:moff + 64], op=ALU.mult), reads=[bST, p.b_tri], writes=[bA])

                def S3(c, par, cb):
                    cs = slice(c * 64, (c + 1) * 64)
                    Kt, bKt = Kts[par]
                    A, bA = As[par]
                    Cbf, bCbf = Cbfs[cb]
                    DN, bDN = DNs[cb]
                    k.op("act", lambda: nc.scalar.activation(out=Cbf[:, :, :], in_=Cst[:, :, :], func=AF.Copy, scale=Wp[:, c:c + 1]), reads=[bCst, bWp], writes=[bCbf])
                    for dc, (UV, bUV) in enumerate(((UV0, bUV0), (UV1, bUV1))):
                        k.op("pe", lambda dc=dc, UV=UV: nc.tensor.matmul(UV[:, 0:512], lhsT=Kt[:, dc * 128:(dc + 1) * 128], rhs=Vp[:, c, 0:512], start=True, stop=True),
                             reads=[bKt, bVp], writes=[bUV])
                        k.op("pe", lambda dc=dc: nc.tensor.matmul(UN[:, dc * 128:(dc + 1) * 128], lhsT=Kt[:, dc * 128:(dc + 1) * 128], rhs=Vp[:, c, 512:640], start=True, stop=True),
                             reads=[bKt, bVp], writes=[bUN])
                    for dc, (UV, bUV) in enumerate(((UV0, bUV0), (UV1, bUV1))):
                        k.op("dve", lambda dc=dc, UV=UV: nc.vector.scalar_tensor_tensor(out=Cst[:, dc, 0:512], in0=Cst[:, dc, 0:512], scalar=Wp[:, c:c + 1], in1=UV[:, 0:512],
                                                                                       op0=ALU.mult, op1=ALU.add),
                             reads=[bCst, bWp, bUV], writes=[bCst])
                    k.op("dve", lambda: nc.vector.scalar_tensor_tensor(out=Cst[:, :, 512:640], in0=Cst[:, :, 512:640], scalar=Wp[:, c:c + 1],
                                                                       in1=UN[:, 0:256].rearrange("p (a b) -> p a b", b=128), op0=ALU.mult, op1=ALU.add),
                         reads=[bCst, bWp, bUN], writes=[bCst])
                    for vj in range(5):
                        k.op("pe", lambda vj=vj: nc.tensor.matmul(NP[:, vj * 64:(vj + 1) * 64], lhsT=Vp[:, c, vj * 128:(vj + 1) * 128], rhs=A[:, :], start=True, stop=False),
                             reads=[bVp, bA], writes=[bNP], inc=False)
                        k.op("pe", lambda vj=vj: nc.tensor.matmul(NP[:, vj * 64:(vj + 1) * 64], lhsT=Cbf[:, 0, vj * 128:(vj + 1) * 128], rhs=qT[:, 0, cs], start=False, stop=False),
                             reads=[bCbf, bqT], writes=[bNP], inc=False)
                        k.op("pe", lambda vj=vj: nc.tensor.matmul(NP[:, vj * 64:(vj + 1) * 64], lhsT=Cbf[:, 1, vj * 128:(vj + 1) * 128], rhs=qT[:, 1, cs], start=False, stop=True),
                             reads=[bCbf, bqT], writes=[bNP], inc=(vj == 4))
                    k.op("act", lambda: nc.scalar.activation(out=DN[:, :], in_=NP[:, 256:320], func=AF.Abs), reads=[bNP], writes=[bDN])
                    k.op("dve", lambda: nc.vector.tensor_tensor(out=DN[:, :], in0=DN[:, :], in1=Ebc[:, cs], op=ALU.max), reads=[bDN, bEbc], writes=[bDN])
                    k.op("dve", lambda: nc.vector.reciprocal(out=DN[:, :], in_=DN[:, :]), reads=[bDN], writes=[bDN])
                    np3 = NP[:, 0:256].rearrange("p (v t) -> p v t", t=64)
                    dnb = DN[:, :].unsqueeze(1).broadcast_to([128, 4, 64])
                    if d == 0:
                        k.op("dve", lambda: nc.vector.tensor_tensor(out=O[:, :, cs], in0=np3, in1=dnb, op=ALU.mult), reads=[bNP, bDN], writes=[bO])
                    else:
                        k.op("dve", lambda: nc.vector.tensor_tensor(out=TH[:, :, :], in0=np3, in1=dnb, op=ALU.mult), reads=[bNP, bDN], writes=[bTH])
                        k.op("pool", lambda: nc.gpsimd.tensor_tensor(out=O[:, :, cs], in0=TH[:, :, :], in1=Ofw[:, :, cs], op=ALU.add), reads=[bTH, bOfw], writes=[bO])

                clist = list(crange)
                S1(clist[0], cpar % 2)
                for idx, c in enumerate(clist):
                    par = (cpar + idx) % 2
                    if idx + 1 < len(clist):
                        S1(clist[idx + 1], 1 - par)
                    S3(c, par, idx % 2)
                cpar += len(clist)
                if d == 0:
                    k.dma("pool", mkap(ml_h, s0, [[S, 128], [128 * S, 4], [1, T]]), O[:, :, 0:T], b_mlh, bO)
                else:
                    p.head_post(O, bO, 4, T, SQ, bSQ, R, bR, OG, bOG, lambda vj: mlg[:, vj:vj + 1], bmlg, OUTb, bOUTb, o_m, b_om, 0, s0)
        k.barrier()
    return p.finish()


def rope_tables(ROWS):
    n_freq = 64
    inv = (10000.0 ** (-np.arange(n_freq, dtype=np.float32) / n_freq)).astype(np.float32)
    sign = np.concatenate([-np.ones(64, np.float32), np.ones(64, np.float32)])
    f2 = np.concatenate([inv, inv])
    rows = np.arange(ROWS, dtype=np.float32)
    cols = np.arange(64, dtype=np.float32)
    angR = (rows[None, :] * f2[:, None]).astype(np.float32)
    angC = (cols[None, :] * f2[:, None]).astype(np.float32)
    ropeR = np.concatenate([np.cos(angR), sign[:, None] * np.sin(angR)], axis=1).astype(np.float32)
    ropeC = np.concatenate([np.cos(angC), sign[:, None] * np.sin(angC)], axis=1).astype(np.float32)
    return ropeR, ropeC


def prep_odd_win(w, h):
    perm = np.concatenate([np.arange(64, 128), np.arange(0, 64), np.arange(192, 256), np.arange(128, 192)])
    q0, k0, v0, o0, g0 = h * 256, 2048 + h * 256, 4096 + h * 512, 8192 + h * 512, 12288
    cols = np.concatenate([q0 + np.arange(256), q0 + perm, k0 + np.arange(256), k0 + perm, o0 + np.arange(512), v0 + np.arange(512)])
    wg = np.ascontiguousarray(w[:, [g0 + h, g0 + 8 + h, g0 + 16 + h, g0 + 24 + h]])
    return np.ascontiguousarray(w[:, cols]), wg


def build_W(FF):
    p = Base()
    k = p.k
    HC = FF // 128
    i_wout = p.din("wout", [512, D])
    i_wup = p.din("wup", [512, 2 * FF])
    i_wdn = p.din("wdn", [FF, 512])
    o_wout = p.dout("woutb", [32 * 128 * 4 * 128], BF16)
    o_wup = p.dout("wupb", [(2 * FF // 128) * 128 * 4 * 128], BF16)
    o_wdn = p.dout("wdnb", [4 * 128 * HC * 128], BF16)
    p.cast_tiles(i_wout, 512, D, o_wout, k.buf("o_wout"), p.es)
    p.cast_tiles(i_wup, 512, 2 * FF, o_wup, k.buf("o_wup"), p.es)
    p.cast_tiles(i_wdn, FF, 512, o_wdn, k.buf("o_wdn"), p.es)
    return p.finish()


def run_W(ncW, wout, wup, wdn, FF, ncores=8):
    HC = FF // 128
    ims = [{"wout": np.ascontiguousarray(wout[c * 512:(c + 1) * 512]), "wup": np.ascontiguousarray(wup[c * 512:(c + 1) * 512]),
            "wdn": np.ascontiguousarray(wdn[:, c * 512:(c + 1) * 512])} for c in range(ncores)]
    r = _run(ncW, ims)
    woutb = np.concatenate([r[c]["woutb"].reshape(32, 128, 4, 128) for c in range(ncores)], axis=2).reshape(-1)
    wupb = np.concatenate([r[c]["wupb"].reshape(2 * HC, 128, 4, 128) for c in range(ncores)], axis=2).reshape(-1)
    wdnb = np.concatenate([r[c]["wdnb"].reshape(4, 128, HC, 128) for c in range(ncores)], axis=0).reshape(-1)
    return woutb, wupb, wdnb


ROWS_FULL = 256
SEQ_FULL = ROWS_FULL * 64
CTX_LEN = 256
S_FULL = SEQ_FULL + CTX_LEN
NCT = 8
FF_FULL = 11008
DEPTH = 4
_PROGS = {}


def _prog(name, fn):
    if name not in _PROGS:
        _PROGS[name] = fn()
    return _PROGS[name]


def _run(nc, in_maps):
    res = run_bass_kernel_spmd(nc, in_maps, core_ids=list(range(len(in_maps))))
    return res.results


def kernel(x, c, ctx, c_ctx, norm_mix_g, norm_ffn_g, mod_w_a, mod_w_b, mod_b,
           even_w_in, even_w_out, na_rpb, hg_lb_logits, hg_norm_g,
           odd_w_in, odd_b_gates, odd_w_out, ml_norm_g,
           ffn_w_up, ffn_w_down, final_norm_g):
    f = lambda a: np.asarray(a, dtype=np.float32)
    x, c, ctx, c_ctx = f(x), f(c), f(ctx), f(c_ctx)
    norm_mix_g, norm_ffn_g, mod_w_a, mod_w_b, mod_b = f(norm_mix_g), f(norm_ffn_g), f(mod_w_a), f(mod_w_b), f(mod_b)
    even_w_in, even_w_out, na_rpb, hg_lb_logits, hg_norm_g = f(even_w_in), f(even_w_out), f(na_rpb), f(hg_lb_logits), f(hg_norm_g)
    odd_w_in, odd_b_gates, odd_w_out, ml_norm_g = f(odd_w_in), f(odd_b_gates), f(odd_w_out), f(ml_norm_g)
    ffn_w_up, ffn_w_down, final_norm_g = f(ffn_w_up), f(ffn_w_down), f(final_norm_g)
    SEQ, S, ROWS = SEQ_FULL, S_FULL, ROWS_FULL
    TL, TC = SEQ // NCT, CTX_LEN // NCT
    tri, rst, rst64, ident = const_tables()
    ropeR, ropeC = rope_tables(ROWS)

    cvec = np.concatenate([col_layout(c[0]), col_layout(c_ctx)], axis=1)
    ncM = _prog("M", build_M)
    rM = _run(ncM, [{"wa": mod_w_a[l], "wb": mod_w_b[l], "modb": col_layout(mod_b[l]), "cvec": cvec} for l in range(DEPTH)])
    modT = [rM[l]["modT"] for l in range(DEPTH)]

    xT = np.ascontiguousarray(x[0].T)
    cT = np.ascontiguousarray(ctx[0].T)
    xloc = [np.ascontiguousarray(np.concatenate([xT[:, t * TL:(t + 1) * TL], cT[:, t * TC:(t + 1) * TC]], axis=1)) for t in range(NCT)]
    del xT, cT

    def gather_hx(hs):
        out = np.empty((D, S), NPBF)
        for t in range(NCT):
            out[:, t * TC:(t + 1) * TC] = hs[t][:, TL:TL + TC]
            out[:, CTX_LEN + t * TL:CTX_LEN + (t + 1) * TL] = hs[t][:, 0:TL]
        return out

    ncA = _prog("A", lambda: build_A(TL, TC, False))
    rA = _run(ncA, [{"xT": xloc[t], "modT": modT[0], "ng": col_layout(norm_mix_g[0])} for t in range(NCT)])
    hx_all = gather_hx([rA[t]["hx"] for t in range(NCT)])
    del rA

    for l in range(DEPTH):
        j = l // 2
        if l % 2 == 0:
            ncP = _prog("PE", lambda: build_PE(ROWS))
            ims = []
            for cc in range(8):
                lbl, jsel, hg = even_small_params(hg_lb_logits, j, hg_norm_g, cc)
                ims.append({"hx": hx_all, "win": prep_even_win(even_w_in[j], cc), "lbl": lbl, "jsel": jsel, "hgg": hg,
                            "nab": na_bias_tables(na_rpb[j], cc, ROWS), "rst": rst, "ident": ident, "tri": tri})
            rP = _run(ncP, ims)
            del ims
            M_all = np.empty((D, S), NPBF)
            for cc in range(8):
                M_all[256 * cc:256 * cc + 256] = rP[cc]["mloc"][0:256]
                M_all[2048 + 256 * cc:2048 + 256 * cc + 256] = rP[cc]["mloc"][256:512]
            wout = even_w_out[j]
        else:
            ncP = _prog("PO", lambda: build_PO(ROWS))
            ims = []
            for h in range(8):
                win, wg = prep_odd_win(odd_w_in[j], h)
                ims.append({"hx": hx_all, "win": win, "wg": wg, "bg": np.ascontiguousarray(odd_b_gates[j][[h, 8 + h, 16 + h, 24 + h]][None]),
                            "mlg": col_layout(ml_norm_g[j][h * 512:(h + 1) * 512]), "ropeR": ropeR, "ropeC": ropeC, "rst64": rst64, "ident": ident, "tri": tri})
            rP = _run(ncP, ims)
            del ims
            M_all = np.empty((D, S), NPBF)
            for h in range(8):
                M_all[512 * h:512 * h + 512] = rP[h]["mloc"]
            wout = odd_w_out[j]
        del rP
        last = (l == DEPTH - 1)
        modn = np.zeros_like(modT[0]) if last else modT[l + 1]
        ngn = col_layout(final_norm_g) if last else col_layout(norm_mix_g[l + 1])
        ncW = _prog("W", lambda: build_W(FF_FULL))
        woutb, wupb, wdnb = run_W(ncW, wout, ffn_w_up[l], ffn_w_down[l], FF_FULL)
        ncT = _prog("T", lambda: build_T(TL, TC, FF_FULL))
        ims = []
        for t in range(NCT):
            mloc = np.ascontiguousarray(np.concatenate([M_all[:, CTX_LEN + t * TL:CTX_LEN + (t + 1) * TL], M_all[:, t * TC:(t + 1) * TC]], axis=1))
            ims.append({"xT": xloc[t], "mT": mloc, "woutb": woutb, "wupb": wupb, "wdnb": wdnb, "modT": modT[l],
                        "nfg": col_layout(norm_ffn_g[l]), "modTn": modn, "ngn": ngn})
        del M_all
        rT = _run(ncT, ims)
        del ims, woutb, wupb, wdnb
        xloc = [rT[t]["xo"] for t in range(NCT)]
        if not last:
            hx_all = gather_hx([rT[t]["hxo"] for t in range(NCT)])
        del rT

    ncF = _prog("F", lambda: build_A(TL, TC, True))
    zero_mod = np.zeros_like(modT[0])
    rF = _run(ncF, [{"xT": xloc[t], "modT": zero_mod, "ng": col_layout(final_norm_g)} for t in range(NCT)])
    out = np.empty((1, SEQ, D), np.float32)
    for t in range(NCT):
        out[0, t * TL:(t + 1) * TL, :] = rF[t]["hx"][:, 0:TL].T
    return out
```
